# Optimizing a Trainium2 kernel written in Bass

```python
import jax, jax.numpy as jnp
from jax import lax
import numpy as np

D_MODEL = 1024
BATCH = 8
SEQ = 8192
DEPTH = 2

GRID_W = 64
Q_BLOCK = 128
ROPE_THETA = 10000.0
EPS = 1e-6
MLA_HEADS = 6
MLA_NOPE = 64
MLA_ROPE = 32
MLA_V = 64
MLA_Q_RANK = 256
MLA_KV_RANK = 128
GQA_HEADS = 6
GQA_KV_HEADS = 2
GQA_DIM = 64
GMLP_GROUPS = 4
GMLP_DIM = 64
GMLP_CHUNK = 128
W_A = MLA_HEADS * MLA_V
W_B = GQA_HEADS * GQA_DIM
W_C = GMLP_GROUPS * GMLP_DIM
D_MIX = W_A + W_B + W_C
IN_SPLITS = (MLA_Q_RANK, MLA_KV_RANK, MLA_ROPE, W_B, GQA_KV_HEADS * GQA_DIM, GQA_KV_HEADS * GQA_DIM, 2 * W_C)
D_IN = 1568
MEM_TOKENS = 256
MEM_HEADS = 4
MEM_DIM = 128
D_FF = 2816
CONV_W = 3

kernel_name = "hybrid_mla_gqa_gmlp_encoder"


def rms_norm(x, g):
    xf = x.astype(jnp.float32)
    y = xf * lax.rsqrt(jnp.mean(xf * xf, axis=-1, keepdims=True) + EPS)
    return (y * g.astype(jnp.float32)).astype(x.dtype)


def axial_rope_table(S, d_rot):
    rows = S // GRID_W
    row = jnp.repeat(jnp.arange(rows, dtype=jnp.float32), GRID_W)
    col = jnp.tile(jnp.arange(GRID_W, dtype=jnp.float32), rows)
    n = d_rot // 4
    inv = ROPE_THETA ** (-jnp.arange(n, dtype=jnp.float32) / n)
    ang = jnp.concatenate([row[:, None] * inv, col[:, None] * inv], axis=-1)
    return jnp.cos(ang)[:, None, :], jnp.sin(ang)[:, None, :]


def apply_rope(x, cos, sin):
    d = x.shape[-1]
    xp = x.reshape(*x.shape[:-1], d // 2, 2)
    a, b = xp[..., 0], xp[..., 1]
    c, s = cos.astype(x.dtype), sin.astype(x.dtype)
    return jnp.stack([a * c - b * s, a * s + b * c], axis=-1).reshape(x.shape)


def block_attention(q, k, v, scale):
    B, S, H, dk = q.shape
    Hk, dv = k.shape[2], v.shape[-1]
    G = H // Hk
    nb = S // Q_BLOCK
    qb = q.reshape(B, nb, Q_BLOCK, Hk, G, dk).transpose(1, 0, 2, 3, 4, 5)

    def one_block(qi):
        s = jnp.einsum('bqhgd,bkhd->bhgqk', qi, k, preferred_element_type=jnp.float32) * scale
        p = jax.nn.softmax(s, axis=-1).astype(v.dtype)
        return jnp.einsum('bhgqk,bkhe->bqhge', p, v)

    ob = lax.map(one_block, qb)
    return ob.transpose(1, 0, 2, 3, 4, 5).reshape(B, S, H * dv)


def hybrid_mixer(h, rope_a, rope_b, w_in, mla_q_norm, mla_w_uq, mla_kv_norm, mla_w_ukv,
                 gqa_q_norm, gqa_k_norm, gmlp_v_norm, gmlp_w_s, gmlp_b_s, out_norm, w_out):
    B, S, _ = h.shape
    z = h @ w_in
    offs = [int(i) for i in np.cumsum(IN_SPLITS)[:-1]]
    c_q, c_kv, k_rope, g_q, g_k, g_v, g_m = jnp.split(z, offs, axis=-1)

    qa = (rms_norm(c_q, mla_q_norm) @ mla_w_uq).reshape(B, S, MLA_HEADS, MLA_NOPE + MLA_ROPE)
    q_nope, q_pe = qa[..., :MLA_NOPE], apply_rope(qa[..., MLA_NOPE:], *rope_a)
    kva = (rms_norm(c_kv, mla_kv_norm) @ mla_w_ukv).reshape(B, S, MLA_HEADS, MLA_NOPE + MLA_V)
    k_nope, v_a = kva[..., :MLA_NOPE], kva[..., MLA_NOPE:]
    k_pe = apply_rope(k_rope.reshape(B, S, 1, MLA_ROPE), *rope_a)
    q_a = jnp.concatenate([q_nope, q_pe], axis=-1)
    k_a = jnp.concatenate([k_nope, jnp.broadcast_to(k_pe, (B, S, MLA_HEADS, MLA_ROPE))], axis=-1)
    y_a = block_attention(q_a, k_a, v_a, (MLA_NOPE + MLA_ROPE) ** -0.5)

    q_b = apply_rope(rms_norm(g_q.reshape(B, S, GQA_HEADS, GQA_DIM), gqa_q_norm), *rope_b)
    k_b = apply_rope(rms_norm(g_k.reshape(B, S, GQA_KV_HEADS, GQA_DIM), gqa_k_norm), *rope_b)
    v_b = g_v.reshape(B, S, GQA_KV_HEADS, GQA_DIM)
    y_b = block_attention(q_b, k_b, v_b, GQA_DIM ** -0.5)

    g_m = jax.nn.gelu(g_m)
    u, vv = g_m[..., :W_C], g_m[..., W_C:]
    vv = rms_norm(vv, gmlp_v_norm).reshape(B, S // GMLP_CHUNK, GMLP_CHUNK, GMLP_GROUPS, GMLP_DIM)
    mixed = jnp.einsum('gpq,bnqgc->bnpgc', gmlp_w_s, vv) + gmlp_b_s.T[None, None, :, :, None]
    y_c = u * mixed.reshape(B, S, W_C)

    y = jnp.concatenate([rms_norm(y_a, out_norm[:W_A]),
                         rms_norm(y_b, out_norm[W_A:W_A + W_B]),
                         rms_norm(y_c, out_norm[W_A + W_B:])], axis=-1)
    return y @ w_out


def memory_cross_attention(h, mem, mem_kv_norm, mem_w_q, mem_w_kv, mem_w_o):
    B, S, _ = h.shape
    T = mem.shape[1]
    q = (h @ mem_w_q).reshape(B, S, MEM_HEADS, MEM_DIM)
    kv = rms_norm(mem, mem_kv_norm) @ mem_w_kv
    k = kv[..., :MEM_HEADS * MEM_DIM].reshape(B, T, MEM_HEADS, MEM_DIM)
    v = kv[..., MEM_HEADS * MEM_DIM:].reshape(B, T, MEM_HEADS, MEM_DIM)
    return block_attention(q, k, v, MEM_DIM ** -0.5) @ mem_w_o


def conv_gated_ffn(h, w_up, conv_w, conv_b, w_down):
    a = h @ w_up
    ap = jnp.pad(a, ((0, 0), (1, 1), (0, 0)))
    a = ap[:, :-2] * conv_w[0] + ap[:, 1:-1] * conv_w[1] + ap[:, 2:] * conv_w[2] + conv_b
    gate, val = a[..., :D_FF], a[..., D_FF:]
    return (jax.nn.silu(gate) * val) @ w_down


def setup_inputs(seed: int = 0) -> dict:
    key = jax.random.key(seed)
    ks = iter(jax.random.split(key, 48))

    def nrm(shape, scale):
        return jax.random.normal(next(ks), shape, jnp.float32) * scale

    def gain(shape):
        return 1.0 + nrm(shape, 0.05)

    L = DEPTH
    return {
        "x": nrm((BATCH, SEQ, D_MODEL), 1.0),
        "mem": nrm((BATCH, MEM_TOKENS, D_MODEL), 1.0),
        "mix_norm": gain((L, D_MODEL)),
        "w_in": nrm((L, D_MODEL, D_IN), D_MODEL ** -0.5),
        "mla_q_norm": gain((L, MLA_Q_RANK)),
        "mla_w_uq": nrm((L, MLA_Q_RANK, MLA_HEADS * (MLA_NOPE + MLA_ROPE)), MLA_Q_RANK ** -0.5),
        "mla_kv_norm": gain((L, MLA_KV_RANK)),
        "mla_w_ukv": nrm((L, MLA_KV_RANK, MLA_HEADS * (MLA_NOPE + MLA_V)), MLA_KV_RANK ** -0.5),
        "gqa_q_norm": gain((L, GQA_DIM)),
        "gqa_k_norm": gain((L, GQA_DIM)),
        "gmlp_v_norm": gain((L, W_C)),
        "gmlp_w_s": nrm((L, GMLP_GROUPS, GMLP_CHUNK, GMLP_CHUNK), GMLP_CHUNK ** -0.5),
        "gmlp_b_s": gain((L, GMLP_GROUPS, GMLP_CHUNK)),
        "out_norm": gain((L, D_MIX)),
        "w_out": nrm((L, D_MIX, D_MODEL), D_MIX ** -0.5),
        "mem_x_norm": gain((L, D_MODEL)),
        "mem_kv_norm": gain((L, D_MODEL)),
        "mem_w_q": nrm((L, D_MODEL, MEM_HEADS * MEM_DIM), D_MODEL ** -0.5),
        "mem_w_kv": nrm((L, D_MODEL, 2 * MEM_HEADS * MEM_DIM), D_MODEL ** -0.5),
        "mem_w_o": nrm((L, MEM_HEADS * MEM_DIM, D_MODEL), (MEM_HEADS * MEM_DIM) ** -0.5),
        "ffn_norm": gain((L, D_MODEL)),
        "ffn_w_up": nrm((L, D_MODEL, 2 * D_FF), D_MODEL ** -0.5),
        "ffn_conv_w": nrm((L, CONV_W, 2 * D_FF), CONV_W ** -0.5),
        "ffn_conv_b": nrm((L, 2 * D_FF), 0.01),
        "ffn_w_down": nrm((L, D_FF, D_MODEL), D_FF ** -0.5),
        "final_norm": gain((D_MODEL,)),
    }


def reference(x, mem, mix_norm, w_in, mla_q_norm, mla_w_uq, mla_kv_norm, mla_w_ukv,
              gqa_q_norm, gqa_k_norm, gmlp_v_norm, gmlp_w_s, gmlp_b_s, out_norm, w_out,
              mem_x_norm, mem_kv_norm, mem_w_q, mem_w_kv, mem_w_o,
              ffn_norm, ffn_w_up, ffn_conv_w, ffn_conv_b, ffn_w_down, final_norm):
    S = x.shape[1]
    rope_a = axial_rope_table(S, MLA_ROPE)
    rope_b = axial_rope_table(S, GQA_DIM)
    for l in range(DEPTH):
        h = rms_norm(x, mix_norm[l])
        x = x + hybrid_mixer(h, rope_a, rope_b, w_in[l], mla_q_norm[l], mla_w_uq[l], mla_kv_norm[l],
                             mla_w_ukv[l], gqa_q_norm[l], gqa_k_norm[l], gmlp_v_norm[l], gmlp_w_s[l],
                             gmlp_b_s[l], out_norm[l], w_out[l])
        h = rms_norm(x, mem_x_norm[l])
        x = x + memory_cross_attention(h, mem, mem_kv_norm[l], mem_w_q[l], mem_w_kv[l], mem_w_o[l])
        h = rms_norm(x, ffn_norm[l])
        x = x + conv_gated_ffn(h, ffn_w_up[l], ffn_conv_w[l], ffn_conv_b[l], ffn_w_down[l])
    return rms_norm(x, final_norm)
```

```python
from contextlib import ExitStack
import numpy as np
import concourse.bass as bass
import concourse.mybir as mybir
from concourse.bass_utils import run_bass_kernel_spmd

F32 = mybir.dt.float32
BF16 = mybir.dt.bfloat16
AF = mybir.ActivationFunctionType
ALU = mybir.AluOpType

S = 8192
D = 1024
TS = 512
NT = S // TS
DEPTH = 2
D_IN = 1568
D_FF = 2816
EPS = 1e-6
NCORES = 8


class Buf:
    __slots__ = ("name", "w", "r", "sem")

    def __init__(self, name):
        self.name = name
        self.w = []
        self.r = []
        self.sem = None


def _prune(ts):
    best = {}
    for s, v in ts:
        k = id(s)
        if k not in best or best[k][1] < v:
            best[k] = (s, v)
    return list(best.values())


class Eng:
    def __init__(self, prog, name):
        self.name = name
        self.ops = []
        self.sem = prog.es.enter_context(prog.nc.semaphore("s_" + name))
        self.count = 0
        self.waited = {}

    def wait(self, tickets, skip_own=False):
        for s, v in _prune(tickets):
            if skip_own and s is self.sem:
                continue
            k = id(s)
            if self.waited.get(k, 0) < v:
                self.ops.append(lambda e, s=s, v=v: e.wait_ge(s, v))
                self.waited[k] = v


class Prog:
    def __init__(self, nc, es, n_dma_sems=48):
        self.nc = nc
        self.es = es
        self.pe = Eng(self, "pe")
        self.act = Eng(self, "act")
        self.dve = Eng(self, "dve")
        self.pool = Eng(self, "pool")
        self.sp = Eng(self, "sp")
        self.engs = [self.pe, self.act, self.dve, self.pool, self.sp]
        self.dsems = [[es.enter_context(nc.semaphore("d%d" % i)), 0] for i in range(n_dma_sems)]
        self.free_dsems = list(range(n_dma_sems))
        self.bufs = []

    def buf(self, name="b"):
        b = Buf(name)
        self.bufs.append(b)
        return b

    def bufs_n(self, n, name="b"):
        return [self.buf(name + str(i)) for i in range(n)]

    def _deps(self, reads, writes):
        deps = []
        for b in reads:
            deps += b.w
        for b in writes:
            deps += b.w
            deps += b.r
        return deps

    def op(self, eng, fn, reads=(), writes=()):
        eng.wait(self._deps(reads, writes), skip_own=(eng is self.pe))
        eng.count += 1
        t = (eng.sem, eng.count)
        eng.ops.append(lambda e, fn=fn, s=eng.sem: fn(e).then_inc(s, 1))
        for b in reads:
            b.r = _prune(b.r + [t])
        for b in writes:
            b.w = [t]
            b.r = []
        return t

    def dma(self, eng, fn, reads=(), writes=()):
        owner = (list(writes) + list(reads))[0]
        if owner.sem is None:
            owner.sem = self.dsems[self.free_dsems.pop()]
        rec = owner.sem
        eng.wait(self._deps(reads, writes))
        rec[1] += 16
        t = (rec[0], rec[1])
        eng.ops.append(lambda e, fn=fn, s=rec[0]: fn(e).then_inc(s, 16))
        for b in reads:
            b.r = _prune(b.r + [t])
        for b in writes:
            b.w = _prune([x for x in b.w if x[0] is rec[0]] + [t])
            b.r = []
        return t

    def barrier(self):
        ts = [(e.sem, e.count) for e in self.engs if e.count > 0]
        ts += [(r[0], r[1]) for r in self.dsems if r[1] > 0]
        for e in self.engs:
            e.wait(ts)
        for b in self.bufs:
            b.w = []
            b.r = []
            if b.sem is not None:
                self.free_dsems.append(self.dsems.index(b.sem))
                b.sem = None
        self.bufs = []

    def emit(self):
        with self.nc.Block() as block:
            @block.sync
            def _(e):
                for op in self.sp.ops:
                    op(e)

            @block.tensor
            def _(e):
                for op in self.pe.ops:
                    op(e)

            @block.scalar
            def _(e):
                for op in self.act.ops:
                    op(e)

            @block.vector
            def _(e):
                for op in self.dve.ops:
                    op(e)

            @block.gpsimd
            def _(e):
                for op in self.pool.ops:
                    op(e)


class Rot:
    def __init__(self, P, alloc, n, name, shape, dt):
        self.t = [alloc("%s%d" % (name, i), shape, dt) for i in range(n)]
        self.b = [P.buf("%s%d" % (name, i)) for i in range(n)]
        self.i = 0

    def next(self):
        k = self.i % len(self.t)
        self.i += 1
        return self.t[k], self.b[k]


def build(debug=False, stop_after=None, taps=()):
    nc = bass.Bass("TRN2", target_bir_lowering=False)
    dram_in = lambda n, sh, dt=F32: nc.dram_tensor(n, list(sh), dt, kind="ExternalInput").ap()
    dram_sc = lambda n, sh, dt=F32: nc.dram_tensor(n, list(sh), dt, kind=("ExternalOutput" if n in taps else "Internal")).ap()

    x_in = dram_in("x", [S, D])
    mem_in = dram_in("mem", [256, D])
    ident_in = dram_in("ident", [128, 128])
    ropeA_in = dram_in("ropeA", [2, 32, S])
    ropeB_in = dram_in("ropeB", [2, 128, S])
    w_in_d = dram_in("w_in", [DEPTH, D, D_IN])
    w_in_sw_d = dram_in("w_in_sw", [DEPTH, D, 608])
    w_uq_d = dram_in("w_uq", [DEPTH, 256, 576])
    w_uq_sw_d = dram_in("w_uq_sw", [DEPTH, 256, 576])
    w_ukv_d = dram_in("w_ukv", [DEPTH, 128, 768])
    wsT_d = dram_in("wsT", [DEPTH, 4, 128, 128])
    bs_d = dram_in("bs", [DEPTH, 4, 128])
    gvn_d = dram_in("gvn", [DEPTH, 256])
    w_out_d = dram_in("w_out", [DEPTH, D, D])
    mem_w_q_d = dram_in("mem_w_q", [DEPTH, D, 512])
    mem_w_kv_d = dram_in("mem_w_kv", [DEPTH, D, 1024])
    mem_w_o_d = dram_in("mem_w_o", [DEPTH, 512, D])
    w_up_d = dram_in("w_up", [DEPTH, D, 2 * D_FF])
    w_dn_d = dram_in("w_dn", [DEPTH, D_FF, D])
    NSM = 223
    small_d = dram_in("small", [DEPTH, 128, NSM])
    fin_d = dram_in("fin", [128, 8])
    y_out = nc.dram_tensor("y", [S, D], F32, kind="ExternalOutput").ap()

    xA = dram_sc("xA", [8, 128, S])
    xB = dram_sc("xB", [8, 128, S])
    xC = dram_sc("xC", [8, 128, S])
    QA = dram_sc("QA", [6, 96, S], BF16)
    KN = dram_sc("KN", [3, 128, S], BF16)
    KPE = dram_sc("KPE", [32, S], BF16)
    VA = dram_sc("VA", [128, 64, 390], BF16)
    QB = dram_sc("QB", [3, 128, S], BF16)
    KB = dram_sc("KB", [128, S], BF16)
    VB = dram_sc("VB", [128, 64, 130], BF16)
    YT = dram_sc("YT", [8, 128, S])

    O_MIX, O_OUTN, O_MEMX, O_MEMKV, O_FFN = 0, 8, 16, 24, 32
    O_QN, O_KVN = 40, 42
    O_GQ1, O_GQ2, O_GK1, O_GK2 = 43, 44, 45, 46
    O_CW, O_CB = 47, 47 + 132

    with ExitStack() as es:
        P = Prog(nc, es)
        uid = [0]

        def U(n):
            uid[0] += 1
            return "%s_u%d" % (n, uid[0])

        SB = lambda n, sh, dt: es.enter_context(nc.sbuf_tensor(U(n), list(sh), dt))

        def MM(out, lhsT, rhs, start=True, stop=True, R=(), W=()):
            return P.op(P.pe, lambda e: e.matmul(out, lhsT=lhsT, rhs=rhs, start=start, stop=stop), reads=R, writes=W)

        def TR(out, in_, ident, R=(), W=()):
            return P.op(P.pe, lambda e: e.transpose(out, in_, ident), reads=R, writes=W)

        def ACT(out, in_, func, R=(), W=(), **kw):
            return P.op(P.act, lambda e: e.activation(out=out, in_=in_, func=func, **kw), reads=R, writes=W)

        def TS_(eng, out, in0, s1, s2, op0, op1=None, R=(), W=()):
            if op1 is None:
                return P.op(eng, lambda e: e.tensor_scalar(out=out, in0=in0, scalar1=s1, scalar2=None, op0=op0), reads=R, writes=W)
            return P.op(eng, lambda e: e.tensor_scalar(out=out, in0=in0, scalar1=s1, scalar2=s2, op0=op0, op1=op1), reads=R, writes=W)

        def STT(eng, out, in0, scalar, in1, op0, op1, R=(), W=()):
            return P.op(eng, lambda e: e.scalar_tensor_tensor(out=out, in0=in0, scalar=scalar, in1=in1, op0=op0, op1=op1), reads=R, writes=W)

        def TT(eng, out, in0, in1, op, R=(), W=()):
            return P.op(eng, lambda e: e.tensor_tensor(out=out, in0=in0, in1=in1, op=op), reads=R, writes=W)

        def CP(eng, out, in_, R=(), W=()):
            return P.op(eng, lambda e: e.tensor_copy(out=out, in_=in_), reads=R, writes=W)

        def RCP(out, in_, R=(), W=()):
            return P.op(P.dve, lambda e: e.reciprocal(out=out, in_=in_), reads=R, writes=W)

        def MSET(eng, ap, val, W=()):
            return P.op(eng, lambda e: e.memset(ap, val), writes=W)

        def LD(out, in_, W, R=()):
            return P.dma(P.sp, lambda e: e.dma_start(out=out, in_=in_), reads=R, writes=W)

        def ST(out, in_, R):
            return P.dma(P.pool, lambda e: e.dma_start(out=out, in_=in_), reads=R)

        ident = SB("ident", [128, 128], F32); b_ident = P.buf()
        ones_bf = SB("ones_bf", [128, 128], BF16); b_ones = P.buf()
        blk_bf = SB("blk_bf", [128, 128], BF16); b_blk = P.buf()
        ones_f = SB("ones_f", [128, 64], F32); b_onesf = P.buf()
        small = SB("small", [128, DEPTH, NSM], F32); b_small = P.buf()
        fin = SB("fin", [128, 8], F32); b_fin = P.buf()
        LD(ident[:], ident_in[:, :], W=[b_ident])
        LD(small[:], small_d.rearrange("l p n -> p l n"), W=[b_small])
        LD(fin[:], fin_d[:, :], W=[b_fin])
        MSET(P.dve, ones_bf[:], 1.0, W=[b_ones])
        MSET(P.dve, blk_bf[:], 0.0, W=[b_blk])
        MSET(P.dve, blk_bf[0:64, 0:64], 1.0, W=[b_blk])
        MSET(P.dve, blk_bf[64:128, 64:128], 1.0, W=[b_blk])
        MSET(P.dve, ones_f[:], 1.0, W=[b_onesf])
        constbufs = [b_ident, b_ones, b_blk, b_onesf, b_small, b_fin]

        def keep_consts():
            P.bufs.extend(constbufs)

        def smallcol(l, off, n=1):
            return small[:, l, off:off + n]

        def phase_scope():
            pes = ExitStack()
            sb = lambda n, sh, dt: pes.enter_context(nc.sbuf_tensor(U(n), list(sh), dt))
            ps = lambda n, sh, dt: pes.enter_context(nc.psum_tensor(U(n), list(sh), dt))
            return pes, sb, ps

        def rsqrt_tile(dst, src_ps, inv_n, R, W):
            ACT(dst, src_ps, AF.Sqrt, R=R, W=W, scale=inv_n, bias=EPS)
            RCP(dst, dst, R=W, W=W)

        def load_weight(dst_bf, src_dram, rows, cols, stage, gain_ap=None, col_chunk=2048):
            for c0 in range(0, cols, col_chunk):
                c1 = min(cols, c0 + col_chunk)
                st, bst = stage.next()
                LD(st[0:rows, 0:c1 - c0], src_dram[:, c0:c1], W=[bst])
                if gain_ap is None:
                    CP(P.dve, dst_bf(c0, c1), st[0:rows, 0:c1 - c0], R=[bst], W=[])
                else:
                    TS_(P.dve, dst_bf(c0, c1), st[0:rows, 0:c1 - c0], gain_ap, None, ALU.mult, R=[bst, b_small], W=[])

        memT = SB("memT", [128, 8, 256], F32); b_memT = P.buf()
        constbufs.append(b_memT)
        pes, sb, ps = phase_scope()
        with pes:
            xin = Rot(P, sb, 2, "xin", [128, 4, D], F32)
            xo = Rot(P, sb, 2, "xo", [128, 8, TS], F32)
            pbank = Rot(P, ps, 4, "pb", [128, 512], F32)
            mt, bmt = xin.next()
            LD(mt[:, 0:2, :], mem_in.rearrange("(s p) d -> p s d", p=128), W=[bmt])
            for c in range(8):
                pb, bpb = pbank.next()
                for s_ in range(2):
                    TR(pb[:, s_ * 128:(s_ + 1) * 128], mt[:, s_, c * 128:(c + 1) * 128], ident[:], R=[bmt, b_ident], W=[bpb])
                CP(P.dve, memT[:, c, :], pb[:, 0:256], R=[bpb], W=[b_memT])
            for i in range(NT):
                xt, bxt = xin.next()
                LD(xt[:], x_in[i * TS:(i + 1) * TS, :].rearrange("(s p) d -> p s d", p=128), W=[bxt])
                xo_t, bxo = xo.next()
                for c in range(8):
                    pb, bpb = pbank.next()
                    for s_ in range(4):
                        TR(pb[:, s_ * 128:(s_ + 1) * 128], xt[:, s_, c * 128:(c + 1) * 128], ident[:], R=[bxt, b_ident], W=[bpb])
                    if c % 2 == 0:
                        CP(P.dve, xo_t[:, c, :], pb[:], R=[bpb], W=[bxo])
                    else:
                        P.op(P.act, lambda e, o=xo_t[:, c, :], i_=pb[:]: e.copy(out=o, in_=i_), reads=[bpb], writes=[bxo])
                ST(xA[:, :, i * TS:(i + 1) * TS].rearrange("c p t -> p c t"), xo_t[:], R=[bxo])
            P.barrier()
        keep_consts()
        if stop_after == "p0":
            P.emit()
            return nc

        for l in range(DEPTH):
            pes, sb, ps = phase_scope()
            with pes:
                w_in = sb("w_in", [128, 8, D_IN], BF16)
                w_sw = sb("w_sw", [128, 8, 608], BF16)
                w_uq = sb("w_uq", [128, 2, 576], BF16)
                w_uqs = sb("w_uqs", [128, 2, 576], BF16)
                w_ukv = sb("w_ukv", [128, 768], BF16)
                wsT = sb("wsT", [128, 4, 128], BF16)
                gvn = sb("gvn", [128, 256], F32); b_gvn = P.buf()
                biasT = sb("biasT", [128, 2, 128], F32); b_biasT = P.buf()
                b_w = P.buf("weights")
                wes = ExitStack()
                with wes:
                    stage = Rot(P, lambda n, sh, dt: wes.enter_context(nc.sbuf_tensor(U(n), list(sh), dt)), 3, "stg", [128, 2048], F32)
                    for c in range(8):
                        g = smallcol(l, O_MIX + c)
                        load_weight(lambda a, b, c=c: w_in[:, c, a:b], w_in_d[l, c * 128:(c + 1) * 128, :], 128, D_IN, stage, g)
                        load_weight(lambda a, b, c=c: w_sw[:, c, a:b], w_in_sw_d[l, c * 128:(c + 1) * 128, :], 128, 608, stage, g)
                    for c in range(2):
                        g = smallcol(l, O_QN + c)
                        load_weight(lambda a, b, c=c: w_uq[:, c, a:b], w_uq_d[l, c * 128:(c + 1) * 128, :], 128, 576, stage, g)
                        load_weight(lambda a, b, c=c: w_uqs[:, c, a:b], w_uq_sw_d[l, c * 128:(c + 1) * 128, :], 128, 576, stage, g)
                    load_weight(lambda a, b: w_ukv[:, a:b], w_ukv_d[l, :, :], 128, 768, stage, smallcol(l, O_KVN))
                    for g_ in range(4):
                        load_weight(lambda a, b, g_=g_: wsT[:, g_, a:b], wsT_d[l, g_, :, :], 128, 128, stage, None)
                    LD(gvn[:], gvn_d[l:l + 1, :].broadcast_to([128, 256]), W=[b_gvn])
                    for g_ in range(4):
                        k_, half = g_ // 2, g_ % 2
                        LD(biasT[half * 64:(half + 1) * 64, k_, :], bs_d[l, g_:g_ + 1, :].broadcast_to([64, 128]), W=[b_biasT])
                    P.barrier()
                keep_consts()
                P.bufs.extend([b_w, b_gvn, b_biasT])

                xt_r = Rot(P, sb, 2, "xt", [128, 8, TS], F32)
                sq_r = Rot(P, sb, 1, "sq", [128, 8, TS], BF16)
                h_r = Rot(P, sb, 2, "h", [128, 8, TS], BF16)
                tabA_r = Rot(P, sb, 2, "tabA", [96, 2, TS], F32)
                tabB_r = Rot(P, sb, 2, "tabB", [128, 2, TS], F32)
                tf = Rot(P, sb, 8, "tf", [128, TS], F32)
                tb = Rot(P, sb, 4, "tb", [128, TS], BF16)
                cqn_r = Rot(P, sb, 1, "cqn", [128, 2, TS], BF16)
                qa_r = Rot(P, sb, 2, "qa", [96, 6, TS], BF16)
                kn_r = Rot(P, sb, 2, "kn", [128, 3, TS], BF16)
                kpe_r = Rot(P, sb, 2, "kpe", [96, TS], BF16)
                va_r = Rot(P, sb, 2, "va", [128, 4, 390], BF16)
                qb_r = Rot(P, sb, 2, "qb", [128, 3, TS], BF16)
                kb_r = Rot(P, sb, 2, "kb", [128, TS], BF16)
                vb_r = Rot(P, sb, 2, "vb", [128, 4, 130], BF16)
                u_r = Rot(P, sb, 1, "u", [128, 2, TS], F32)
                yc_r = Rot(P, sb, 2, "yc", [128, 2, TS], F32)
                vvg_r = Rot(P, sb, 2, "vvg", [128, 256], F32)
                vvn_r = Rot(P, sb, 2, "vvn", [128, 256], BF16)
                sc_r = Rot(P, sb, 4, "sc", [128, 2], F32)
                pbank = Rot(P, ps, 8, "pb", [128, 512], F32)
                for k in range(2):
                    for s_ in range(4):
                        MSET(P.pool, va_r.t[k][:, s_, :], 1.0, W=[va_r.b[k]])
                        MSET(P.pool, vb_r.t[k][:, s_, :], 1.0, W=[vb_r.b[k]])

                for i in range(NT):
                    t0 = i * TS
                    xt, bxt = xt_r.next()
                    LD(xt[:], xA[:, :, t0:t0 + TS].rearrange("c p t -> p c t"), W=[bxt])
                    tabA, btA = tabA_r.next()
                    LD(tabA[64:96, :, :], ropeA_in[:, :, t0:t0 + TS].rearrange("k p t -> p k t"), W=[btA])
                    tabB, btB = tabB_r.next()
                    LD(tabB[:], ropeB_in[:, :, t0:t0 + TS].rearrange("k p t -> p k t"), W=[btB])
                    sq, bsq = sq_r.next()
                    ACT(sq[:], xt[:], AF.Square, R=[bxt], W=[bsq])
                    pb, bpb = pbank.next()
                    for c in range(8):
                        MM(pb[:], ones_bf[:], sq[:, c, :], start=(c == 0), stop=(c == 7), R=[bsq, b_ones], W=[bpb])
                    r0, br0 = tf.next()
                    rsqrt_tile(r0[:], pb[:], 1.0 / D, R=[bpb], W=[br0])
                    h, bh = h_r.next()
                    for c in range(8):
                        TT(P.dve if c % 2 == 0 else P.pool, h[:, c, :], xt[:, c, :], r0[:], ALU.mult, R=[bxt, br0], W=[bh])

                    def proj(wt, col0, ncol, R=(bh, b_w)):
                        pb_, bpb_ = pbank.next()
                        for c in range(8):
                            MM(pb_[0:ncol, :], wt[:, c, col0:col0 + ncol], h[:, c, :], start=(c == 0), stop=(c == 7), R=list(R), W=[bpb_])
                        return pb_, bpb_

                    cq = [proj(w_in, 0, 128), proj(w_in, 128, 128)]
                    sqc, bsqc = [], []
                    for k in range(2):
                        t_, b_ = tb.next()
                        ACT(t_[:], cq[k][0][:], AF.Square, R=[cq[k][1]], W=[b_])
                        sqc.append(t_); bsqc.append(b_)
                    pss, bpss = pbank.next()
                    for k in range(2):
                        MM(pss[:], ones_bf[:], sqc[k][:], start=(k == 0), stop=(k == 1), R=[bsqc[k], b_ones], W=[bpss])
                    rq, brq = tf.next()
                    rsqrt_tile(rq[:], pss[:], 1.0 / 256, R=[bpss], W=[brq])
                    cqn, bcqn = cqn_r.next()
                    for k in range(2):
                        TT(P.dve, cqn[:, k, :], cq[k][0][:], rq[:], ALU.mult, R=[cq[k][1], brq], W=[bcqn])
                    qa, bqa = qa_r.next()
                    for hh in range(6):
                        pq, bpq = pbank.next()
                        pqs, bpqs = pbank.next()
                        for k in range(2):
                            MM(pq[0:96, :], w_uq[:, k, hh * 96:(hh + 1) * 96], cqn[:, k, :], start=(k == 0), stop=(k == 1), R=[bcqn, b_w], W=[bpq])
                        for k in range(2):
                            MM(pqs[0:96, :], w_uqs[:, k, hh * 96:(hh + 1) * 96], cqn[:, k, :], start=(k == 0), stop=(k == 1), R=[bcqn, b_w], W=[bpqs])
                        P.op(P.act, lambda e, o=qa[0:64, hh, :], i_=pq[0:64, :]: e.copy(out=o, in_=i_), reads=[bpq], writes=[bqa])
                        t1, bt1 = tf.next()
                        t2, bt2 = tf.next()
                        TT(P.dve, t1[64:96, :], pq[64:96, :], tabA[64:96, 0, :], ALU.mult, R=[bpq, btA], W=[bt1])
                        TT(P.dve, t2[64:96, :], pqs[64:96, :], tabA[64:96, 1, :], ALU.mult, R=[bpqs, btA], W=[bt2])
                        TT(P.pool, qa[64:96, hh, :], t1[64:96, :], t2[64:96, :], ALU.add, R=[bt1, bt2], W=[bqa])
                    ST(QA[:, :, t0:t0 + TS].rearrange("h p t -> p h t"), qa[:], R=[bqa])

                    ckv = proj(w_in, 256, 128)
                    kr = proj(w_in, 320, 96)
                    krs = proj(w_sw, 0, 96)
                    t_, b_ = tb.next()
                    ACT(t_[:], ckv[0][:], AF.Square, R=[ckv[1]], W=[b_])
                    pss, bpss = pbank.next()
                    MM(pss[:], ones_bf[:], t_[:], R=[b_, b_ones], W=[bpss])
                    rk, brk = tf.next()
                    rsqrt_tile(rk[:], pss[:], 1.0 / 128, R=[bpss], W=[brk])
                    ckvn, bckvn = tb.next()
                    TT(P.dve, ckvn[:], ckv[0][:], rk[:], ALU.mult, R=[ckv[1], brk], W=[bckvn])
                    kn, bkn = kn_r.next()
                    for b3 in range(3):
                        pk, bpk = pbank.next()
                        MM(pk[:], w_ukv[:, b3 * 128:(b3 + 1) * 128], ckvn[:], R=[bckvn, b_w], W=[bpk])
                        if b3 % 2 == 0:
                            P.op(P.act, lambda e, o=kn[:, b3, :], i_=pk[:]: e.copy(out=o, in_=i_), reads=[bpk], writes=[bkn])
                        else:
                            CP(P.dve, kn[:, b3, :], pk[:], R=[bpk], W=[bkn])
                    ST(KN[:, :, t0:t0 + TS].rearrange("c p t -> p c t"), kn[:], R=[bkn])
                    kpe, bkpe = kpe_r.next()
                    t1, bt1 = tf.next()
                    t2, bt2 = tf.next()
                    TT(P.dve, t1[64:96, :], kr[0][64:96, :], tabA[64:96, 0, :], ALU.mult, R=[kr[1], btA], W=[bt1])
                    TT(P.dve, t2[64:96, :], krs[0][64:96, :], tabA[64:96, 1, :], ALU.mult, R=[krs[1], btA], W=[bt2])
                    TT(P.pool, kpe[64:96, :], t1[64:96, :], t2[64:96, :], ALU.add, R=[bt1, bt2], W=[bkpe])
                    ST(KPE[:, t0:t0 + TS], kpe[64:96, :], R=[bkpe])
                    va, bva = va_r.next()
                    for s_ in range(4):
                        pv, bpv = pbank.next()
                        MM(pv[:, 0:384], ckvn[:, s_ * 128:(s_ + 1) * 128], w_ukv[:, 384:768], R=[bckvn, b_w], W=[bpv])
                        dst = va[:, s_, :].rearrange("p (h c) -> p h c", c=65)[:, :, 0:64]
                        src = pv[:, 0:384].rearrange("p (h c) -> p h c", c=64)
                        if s_ % 2 == 0:
                            CP(P.dve, dst, src, R=[bpv], W=[bva])
                        else:
                            P.op(P.act, lambda e, o=dst, i_=src: e.copy(out=o, in_=i_), reads=[bpv], writes=[bva])
                    ST(VA[:, 4 * i:4 * i + 4, :], va[:], R=[bva])

                    qb, bqb = qb_r.next()
                    kb, bkb = kb_r.next()
                    for cc in range(4):
                        if cc < 3:
                            raw = proj(w_in, 416 + 128 * cc, 128)
                            swp = proj(w_sw, 96 + 128 * cc, 128)
                            og1, og2 = O_GQ1, O_GQ2
                            dst, bdst = qb[:, cc, :], bqb
                        else:
                            raw = proj(w_in, 800, 128)
                            swp = proj(w_sw, 96 + 384, 128)
                            og1, og2 = O_GK1, O_GK2
                            dst, bdst = kb[:], bkb
                        t_, b_ = tb.next()
                        ACT(t_[:], raw[0][:], AF.Square, R=[raw[1]], W=[b_])
                        pss, bpss = pbank.next()
                        MM(pss[:], blk_bf[:], t_[:], R=[b_, b_blk], W=[bpss])
                        rr, brr = tf.next()
                        rsqrt_tile(rr[:], pss[:], 1.0 / 64, R=[bpss], W=[brr])
                        t1, bt1 = tf.next()
                        t2, bt2 = tf.next()
                        STT(P.dve, t1[:], raw[0][:], smallcol(l, og1), tabB[:, 0, :], ALU.mult, ALU.mult, R=[raw[1], btB, b_small], W=[bt1])
                        STT(P.dve, t2[:], swp[0][:], smallcol(l, og2), tabB[:, 1, :], ALU.mult, ALU.mult, R=[swp[1], btB, b_small], W=[bt2])
                        TT(P.pool, t1[:], t1[:], t2[:], ALU.add, R=[bt2], W=[bt1])
                        TT(P.pool, dst, t1[:], rr[:], ALU.mult, R=[bt1, brr], W=[bdst])
                    ST(QB[:, :, t0:t0 + TS].rearrange("c p t -> p c t"), qb[:], R=[bqb])
                    ST(KB[:, t0:t0 + TS], kb[:], R=[bkb])
                    vb, bvb = vb_r.next()
                    for s_ in range(4):
                        pv, bpv = pbank.next()
                        for c in range(8):
                            MM(pv[:, 0:128], h[:, c, s_ * 128:(s_ + 1) * 128], w_in[:, c, 928:1056], start=(c == 0), stop=(c == 7), R=[bh, b_w], W=[bpv])
                        dst = vb[:, s_, :].rearrange("p (h c) -> p h c", c=65)[:, :, 0:64]
                        src = pv[:, 0:128].rearrange("p (h c) -> p h c", c=64)
                        P.op(P.act, lambda e, o=dst, i_=src: e.copy(out=o, in_=i_), reads=[bpv], writes=[bvb])
                    ST(VB[:, 4 * i:4 * i + 4, :], vb[:], R=[bvb])

                    u, bu = u_r.next()
                    for k in range(2):
                        pu = proj(w_in, 1056 + 128 * k, 128)
                        ACT(u[:, k, :], pu[0][:], AF.Gelu_apprx_tanh, R=[pu[1]], W=[bu])
                    yc, byc = yc_r.next()
                    for s_ in range(4):
                        pv, bpv = pbank.next()
                        for c in range(8):
                            MM(pv[:, 0:256], h[:, c, s_ * 128:(s_ + 1) * 128], w_in[:, c, 1312:1568], start=(c == 0), stop=(c == 7), R=[bh, b_w], W=[bpv])
                        vvg, bvvg = vvg_r.next()
                        ACT(vvg[:], pv[:, 0:256], AF.Gelu_apprx_tanh, R=[bpv], W=[bvvg])
                        sc, bsc = sc_r.next()
                        junk, bjunk = tf.next()
                        ACT(junk[:, 0:256], vvg[:], AF.Square, R=[bvvg], W=[bjunk, bsc], accum_out=sc[:, 0:1])
                        ACT(sc[:, 1:2], sc[:, 0:1], AF.Sqrt, R=[bsc], W=[bsc], scale=1.0 / 256, bias=EPS)
                        RCP(sc[:, 1:2], sc[:, 1:2], R=[bsc], W=[bsc])
                        vvn, bvvn = vvn_r.next()
                        STT(P.dve, vvn[:], vvg[:], sc[:, 1:2], gvn[:], ALU.mult, ALU.mult, R=[bvvg, bsc, b_gvn], W=[bvvn])
                        for k in range(2):
                            for half in range(2):
                                g_ = 2 * k + half
                                pm, bpm = pbank.next()
                                MM(pm[:, 0:128], vvn[:, k * 128:(k + 1) * 128], wsT[:, g_, :], R=[bvvn, b_w], W=[bpm])
                                lo, hi = half * 64, half * 64 + 64
                                tm, btm = tf.next()
                                TT(P.dve, tm[lo:hi, 0:128], pm[lo:hi, 0:128], biasT[lo:hi, k, :], ALU.add, R=[bpm, b_biasT], W=[btm])
                                TT(P.pool, yc[lo:hi, k, s_ * 128:(s_ + 1) * 128], tm[lo:hi, 0:128], u[lo:hi, k, s_ * 128:(s_ + 1) * 128], ALU.mult, R=[btm, bu], W=[byc])
                    ST(YT[6:8, :, t0:t0 + TS].rearrange("c p t -> p c t"), yc[:], R=[byc])
                P.barrier()
            keep_consts()
            if stop_after == "a%d" % l:
                P.emit()
                return nc

            def attention(group):
                pes, sb, ps = phase_scope()
                with pes:
                    if group == "a":
                        dk, nh, vw = 96, 6, 390
                        Vd = VA
                        scale = 96 ** -0.5
                    else:
                        dk, nh, vw = 64, 6, 130
                        Vd = VB
                        scale = 64 ** -0.5
                    v_sb = sb("v_sb", [128, 64, vw], BF16); b_v = P.buf()
                    LD(v_sb[:, 0:32, :], Vd[:, 0:32, :], W=[b_v])
                    LD(v_sb[:, 32:64, :], Vd[:, 32:64, :], W=[b_v])
                    kt_r = Rot(P, sb, 2, "kt", [dk, S], BF16)
                    q_r = Rot(P, sb, 3, "q", [dk, TS], BF16)
                    p_r = Rot(P, sb, 3, "p", [128, 2 * TS], BF16)
                    rd_r = Rot(P, sb, 2, "rd", [65, TS], F32)
                    bc_r = Rot(P, sb, 2, "bc", [64, TS], F32)
                    o_r = Rot(P, sb, 2, "o", [64, TS], F32)
                    s_r = Rot(P, ps, 3, "s", [128, 2 * TS], F32)
                    o_ps = ps("o_ps", [128, TS], F32); b_ops = P.buf()
                    bc_ps = ps("bc_ps", [128, TS], F32); b_bcps = P.buf()

                    def load_k(hh):
                        kt, bkt = kt_r.next()
                        if group == "a":
                            src = KN[hh // 2, (hh % 2) * 64:(hh % 2) * 64 + 64, :]
                            for q4 in range(4):
                                LD(kt[0:64, q4 * 2048:(q4 + 1) * 2048], src[:, q4 * 2048:(q4 + 1) * 2048], W=[bkt])
                                LD(kt[64:96, q4 * 2048:(q4 + 1) * 2048], KPE[:, q4 * 2048:(q4 + 1) * 2048], W=[bkt])
                        else:
                            src = KB[hh * 64:hh * 64 + 64, :]
                            for q4 in range(4):
                                LD(kt[:, q4 * 2048:(q4 + 1) * 2048], src[:, q4 * 2048:(q4 + 1) * 2048], W=[bkt])
                        return kt, bkt

                    def load_q(hh, qt):
                        q, bq = q_r.next()
                        if group == "a":
                            LD(q[:], QA[hh, :, qt * TS:(qt + 1) * TS], W=[bq])
                        else:
                            LD(q[:], QB[hh // 2, (hh % 2) * 64:(hh % 2) * 64 + 64, qt * TS:(qt + 1) * TS], W=[bq])
                        return q, bq

                    nkv = 6 if group == "a" else 2
                    kcur = load_k(0)
                    for hh in range(nh):
                        kvh = hh if group == "a" else hh // 3
                        if group == "a":
                            knext = load_k(hh + 1) if hh + 1 < nh else None
                        else:
                            knext = load_k(1) if hh == 2 else None
                        kt, bkt = kcur
                        qn = load_q(hh, 0)
                        for qt in range(NT):
                            q, bq = qn
                            if qt + 1 < NT:
                                qn = load_q(hh, qt + 1)
                            NJ = 32
                            sb_list = {}

                            def do_s(jj):
                                st, bst = s_r.next()
                                for k2 in range(2):
                                    j = 2 * jj + k2
                                    MM(st[:, k2 * TS:(k2 + 1) * TS], kt[:, j * 128:(j + 1) * 128], q[:], R=[bkt, bq], W=[bst])
                                sb_list[jj] = (st, bst)

                            def do_exp(jj):
                                st, bst = sb_list[jj]
                                p, bp = p_r.next()
                                ACT(p[:], st[:], AF.Exp, R=[bst], W=[bp], scale=scale)
                                return p, bp

                            def do_pv(jj, p, bp):
                                for k2 in range(2):
                                    j = 2 * jj + k2
                                    MM(o_ps[0:65, :], v_sb[:, j, kvh * 65:(kvh + 1) * 65], p[:, k2 * TS:(k2 + 1) * TS],
                                       start=(j == 0), stop=(j == 63), R=[b_v, bp], W=[b_ops])

                            do_s(0)
                            do_s(1)
                            for jj in range(NJ):
                                p, bp = do_exp(jj)
                                if jj + 2 < NJ:
                                    do_s(jj + 2)
                                do_pv(jj, p, bp)
                            rd, brd = rd_r.next()
                            RCP(rd[64:65, :], o_ps[64:65, :], R=[b_ops], W=[brd])
                            MM(bc_ps[0:64, :], ones_f[64:65, 0:64], rd[64:65, :], R=[brd, b_onesf], W=[b_bcps])
                            bc, bbc = bc_r.next()
                            CP(P.dve, bc[:], bc_ps[0:64, :], R=[b_bcps], W=[bbc])
                            o, bo = o_r.next()
                            TT(P.dve, o[:], o_ps[0:64, :], bc[:], ALU.mult, R=[b_ops, bbc], W=[bo])
                            chunk = (0 if group == "a" else 3) + hh // 2
                            ST(YT[chunk, (hh % 2) * 64:(hh % 2) * 64 + 64, qt * TS:(qt + 1) * TS], o[:], R=[bo])
                        if knext is not None:
                            kcur = knext
                    P.barrier()
                keep_consts()

            attention("a")
            if stop_after == "ba%d" % l:
                P.emit()
                return nc
            attention("b")
            if stop_after == "b%d" % l:
                P.emit()
                return nc

            pes, sb, ps = phase_scope()
            with pes:
                w_out = sb("w_out", [128, 8, D], BF16)
                w_q = sb("w_q", [128, 8, 512], BF16)
                w_kv = sb("w_kv", [128, 8, 1024], BF16)
                w_o = sb("w_o", [128, 4, D], BF16)
                km = sb("km", [128, 4, 256], BF16); b_km = P.buf()
                vm = sb("vm", [128, 2, 512], BF16); b_vm = P.buf()
                b_w = P.buf("weights")
                wes = ExitStack()
                with wes:
                    wsb = lambda n, sh, dt: wes.enter_context(nc.sbuf_tensor(U(n), list(sh), dt))
                    stage = Rot(P, wsb, 3, "stg", [128, 2048], F32)
                    for c in range(8):
                        load_weight(lambda a, b, c=c: w_out[:, c, a:b], w_out_d[l, c * 128:(c + 1) * 128, :], 128, D, stage, smallcol(l, O_OUTN + c))
                        load_weight(lambda a, b, c=c: w_q[:, c, a:b], mem_w_q_d[l, c * 128:(c + 1) * 128, :], 128, 512, stage, smallcol(l, O_MEMX + c))
                        load_weight(lambda a, b, c=c: w_kv[:, c, a:b], mem_w_kv_d[l, c * 128:(c + 1) * 128, :], 128, 1024, stage, smallcol(l, O_MEMKV + c))
                    for c in range(4):
                        load_weight(lambda a, b, c=c: w_o[:, c, a:b], mem_w_o_d[l, c * 128:(c + 1) * 128, :], 128, D, stage, None)
                    P.barrier()
                    keep_consts()
                    P.bufs.extend([b_km, b_vm, b_w])
                    msq = wsb("msq", [128, 8, 256], BF16); b_msq = P.buf()
                    memn = wsb("memn", [128, 8, 256], BF16); b_memn = P.buf()
                    mr = wsb("mr", [128, 256], F32); b_mr = P.buf()
                    pbk = Rot(P, lambda n, sh, dt: wes.enter_context(nc.psum_tensor(U(n), list(sh), dt)), 4, "pbm", [128, 512], F32)
                    ACT(msq[:], memT[:], AF.Square, R=[b_memT], W=[b_msq])
                    pb, bpb = pbk.next()
                    for c in range(8):
                        MM(pb[:, 0:256], ones_bf[:], msq[:, c, :], start=(c == 0), stop=(c == 7), R=[b_msq, b_ones], W=[bpb])
                    rsqrt_tile(mr[:], pb[:, 0:256], 1.0 / D, R=[bpb], W=[b_mr])
                    for c in range(8):
                        TT(P.dve, memn[:, c, :], memT[:, c, :], mr[:], ALU.mult, R=[b_memT, b_mr], W=[b_memn])
                    for hm in range(4):
                        pb, bpb = pbk.next()
                        for c in range(8):
                            MM(pb[:, 0:256], w_kv[:, c, hm * 128:(hm + 1) * 128], memn[:, c, :], start=(c == 0), stop=(c == 7), R=[b_memn], W=[bpb])
                        CP(P.dve, km[:, hm, :], pb[:, 0:256], R=[bpb], W=[b_km])
                    for kt_ in range(2):
                        pb, bpb = pbk.next()
                        for c in range(8):
                            MM(pb[:], memn[:, c, kt_ * 128:(kt_ + 1) * 128], w_kv[:, c, 512:1024], start=(c == 0), stop=(c == 7), R=[b_memn], W=[bpb])
                        CP(P.dve, vm[:, kt_, :], pb[:], R=[bpb], W=[b_vm])
                    P.barrier()
                keep_consts()
                P.bufs.extend([b_km, b_vm, b_w])

                xt_r = Rot(P, sb, 2, "xt", [128, 8, TS], F32)
                yt_r = Rot(P, sb, 2, "yt", [128, 8, TS], F32)
                sq_r = Rot(P, sb, 1, "sq", [128, 8, TS], BF16)
                yn_r = Rot(P, sb, 1, "yn", [128, 8, TS], BF16)
                h2_r = Rot(P, sb, 1, "h2", [128, 8, TS], BF16)
                qm_r = Rot(P, sb, 1, "qm", [128, 4, TS], BF16)
                om_r = Rot(P, sb, 1, "om", [128, 4, TS], BF16)
                pm_r = Rot(P, sb, 2, "pm", [128, 2 * TS], BF16)
                tf = Rot(P, sb, 6, "tf", [128, TS], F32)
                pbank = Rot(P, ps, 4, "pb", [128, 512], F32)
                s_r = Rot(P, ps, 2, "s", [128, 2 * TS], F32)
                mscale = 128 ** -0.5
                for i in range(NT):
                    t0 = i * TS
                    xt, bxt = xt_r.next()
                    LD(xt[:], xA[:, :, t0:t0 + TS].rearrange("c p t -> p c t"), W=[bxt])
                    yt, byt = yt_r.next()
                    LD(yt[:], YT[:, :, t0:t0 + TS].rearrange("c p t -> p c t"), W=[byt])
                    sq, bsq = sq_r.next()
                    ACT(sq[:], yt[:], AF.Square, R=[byt], W=[bsq])
                    yn, byn = yn_r.next()
                    for (c0, c1) in ((0, 3), (3, 6), (6, 8)):
                        pb, bpb = pbank.next()
                        for c in range(c0, c1):
                            MM(pb[:], ones_bf[:], sq[:, c, :], start=(c == c0), stop=(c == c1 - 1), R=[bsq, b_ones], W=[bpb])
                        rr, brr = tf.next()
                        rsqrt_tile(rr[:], pb[:], 1.0 / (128 * (c1 - c0)), R=[bpb], W=[brr])
                        for c in range(c0, c1):
                            TT(P.dve if c % 2 == 0 else P.pool, yn[:, c, :], yt[:, c, :], rr[:], ALU.mult, R=[byt, brr], W=[byn])
                    for nb in range(8):
                        pb, bpb = pbank.next()
                        for c in range(8):
                            MM(pb[:], w_out[:, c, nb * 128:(nb + 1) * 128], yn[:, c, :], start=(c == 0), stop=(c == 7), R=[byn, b_w], W=[bpb])
                        TT(P.dve, xt[:, nb, :], pb[:], xt[:, nb, :], ALU.add, R=[bpb], W=[bxt])
                    sq, bsq = sq_r.next()
                    ACT(sq[:], xt[:], AF.Square, R=[bxt], W=[bsq])
                    pb, bpb = pbank.next()
                    for c in range(8):
                        MM(pb[:], ones_bf[:], sq[:, c, :], start=(c == 0), stop=(c == 7), R=[bsq, b_ones], W=[bpb])
                    rr, brr = tf.next()
                    rsqrt_tile(rr[:], pb[:], 1.0 / D, R=[bpb], W=[brr])
                    h2, bh2 = h2_r.next()
                    for c in range(8):
                        TT(P.dve if c % 2 == 0 else P.pool, h2[:, c, :], xt[:, c, :], rr[:], ALU.mult, R=[bxt, brr], W=[bh2])
                    qm, bqm = qm_r.next()
                    for hm in range(4):
                        pb, bpb = pbank.next()
                        for c in range(8):
                            MM(pb[:], w_q[:, c, hm * 128:(hm + 1) * 128], h2[:, c, :], start=(c == 0), stop=(c == 7), R=[bh2, b_w], W=[bpb])
                        P.op(P.act, lambda e, o=qm[:, hm, :], i_=pb[:]: e.copy(out=o, in_=i_), reads=[bpb], writes=[bqm])
                    om, bom = om_r.next()
                    for hm in range(4):
                        st, bst = s_r.next()
                        for kt_ in range(2):
                            MM(st[:, kt_ * TS:(kt_ + 1) * TS], km[:, hm, kt_ * 128:(kt_ + 1) * 128], qm[:, hm, :], R=[b_km, bqm], W=[bst])
                        pm, bpm = pm_r.next()
                        ACT(pm[:], st[:], AF.Exp, R=[bst], W=[bpm], scale=mscale)
                        po, bpo = pbank.next()
                        for kt_ in range(2):
                            MM(po[:], vm[:, kt_, hm * 128:(hm + 1) * 128], pm[:, kt_ * TS:(kt_ + 1) * TS], start=(kt_ == 0), stop=(kt_ == 1), R=[b_vm, bpm], W=[bpo])
                        pd, bpd = pbank.next()
                        for kt_ in range(2):
                            MM(pd[:], ones_bf[:], pm[:, kt_ * TS:(kt_ + 1) * TS], start=(kt_ == 0), stop=(kt_ == 1), R=[b_ones, bpm], W=[bpd])
                        rd, brd = tf.next()
                        RCP(rd[:], pd[:], R=[bpd], W=[brd])
                        TT(P.dve, om[:, hm, :], po[:], rd[:], ALU.mult, R=[bpo, brd], W=[bom])
                    for nb in range(8):
                        pb, bpb = pbank.next()
                        for c in range(4):
                            MM(pb[:], w_o[:, c, nb * 128:(nb + 1) * 128], om[:, c, :], start=(c == 0), stop=(c == 3), R=[bom, b_w], W=[bpb])
                        TT(P.dve, xt[:, nb, :], pb[:], xt[:, nb, :], ALU.add, R=[bpb], W=[bxt])
                    ST(xB[:, :, t0:t0 + TS].rearrange("c p t -> p c t"), xt[:], R=[bxt])
                P.barrier()
            keep_consts()
            if stop_after == "c%d" % l:
                P.emit()
                return nc

            NF = 11
            TW = 510
            NTD = (S + TW - 1) // TW
            for half in range(2):
                pes, sb, ps = phase_scope()
                with pes:
                    w_up = sb("w_up", [128, 8, 2 * NF * 128], BF16)
                    w_dn = sb("w_dn", [128, NF, D], BF16)
                    b_w = P.buf("weights")
                    wes = ExitStack()
                    with wes:
                        stage = Rot(P, lambda n, sh, dt: wes.enter_context(nc.sbuf_tensor(U(n), list(sh), dt)), 3, "stg", [128, 2048], F32)
                        f0 = half * NF * 128
                        for c in range(8):
                            g = smallcol(l, O_FFN + c)
                            load_weight(lambda a, b, c=c: w_up[:, c, a:b], w_up_d[l, c * 128:(c + 1) * 128, f0:f0 + NF * 128], 128, NF * 128, stage, g)
                            load_weight(lambda a, b, c=c: w_up[:, c, NF * 128 + a:NF * 128 + b], w_up_d[l, c * 128:(c + 1) * 128, D_FF + f0:D_FF + f0 + NF * 128], 128, NF * 128, stage, g)
                        for f in range(NF):
                            load_weight(lambda a, b, f=f: w_dn[:, f, a:b], w_dn_d[l, f0 + f * 128:f0 + (f + 1) * 128, :], 128, D, stage, None)
                        P.barrier()
                    keep_consts()
                    P.bufs.append(b_w)
                    xt_r = Rot(P, sb, 2, "xt", [128, 8, TS], F32)
                    ac_r = Rot(P, sb, 2, "ac", [128, 8, TS], F32) if half == 1 else None
                    sq_r = Rot(P, sb, 1, "sq", [128, 8, TS], BF16)
                    h_r = Rot(P, sb, 1, "h", [128, 8, TS], BF16)
                    g_r = Rot(P, sb, 1, "g", [128, NF, TS], BF16)
                    tf = Rot(P, sb, 8, "tf", [128, TS], F32)
                    pbank = Rot(P, ps, 8, "pb", [128, 512], F32)
                    for i in range(NTD):
                        t0 = i * TW
                        a_lo = t0 - 1
                        n_out = min(TW, S - t0)
                        W_ = n_out + 2
                        lo_tok = max(a_lo, 0)
                        hi_tok = min(a_lo + W_, S)
                        c_lo = lo_tok - a_lo
                        c_hi = hi_tok - a_lo
                        xt, bxt = xt_r.next()
                        if c_lo > 0:
                            MSET(P.pool, xt[:, :, 0:c_lo], 0.0, W=[bxt])
                        if c_hi < W_:
                            MSET(P.pool, xt[:, :, c_hi:W_], 0.0, W=[bxt])
                        LD(xt[:, :, c_lo:c_hi], xB[:, :, lo_tok:hi_tok].rearrange("c p t -> p c t"), W=[bxt])
                        if half == 1:
                            ac, bac = ac_r.next()
                            LD(ac[:, :, 1:1 + n_out], xC[:, :, t0:t0 + n_out].rearrange("c p t -> p c t"), W=[bac])
                        else:
                            ac, bac = xt, bxt
                        sq, bsq = sq_r.next()
                        ACT(sq[:, :, 0:W_], xt[:, :, 0:W_], AF.Square, R=[bxt], W=[bsq])
                        pb, bpb = pbank.next()
                        for c in range(8):
                            MM(pb[:, 0:W_], ones_bf[:], sq[:, c, 0:W_], start=(c == 0), stop=(c == 7), R=[bsq, b_ones], W=[bpb])
                        rr, brr = tf.next()
                        rsqrt_tile(rr[:, 0:W_], pb[:, 0:W_], 1.0 / D, R=[bpb], W=[brr])
                        h, bh = h_r.next()
                        for c in range(8):
                            TT(P.dve if c % 2 == 0 else P.pool, h[:, c, 0:W_], xt[:, c, 0:W_], rr[:, 0:W_], ALU.mult, R=[bxt, brr], W=[bh])
                        g, bg = g_r.next()
                        for f in range(NF):
                            fg = half * NF + f
                            res = []
                            for part in range(2):
                                pb, bpb = pbank.next()
                                for c in range(8):
                                    MM(pb[:, 0:W_], w_up[:, c, (part * NF + f) * 128:(part * NF + f + 1) * 128], h[:, c, 0:W_],
                                       start=(c == 0), stop=(c == 7), R=[bh, b_w], W=[bpb])
                                blk = fg + part * 22
                                cw = O_CW + blk * 3
                                t_, bt_ = tf.next()
                                ACT(t_[:, 0:n_out], pb[:, 0:n_out], AF.Identity, R=[bpb, b_small], W=[bt_],
                                    scale=smallcol(l, cw), bias=smallcol(l, O_CB + blk))
                                STT(P.dve, t_[:, 0:n_out], pb[:, 1:1 + n_out], smallcol(l, cw + 1), t_[:, 0:n_out], ALU.mult, ALU.add, R=[bpb, b_small], W=[bt_])
                                STT(P.dve, t_[:, 0:n_out], pb[:, 2:2 + n_out], smallcol(l, cw + 2), t_[:, 0:n_out], ALU.mult, ALU.add, R=[bpb, b_small], W=[bt_])
                                res.append((t_, bt_))
                            ACT(res[0][0][:, 0:n_out], res[0][0][:, 0:n_out], AF.Silu, R=[res[0][1]], W=[res[0][1]])
                            TT(P.pool, g[:, f, 0:n_out], res[0][0][:, 0:n_out], res[1][0][:, 0:n_out], ALU.mult, R=[res[0][1], res[1][1]], W=[bg])
                        for nb in range(8):
                            pb, bpb = pbank.next()
                            for f in range(NF):
                                MM(pb[:, 0:n_out], w_dn[:, f, nb * 128:(nb + 1) * 128], g[:, f, 0:n_out], start=(f == 0), stop=(f == NF - 1), R=[bg, b_w], W=[bpb])
                            TT(P.dve, ac[:, nb, 1:1 + n_out], pb[:, 0:n_out], ac[:, nb, 1:1 + n_out], ALU.add, R=[bpb] + ([bxt] if half == 0 else []), W=[bac])
                        dstT = xC if half == 0 else xA
                        ST(dstT[:, :, t0:t0 + n_out].rearrange("c p t -> p c t"), ac[:, :, 1:1 + n_out], R=[bac])
                    P.barrier()
                keep_consts()
            if stop_after == "d%d" % l:
                P.emit()
                return nc

        pes, sb, ps = phase_scope()
        with pes:
            xt_r = Rot(P, sb, 2, "xt", [128, 8, TS], F32)
            sq_r = Rot(P, sb, 1, "sq", [128, 8, TS], BF16)
            xn_r = Rot(P, sb, 2, "xn", [128, 8, TS], F32)
            yo_r = Rot(P, sb, 2, "yo", [128, 4, D], F32)
            tf = Rot(P, sb, 2, "tf", [128, TS], F32)
            pbank = Rot(P, ps, 6, "pb", [128, 512], F32)
            for i in range(NT):
                t0 = i * TS
                xt, bxt = xt_r.next()
                LD(xt[:], xA[:, :, t0:t0 + TS].rearrange("c p t -> p c t"), W=[bxt])
                sq, bsq = sq_r.next()
                ACT(sq[:], xt[:], AF.Square, R=[bxt], W=[bsq])
                pb, bpb = pbank.next()
                for c in range(8):
                    MM(pb[:], ones_bf[:], sq[:, c, :], start=(c == 0), stop=(c == 7), R=[bsq, b_ones], W=[bpb])
                rr, brr = tf.next()
                rsqrt_tile(rr[:], pb[:], 1.0 / D, R=[bpb], W=[brr])
                xn, bxn = xn_r.next()
                for c in range(8):
                    STT(P.dve, xn[:, c, :], xt[:, c, :], fin[:, c:c + 1], rr[:], ALU.mult, ALU.mult, R=[bxt, brr, b_fin], W=[bxn])
                yo, byo = yo_r.next()
                for s_ in range(4):
                    for c4 in range(2):
                        pb, bpb = pbank.next()
                        for cc in range(4):
                            c = c4 * 4 + cc
                            TR(pb[:, cc * 128:(cc + 1) * 128], xn[:, c, s_ * 128:(s_ + 1) * 128], ident[:], R=[bxn, b_ident], W=[bpb])
                        if (s_ + c4) % 2 == 0:
                            CP(P.dve, yo[:, s_, c4 * 512:(c4 + 1) * 512], pb[:], R=[bpb], W=[byo])
                        else:
                            P.op(P.act, lambda e, o=yo[:, s_, c4 * 512:(c4 + 1) * 512], i_=pb[:]: e.copy(out=o, in_=i_), reads=[bpb], writes=[byo])
                ST(y_out[t0:t0 + TS, :].rearrange("(s p) d -> p s d", p=128), yo[:], R=[byo])
            P.barrier()
        P.emit()
    return nc


def _swap_pairs(w):
    idx = np.arange(w.shape[-1]).reshape(-1, 2)[:, ::-1].reshape(-1)
    return w[..., idx]


def _rope_tables():
    rows = S // 64
    row = np.repeat(np.arange(rows, dtype=np.float32), 64)
    col = np.tile(np.arange(64, dtype=np.float32), rows)

    def tab(d_rot):
        n = d_rot // 4
        inv = (np.float32(10000.0) ** (-np.arange(n, dtype=np.float32) / np.float32(n))).astype(np.float32)
        ang = np.concatenate([row[:, None] * inv, col[:, None] * inv], axis=-1).astype(np.float32)
        c = np.cos(ang).astype(np.float32)
        s = np.sin(ang).astype(np.float32)
        cf = np.repeat(c, 2, axis=1)
        sf = np.repeat(s, 2, axis=1)
        sign = np.tile(np.array([-1.0, 1.0], np.float32), d_rot // 2)
        return np.ascontiguousarray(cf.T), np.ascontiguousarray((sf * sign).T)

    ca, sa = tab(32)
    cb, sb_ = tab(64)
    ropeA = np.stack([ca, sa]).astype(np.float32)
    ropeB = np.stack([np.concatenate([cb, cb]), np.concatenate([sb_, sb_])]).astype(np.float32)
    return ropeA, ropeB


def _host_layout(inputs):
    f = lambda k: np.asarray(inputs[k], dtype=np.float32)
    L = DEPTH
    w_in = f("w_in")
    sw_src = np.concatenate([w_in[:, :, 320:416], w_in[:, :, 416:800], w_in[:, :, 800:928]], axis=-1)
    w_in_sw = _swap_pairs(sw_src)
    w_uq = f("mla_w_uq")
    w_uq_sw = _swap_pairs(w_uq)
    w_ukv = f("mla_w_ukv").reshape(L, 128, 6, 2, 64)
    w_ukv_p = np.concatenate([w_ukv[:, :, :, 0, :].reshape(L, 128, 384), w_ukv[:, :, :, 1, :].reshape(L, 128, 384)], axis=-1)
    wsT = np.ascontiguousarray(f("gmlp_w_s").transpose(0, 1, 3, 2))
    pc = lambda v: v.reshape(L, -1, 128).transpose(0, 2, 1)
    gq = f("gqa_q_norm"); gk = f("gqa_k_norm")
    sw64 = np.arange(64).reshape(-1, 2)[:, ::-1].reshape(-1)
    tile2 = lambda v: np.concatenate([v, v], axis=-1)[:, :, None]
    cw = f("ffn_conv_w")
    cwp = cw.reshape(L, 3, 44, 128).transpose(0, 3, 2, 1).reshape(L, 128, 132)
    cb = pc(f("ffn_conv_b"))
    small = np.concatenate([
        pc(f("mix_norm")), pc(f("out_norm")), pc(f("mem_x_norm")), pc(f("mem_kv_norm")), pc(f("ffn_norm")),
        pc(f("mla_q_norm")), pc(f("mla_kv_norm")),
        tile2(gq), tile2(gq[:, sw64]), tile2(gk), tile2(gk[:, sw64]),
        cwp, cb], axis=-1).astype(np.float32)
    ropeA, ropeB = _rope_tables()
    shared = {
        "ident": np.eye(128, dtype=np.float32),
        "ropeA": ropeA, "ropeB": ropeB,
        "w_in": w_in, "w_in_sw": np.ascontiguousarray(w_in_sw),
        "w_uq": w_uq, "w_uq_sw": np.ascontiguousarray(w_uq_sw),
        "w_ukv": np.ascontiguousarray(w_ukv_p),
        "wsT": wsT, "bs": f("gmlp_b_s"), "gvn": f("gmlp_v_norm"),
        "w_out": f("w_out"), "mem_w_q": f("mem_w_q"), "mem_w_kv": f("mem_w_kv"), "mem_w_o": f("mem_w_o"),
        "w_up": f("ffn_w_up"), "w_dn": f("ffn_w_down"),
        "small": np.ascontiguousarray(small),
        "fin": np.ascontiguousarray(f("final_norm").reshape(8, 128).T),
    }
    return shared


_NC_CACHE = {}


def kernel(**inputs):
    shared = _host_layout(inputs)
    x = np.asarray(inputs["x"], dtype=np.float32)
    mem = np.asarray(inputs["mem"], dtype=np.float32)
    if "nc" not in _NC_CACHE:
        _NC_CACHE["nc"] = build()
    nc = _NC_CACHE["nc"]
    in_maps = []
    for c in range(NCORES):
        m = dict(shared)
        m["x"] = np.ascontiguousarray(x[c])
        m["mem"] = np.ascontiguousarray(mem[c])
        in_maps.append(m)
    res = run_bass_kernel_spmd(nc, in_maps, core_ids=list(range(NCORES)))
    return np.stack([res.results[c]["y"] for c in range(NCORES)], axis=0).astype(np.float32)
```

```python
from contextlib import ExitStack
import numpy as np
import concourse.bass as bass
import concourse.mybir as mybir
from concourse.bass_utils import run_bass_kernel_spmd

F32 = mybir.dt.float32
BF16 = mybir.dt.bfloat16
AF = mybir.ActivationFunctionType
ALU = mybir.AluOpType

S = 8192
D = 1024
TS = 512
NT = S // TS
DEPTH = 2
D_IN = 1568
D_FF = 2816
EPS = 1e-6
NCORES = 8


class Buf:
    __slots__ = ("name", "w", "r", "sem", "sem2")

    def __init__(self, name):
        self.name = name
        self.w = []
        self.r = []
        self.sem = None
        self.sem2 = None


def _prune(ts):
    best = {}
    for s, v in ts:
        k = id(s)
        if k not in best or best[k][1] < v:
            best[k] = (s, v)
    return list(best.values())


class Eng:
    def __init__(self, prog, name):
        self.prog = prog
        self.name = name
        self.ops = []
        self.sem = prog.es.enter_context(prog.nc.semaphore("s_" + name))
        self.count = 0
        self.waited = {}
        self.needed = set()
        prog.sem2eng[id(self.sem)] = self

    def wait(self, tickets, skip_own=False):
        for s, v in _prune(tickets):
            if skip_own and s is self.sem:
                continue
            k = id(s)
            if self.waited.get(k, 0) < v:
                self.ops.append(("wait", s, v))
                self.waited[k] = v
                src = self.prog.sem2eng.get(k)
                if src is not None:
                    src.needed.add(v)

    def replay(self, e):
        ranks = {}
        for eng in self.prog.engs:
            ranks[id(eng.sem)] = {v: i + 1 for i, v in enumerate(sorted(eng.needed))}
        mine = ranks[id(self.sem)]
        for ent in self.ops:
            if ent[0] == "wait":
                _, s_, v = ent
                r = ranks.get(id(s_))
                e.wait_ge(s_, r[v] if r is not None else v)
            elif ent[0] == "op":
                ins = ent[1](e)
                if ent[2] in mine:
                    ins.then_inc(self.sem, 1)
            else:
                ent[1](e).then_inc(ent[2], 16)


class Prog:
    def __init__(self, nc, es, n_dma_sems=48):
        self.nc = nc
        self.es = es
        self.sem2eng = {}
        self.pe = Eng(self, "pe")
        self.act = Eng(self, "act")
        self.dve = Eng(self, "dve")
        self.pool = Eng(self, "pool")
        self.sp = Eng(self, "sp")
        self.engs = [self.pe, self.act, self.dve, self.pool, self.sp]
        self.dsems = [[es.enter_context(nc.semaphore("d%d" % i)), 0] for i in range(n_dma_sems)]
        self.free_dsems = list(range(n_dma_sems))
        self.bufs = []

    def buf(self, name="b"):
        b = Buf(name)
        self.bufs.append(b)
        return b

    def bufs_n(self, n, name="b"):
        return [self.buf(name + str(i)) for i in range(n)]

    def _deps(self, reads, writes):
        deps = []
        for b in reads:
            deps += b.w
        for b in writes:
            deps += b.w
            deps += b.r
        return deps

    def op(self, eng, fn, reads=(), writes=()):
        eng.wait(self._deps(reads, writes), skip_own=(eng is self.pe))
        eng.count += 1
        t = (eng.sem, eng.count)
        eng.ops.append(("op", fn, eng.count))
        for b in reads:
            b.r = _prune(b.r + [t])
        for b in writes:
            b.w = [t]
            b.r = []
        return t

    def dma(self, eng, fn, reads=(), writes=()):
        owner = (list(writes) + list(reads))[0]
        if eng is self.pool:
            if owner.sem2 is None:
                owner.sem2 = self.dsems[self.free_dsems.pop()]
            rec = owner.sem2
        else:
            if owner.sem is None:
                owner.sem = self.dsems[self.free_dsems.pop()]
            rec = owner.sem
        eng.wait(self._deps(reads, writes))
        rec[1] += 16
        t = (rec[0], rec[1])
        eng.ops.append(("dma", fn, rec[0]))
        for b in reads:
            b.r = _prune(b.r + [t])
        for b in writes:
            b.w = _prune([x for x in b.w if x[0] is rec[0]] + [t])
            b.r = []
        return t

    def barrier(self):
        ts = [(e.sem, e.count) for e in self.engs if e.count > 0]
        ts += [(r[0], r[1]) for r in self.dsems if r[1] > 0]
        for e in self.engs:
            e.wait(ts)
        for b in self.bufs:
            b.w = []
            b.r = []
            if b.sem is not None:
                self.free_dsems.append(self.dsems.index(b.sem))
                b.sem = None
            if b.sem2 is not None:
                self.free_dsems.append(self.dsems.index(b.sem2))
                b.sem2 = None
        self.bufs = []

    def emit(self):
        with self.nc.Block() as block:
            block.sync(self.sp.replay)
            block.tensor(self.pe.replay)
            block.scalar(self.act.replay)
            block.vector(self.dve.replay)
            block.gpsimd(self.pool.replay)


class Rot:
    def __init__(self, P, alloc, n, name, shape, dt):
        self.t = [alloc("%s%d" % (name, i), shape, dt) for i in range(n)]
        self.b = [P.buf("%s%d" % (name, i)) for i in range(n)]
        self.i = 0

    def next(self):
        k = self.i % len(self.t)
        self.i += 1
        return self.t[k], self.b[k]


def build(debug=False, stop_after=None, taps=()):
    nc = bass.Bass("TRN2", target_bir_lowering=False)
    dram_in = lambda n, sh, dt=F32: nc.dram_tensor(n, list(sh), dt, kind="ExternalInput").ap()
    dram_sc = lambda n, sh, dt=F32: nc.dram_tensor(n, list(sh), dt, kind=("ExternalOutput" if n in taps else "Internal")).ap()

    x_in = dram_in("x", [S, D])
    mem_in = dram_in("mem", [256, D])
    ident_in = dram_in("ident", [128, 128])
    ropeA_in = dram_in("ropeA", [2, 32, S])
    ropeB_in = dram_in("ropeB", [2, 128, S])
    w_in_d = dram_in("w_in", [DEPTH, D, D_IN])
    w_in_sw_d = dram_in("w_in_sw", [DEPTH, D, 608])
    w_uq_d = dram_in("w_uq", [DEPTH, 256, 576])
    w_uq_sw_d = dram_in("w_uq_sw", [DEPTH, 256, 576])
    w_ukv_d = dram_in("w_ukv", [DEPTH, 128, 768])
    wsT_d = dram_in("wsT", [DEPTH, 4, 128, 128])
    bs_d = dram_in("bs", [DEPTH, 4, 128])
    gvn_d = dram_in("gvn", [DEPTH, 256])
    w_out_d = dram_in("w_out", [DEPTH, D, D])
    mem_w_q_d = dram_in("mem_w_q", [DEPTH, D, 512])
    mem_w_kv_d = dram_in("mem_w_kv", [DEPTH, D, 1024])
    mem_w_o_d = dram_in("mem_w_o", [DEPTH, 512, D])
    w_up_d = dram_in("w_up", [DEPTH, D, 2 * D_FF])
    w_dn_d = dram_in("w_dn", [DEPTH, D_FF, D])
    NSM = 223
    small_d = dram_in("small", [DEPTH, 128, NSM])
    fin_d = dram_in("fin", [128, 8])
    y_out = nc.dram_tensor("y", [S, D], F32, kind="ExternalOutput").ap()

    xA = dram_sc("xA", [8, 128, S])
    xB = dram_sc("xB", [8, 128, S])
    xC = dram_sc("xC", [8, 128, S])
    QA = dram_sc("QA", [6, 96, S], BF16)
    KN = dram_sc("KN", [3, 128, S], BF16)
    KPE = dram_sc("KPE", [32, S], BF16)
    VA = dram_sc("VA", [128, 64, 390], BF16)
    QB = dram_sc("QB", [3, 128, S], BF16)
    KB = dram_sc("KB", [128, S], BF16)
    VB = dram_sc("VB", [128, 64, 130], BF16)
    YT = dram_sc("YT", [8, 128, S])
    DEN = dram_sc("DEN", [4, TS])

    O_MIX, O_OUTN, O_MEMX, O_MEMKV, O_FFN = 0, 8, 16, 24, 32
    O_QN, O_KVN = 40, 42
    O_GQ1, O_GQ2, O_GK1, O_GK2 = 43, 44, 45, 46
    O_CW, O_CB = 47, 47 + 132

    with ExitStack() as es:
        P = Prog(nc, es)
        uid = [0]

        def U(n):
            uid[0] += 1
            return "%s_u%d" % (n, uid[0])

        SB = lambda n, sh, dt: es.enter_context(nc.sbuf_tensor(U(n), list(sh), dt))

        def MM(out, lhsT, rhs, start=True, stop=True, R=(), W=()):
            return P.op(P.pe, lambda e: e.matmul(out, lhsT=lhsT, rhs=rhs, start=start, stop=stop), reads=R, writes=W)

        def TR(out, in_, ident, R=(), W=()):
            return P.op(P.pe, lambda e: e.transpose(out, in_, ident), reads=R, writes=W)

        def ACT(out, in_, func, R=(), W=(), **kw):
            return P.op(P.act, lambda e: e.activation(out=out, in_=in_, func=func, **kw), reads=R, writes=W)

        def TS_(eng, out, in0, s1, s2, op0, op1=None, R=(), W=()):
            if op1 is None:
                return P.op(eng, lambda e: e.tensor_scalar(out=out, in0=in0, scalar1=s1, scalar2=None, op0=op0), reads=R, writes=W)
            return P.op(eng, lambda e: e.tensor_scalar(out=out, in0=in0, scalar1=s1, scalar2=s2, op0=op0, op1=op1), reads=R, writes=W)

        def STT(eng, out, in0, scalar, in1, op0, op1, R=(), W=()):
            return P.op(eng, lambda e: e.scalar_tensor_tensor(out=out, in0=in0, scalar=scalar, in1=in1, op0=op0, op1=op1), reads=R, writes=W)

        def TT(eng, out, in0, in1, op, R=(), W=()):
            return P.op(eng, lambda e: e.tensor_tensor(out=out, in0=in0, in1=in1, op=op), reads=R, writes=W)

        def CP(eng, out, in_, R=(), W=()):
            return P.op(eng, lambda e: e.tensor_copy(out=out, in_=in_), reads=R, writes=W)

        def RCP(out, in_, R=(), W=()):
            return P.op(P.dve, lambda e: e.reciprocal(out=out, in_=in_), reads=R, writes=W)

        def MSET(eng, ap, val, W=()):
            return P.op(eng, lambda e: e.memset(ap, val), writes=W)

        def LD(out, in_, W, R=()):
            return P.dma(P.sp, lambda e: e.dma_start(out=out, in_=in_), reads=R, writes=W)

        def ST(out, in_, R):
            return P.dma(P.pool, lambda e: e.dma_start(out=out, in_=in_), reads=R)

        ident = SB("ident", [128, 128], F32); b_ident = P.buf()
        ones_bf = SB("ones_bf", [128, 128], BF16); b_ones = P.buf()
        blk_bf = SB("blk_bf", [128, 128], BF16); b_blk = P.buf()
        ones_f = SB("ones_f", [128, 64], F32); b_onesf = P.buf()
        small = SB("small", [128, DEPTH, NSM], F32); b_small = P.buf()
        fin = SB("fin", [128, 8], F32); b_fin = P.buf()
        LD(ident[:], ident_in[:, :], W=[b_ident])
        LD(small[:], small_d.rearrange("l p n -> p l n"), W=[b_small])
        LD(fin[:], fin_d[:, :], W=[b_fin])
        MSET(P.dve, ones_bf[:], 1.0, W=[b_ones])
        MSET(P.dve, blk_bf[:], 0.0, W=[b_blk])
        MSET(P.dve, blk_bf[0:64, 0:64], 1.0, W=[b_blk])
        MSET(P.dve, blk_bf[64:128, 64:128], 1.0, W=[b_blk])
        MSET(P.dve, ones_f[:], 1.0, W=[b_onesf])
        constbufs = [b_ident, b_ones, b_blk, b_onesf, b_small, b_fin]

        def keep_consts():
            P.bufs.extend(constbufs)

        def smallcol(l, off, n=1):
            return small[:, l, off:off + n]

        def phase_scope():
            pes = ExitStack()
            sb = lambda n, sh, dt: pes.enter_context(nc.sbuf_tensor(U(n), list(sh), dt))
            ps = lambda n, sh, dt: pes.enter_context(nc.psum_tensor(U(n), list(sh), dt))
            return pes, sb, ps

        def rsqrt_tile(dst, src_ps, inv_n, R, W):
            ACT(dst, src_ps, AF.Sqrt, R=R, W=W, scale=inv_n, bias=EPS)
            RCP(dst, dst, R=W, W=W)

        def load_weight(dst_bf, src_dram, rows, cols, stage, gain_ap=None, col_chunk=2048):
            for c0 in range(0, cols, col_chunk):
                c1 = min(cols, c0 + col_chunk)
                st, bst = stage.next()
                LD(st[0:rows, 0:c1 - c0], src_dram[:, c0:c1], W=[bst])
                if gain_ap is None:
                    CP(P.dve, dst_bf(c0, c1), st[0:rows, 0:c1 - c0], R=[bst], W=[])
                else:
                    TS_(P.dve, dst_bf(c0, c1), st[0:rows, 0:c1 - c0], gain_ap, None, ALU.mult, R=[bst, b_small], W=[])

        memT = SB("memT", [128, 8, 256], F32); b_memT = P.buf()
        constbufs.append(b_memT)
        pes, sb, ps = phase_scope()
        with pes:
            xin = Rot(P, sb, 2, "xin", [128, 4, D], F32)
            xo = Rot(P, sb, 2, "xo", [128, 8, TS], F32)
            pbank = Rot(P, ps, 4, "pb", [128, 512], F32)
            mt, bmt = xin.next()
            LD(mt[:, 0:2, :], mem_in.rearrange("(s p) d -> p s d", p=128), W=[bmt])
            for c in range(8):
                pb, bpb = pbank.next()
                for s_ in range(2):
                    TR(pb[:, s_ * 128:(s_ + 1) * 128], mt[:, s_, c * 128:(c + 1) * 128], ident[:], R=[bmt, b_ident], W=[bpb])
                CP(P.dve, memT[:, c, :], pb[:, 0:256], R=[bpb], W=[b_memT])
            for i in range(NT):
                xt, bxt = xin.next()
                LD(xt[:], x_in[i * TS:(i + 1) * TS, :].rearrange("(s p) d -> p s d", p=128), W=[bxt])
                xo_t, bxo = xo.next()
                for c in range(8):
                    pb, bpb = pbank.next()
                    for s_ in range(4):
                        TR(pb[:, s_ * 128:(s_ + 1) * 128], xt[:, s_, c * 128:(c + 1) * 128], ident[:], R=[bxt, b_ident], W=[bpb])
                    if c % 2 == 0:
                        CP(P.dve, xo_t[:, c, :], pb[:], R=[bpb], W=[bxo])
                    else:
                        P.op(P.act, lambda e, o=xo_t[:, c, :], i_=pb[:]: e.copy(out=o, in_=i_), reads=[bpb], writes=[bxo])
                ST(xA[:, :, i * TS:(i + 1) * TS].rearrange("c p t -> p c t"), xo_t[:], R=[bxo])
            P.barrier()
        keep_consts()
        if stop_after == "p0":
            P.emit()
            return nc

        for l in range(DEPTH):
            pes, sb, ps = phase_scope()
            with pes:
                w_in = sb("w_in", [128, 8, D_IN], BF16)
                w_sw = sb("w_sw", [128, 8, 608], BF16)
                w_uq = sb("w_uq", [128, 2, 576], BF16)
                w_uqs = sb("w_uqs", [128, 2, 576], BF16)
                w_ukv = sb("w_ukv", [128, 768], BF16)
                wsT = sb("wsT", [128, 4, 128], BF16)
                gvn = sb("gvn", [128, 256], F32); b_gvn = P.buf()
                biasT = sb("biasT", [128, 2, 128], F32); b_biasT = P.buf()
                b_w = P.buf("weights")
                wes = ExitStack()
                with wes:
                    stage = Rot(P, lambda n, sh, dt: wes.enter_context(nc.sbuf_tensor(U(n), list(sh), dt)), 3, "stg", [128, 2048], F32)
                    for c in range(8):
                        g = smallcol(l, O_MIX + c)
                        load_weight(lambda a, b, c=c: w_in[:, c, a:b], w_in_d[l, c * 128:(c + 1) * 128, :], 128, D_IN, stage, g)
                        load_weight(lambda a, b, c=c: w_sw[:, c, a:b], w_in_sw_d[l, c * 128:(c + 1) * 128, :], 128, 608, stage, g)
                    for c in range(2):
                        g = smallcol(l, O_QN + c)
                        load_weight(lambda a, b, c=c: w_uq[:, c, a:b], w_uq_d[l, c * 128:(c + 1) * 128, :], 128, 576, stage, g)
                        load_weight(lambda a, b, c=c: w_uqs[:, c, a:b], w_uq_sw_d[l, c * 128:(c + 1) * 128, :], 128, 576, stage, g)
                    load_weight(lambda a, b: w_ukv[:, a:b], w_ukv_d[l, :, :], 128, 768, stage, smallcol(l, O_KVN))
                    for g_ in range(4):
                        load_weight(lambda a, b, g_=g_: wsT[:, g_, a:b], wsT_d[l, g_, :, :], 128, 128, stage, None)
                    LD(gvn[:], gvn_d[l:l + 1, :].broadcast_to([128, 256]), W=[b_gvn])
                    for g_ in range(4):
                        k_, half = g_ // 2, g_ % 2
                        LD(biasT[half * 64:(half + 1) * 64, k_, :], bs_d[l, g_:g_ + 1, :].broadcast_to([64, 128]), W=[b_biasT])
                    P.barrier()
                keep_consts()
                P.bufs.extend([b_w, b_gvn, b_biasT])

                xt_r = Rot(P, sb, 2, "xt", [128, 8, TS], F32)
                sq_r = Rot(P, sb, 1, "sq", [128, 8, TS], BF16)
                h_r = Rot(P, sb, 2, "h", [128, 8, TS], BF16)
                tabA_r = Rot(P, sb, 2, "tabA", [96, 2, TS], F32)
                tabB_r = Rot(P, sb, 2, "tabB", [128, 2, TS], F32)
                tf = Rot(P, sb, 8, "tf", [128, TS], F32)
                tb = Rot(P, sb, 4, "tb", [128, TS], BF16)
                cqn_r = Rot(P, sb, 1, "cqn", [128, 2, TS], BF16)
                qa_r = Rot(P, sb, 2, "qa", [96, 6, TS], BF16)
                kn_r = Rot(P, sb, 2, "kn", [128, 3, TS], BF16)
                kpe_r = Rot(P, sb, 2, "kpe", [96, TS], BF16)
                va_r = Rot(P, sb, 2, "va", [128, 4, 390], BF16)
                qb_r = Rot(P, sb, 2, "qb", [128, 3, TS], BF16)
                kb_r = Rot(P, sb, 2, "kb", [128, TS], BF16)
                vb_r = Rot(P, sb, 2, "vb", [128, 4, 130], BF16)
                u_r = Rot(P, sb, 1, "u", [128, 2, TS], F32)
                yc_r = Rot(P, sb, 2, "yc", [128, 2, TS], F32)
                vvg_r = Rot(P, sb, 2, "vvg", [128, 256], F32)
                vvn_r = Rot(P, sb, 2, "vvn", [128, 256], BF16)
                sc_r = Rot(P, sb, 4, "sc", [128, 2], F32)
                pbank = Rot(P, ps, 8, "pb", [128, 512], F32)
                for k in range(2):
                    for s_ in range(4):
                        MSET(P.pool, va_r.t[k][:, s_, :], 1.0, W=[va_r.b[k]])
                        MSET(P.pool, vb_r.t[k][:, s_, :], 1.0, W=[vb_r.b[k]])

                for i in range(NT):
                    t0 = i * TS
                    xt, bxt = xt_r.next()
                    LD(xt[:], xA[:, :, t0:t0 + TS].rearrange("c p t -> p c t"), W=[bxt])
                    tabA, btA = tabA_r.next()
                    LD(tabA[64:96, :, :], ropeA_in[:, :, t0:t0 + TS].rearrange("k p t -> p k t"), W=[btA])
                    tabB, btB = tabB_r.next()
                    LD(tabB[:], ropeB_in[:, :, t0:t0 + TS].rearrange("k p t -> p k t"), W=[btB])
                    sq, bsq = sq_r.next()
                    ACT(sq[:], xt[:], AF.Square, R=[bxt], W=[bsq])
                    pb, bpb = pbank.next()
                    for c in range(8):
                        MM(pb[:], ones_bf[:], sq[:, c, :], start=(c == 0), stop=(c == 7), R=[bsq, b_ones], W=[bpb])
                    r0, br0 = tf.next()
                    rsqrt_tile(r0[:], pb[:], 1.0 / D, R=[bpb], W=[br0])
                    h, bh = h_r.next()
                    for c in range(8):
                        TT(P.dve if c % 2 == 0 else P.pool, h[:, c, :], xt[:, c, :], r0[:], ALU.mult, R=[bxt, br0], W=[bh])

                    def proj(wt, col0, ncol, R=(bh, b_w)):
                        pb_, bpb_ = pbank.next()
                        for c in range(8):
                            MM(pb_[0:ncol, :], wt[:, c, col0:col0 + ncol], h[:, c, :], start=(c == 0), stop=(c == 7), R=list(R), W=[bpb_])
                        return pb_, bpb_

                    cq = [proj(w_in, 0, 128), proj(w_in, 128, 128)]
                    sqc, bsqc = [], []
                    for k in range(2):
                        t_, b_ = tb.next()
                        ACT(t_[:], cq[k][0][:], AF.Square, R=[cq[k][1]], W=[b_])
                        sqc.append(t_); bsqc.append(b_)
                    pss, bpss = pbank.next()
                    for k in range(2):
                        MM(pss[:], ones_bf[:], sqc[k][:], start=(k == 0), stop=(k == 1), R=[bsqc[k], b_ones], W=[bpss])
                    rq, brq = tf.next()
                    rsqrt_tile(rq[:], pss[:], 1.0 / 256, R=[bpss], W=[brq])
                    cqn, bcqn = cqn_r.next()
                    for k in range(2):
                        TT(P.dve, cqn[:, k, :], cq[k][0][:], rq[:], ALU.mult, R=[cq[k][1], brq], W=[bcqn])
                    qa, bqa = qa_r.next()
                    for hh in range(6):
                        pq, bpq = pbank.next()
                        pqs, bpqs = pbank.next()
                        for k in range(2):
                            MM(pq[0:96, :], w_uq[:, k, hh * 96:(hh + 1) * 96], cqn[:, k, :], start=(k == 0), stop=(k == 1), R=[bcqn, b_w], W=[bpq])
                        for k in range(2):
                            MM(pqs[0:96, :], w_uqs[:, k, hh * 96:(hh + 1) * 96], cqn[:, k, :], start=(k == 0), stop=(k == 1), R=[bcqn, b_w], W=[bpqs])
                        P.op(P.act, lambda e, o=qa[0:64, hh, :], i_=pq[0:64, :]: e.copy(out=o, in_=i_), reads=[bpq], writes=[bqa])
                        t1, bt1 = tf.next()
                        t2, bt2 = tf.next()
                        TT(P.dve, t1[64:96, :], pq[64:96, :], tabA[64:96, 0, :], ALU.mult, R=[bpq, btA], W=[bt1])
                        TT(P.dve, t2[64:96, :], pqs[64:96, :], tabA[64:96, 1, :], ALU.mult, R=[bpqs, btA], W=[bt2])
                        TT(P.pool, qa[64:96, hh, :], t1[64:96, :], t2[64:96, :], ALU.add, R=[bt1, bt2], W=[bqa])
                    ST(QA[:, :, t0:t0 + TS].rearrange("h p t -> p h t"), qa[:], R=[bqa])

                    ckv = proj(w_in, 256, 128)
                    kr = proj(w_in, 320, 96)
                    krs = proj(w_sw, 0, 96)
                    t_, b_ = tb.next()
                    ACT(t_[:], ckv[0][:], AF.Square, R=[ckv[1]], W=[b_])
                    pss, bpss = pbank.next()
                    MM(pss[:], ones_bf[:], t_[:], R=[b_, b_ones], W=[bpss])
                    rk, brk = tf.next()
                    rsqrt_tile(rk[:], pss[:], 1.0 / 128, R=[bpss], W=[brk])
                    ckvn, bckvn = tb.next()
                    TT(P.dve, ckvn[:], ckv[0][:], rk[:], ALU.mult, R=[ckv[1], brk], W=[bckvn])
                    kn, bkn = kn_r.next()
                    for b3 in range(3):
                        pk, bpk = pbank.next()
                        MM(pk[:], w_ukv[:, b3 * 128:(b3 + 1) * 128], ckvn[:], R=[bckvn, b_w], W=[bpk])
                        if b3 % 2 == 0:
                            P.op(P.act, lambda e, o=kn[:, b3, :], i_=pk[:]: e.copy(out=o, in_=i_), reads=[bpk], writes=[bkn])
                        else:
                            CP(P.dve, kn[:, b3, :], pk[:], R=[bpk], W=[bkn])
                    ST(KN[:, :, t0:t0 + TS].rearrange("c p t -> p c t"), kn[:], R=[bkn])
                    kpe, bkpe = kpe_r.next()
                    t1, bt1 = tf.next()
                    t2, bt2 = tf.next()
                    TT(P.dve, t1[64:96, :], kr[0][64:96, :], tabA[64:96, 0, :], ALU.mult, R=[kr[1], btA], W=[bt1])
                    TT(P.dve, t2[64:96, :], krs[0][64:96, :], tabA[64:96, 1, :], ALU.mult, R=[krs[1], btA], W=[bt2])
                    TT(P.pool, kpe[64:96, :], t1[64:96, :], t2[64:96, :], ALU.add, R=[bt1, bt2], W=[bkpe])
                    ST(KPE[:, t0:t0 + TS], kpe[64:96, :], R=[bkpe])
                    va, bva = va_r.next()
                    for s_ in range(4):
                        pv, bpv = pbank.next()
                        MM(pv[:, 0:384], ckvn[:, s_ * 128:(s_ + 1) * 128], w_ukv[:, 384:768], R=[bckvn, b_w], W=[bpv])
                        dst = va[:, s_, :].rearrange("p (h c) -> p h c", c=65)[:, :, 0:64]
                        src = pv[:, 0:384].rearrange("p (h c) -> p h c", c=64)
                        if s_ % 2 == 0:
                            CP(P.dve, dst, src, R=[bpv], W=[bva])
                        else:
                            P.op(P.act, lambda e, o=dst, i_=src: e.copy(out=o, in_=i_), reads=[bpv], writes=[bva])
                    ST(VA[:, 4 * i:4 * i + 4, :], va[:], R=[bva])

                    qb, bqb = qb_r.next()
                    kb, bkb = kb_r.next()
                    for cc in range(4):
                        if cc < 3:
                            raw = proj(w_in, 416 + 128 * cc, 128)
                            swp = proj(w_sw, 96 + 128 * cc, 128)
                            og1, og2 = O_GQ1, O_GQ2
                            dst, bdst = qb[:, cc, :], bqb
                        else:
                            raw = proj(w_in, 800, 128)
                            swp = proj(w_sw, 96 + 384, 128)
                            og1, og2 = O_GK1, O_GK2
                            dst, bdst = kb[:], bkb
                        t_, b_ = tb.next()
                        ACT(t_[:], raw[0][:], AF.Square, R=[raw[1]], W=[b_])
                        pss, bpss = pbank.next()
                        MM(pss[:], blk_bf[:], t_[:], R=[b_, b_blk], W=[bpss])
                        rr, brr = tf.next()
                        rsqrt_tile(rr[:], pss[:], 1.0 / 64, R=[bpss], W=[brr])
                        t1, bt1 = tf.next()
                        t2, bt2 = tf.next()
                        STT(P.dve, t1[:], raw[0][:], smallcol(l, og1), tabB[:, 0, :], ALU.mult, ALU.mult, R=[raw[1], btB, b_small], W=[bt1])
                        STT(P.dve, t2[:], swp[0][:], smallcol(l, og2), tabB[:, 1, :], ALU.mult, ALU.mult, R=[swp[1], btB, b_small], W=[bt2])
                        TT(P.pool, t1[:], t1[:], t2[:], ALU.add, R=[bt2], W=[bt1])
                        TT(P.pool, dst, t1[:], rr[:], ALU.mult, R=[bt1, brr], W=[bdst])
                    ST(QB[:, :, t0:t0 + TS].rearrange("c p t -> p c t"), qb[:], R=[bqb])
                    ST(KB[:, t0:t0 + TS], kb[:], R=[bkb])
                    vb, bvb = vb_r.next()
                    for s_ in range(4):
                        pv, bpv = pbank.next()
                        for c in range(8):
                            MM(pv[:, 0:128], h[:, c, s_ * 128:(s_ + 1) * 128], w_in[:, c, 928:1056], start=(c == 0), stop=(c == 7), R=[bh, b_w], W=[bpv])
                        dst = vb[:, s_, :].rearrange("p (h c) -> p h c", c=65)[:, :, 0:64]
                        src = pv[:, 0:128].rearrange("p (h c) -> p h c", c=64)
                        P.op(P.act, lambda e, o=dst, i_=src: e.copy(out=o, in_=i_), reads=[bpv], writes=[bvb])
                    ST(VB[:, 4 * i:4 * i + 4, :], vb[:], R=[bvb])

                    u, bu = u_r.next()
                    for k in range(2):
                        pu = proj(w_in, 1056 + 128 * k, 128)
                        ACT(u[:, k, :], pu[0][:], AF.Gelu_apprx_tanh, R=[pu[1]], W=[bu])
                    yc, byc = yc_r.next()
                    for s_ in range(4):
                        pv, bpv = pbank.next()
                        for c in range(8):
                            MM(pv[:, 0:256], h[:, c, s_ * 128:(s_ + 1) * 128], w_in[:, c, 1312:1568], start=(c == 0), stop=(c == 7), R=[bh, b_w], W=[bpv])
                        vvg, bvvg = vvg_r.next()
                        ACT(vvg[:], pv[:, 0:256], AF.Gelu_apprx_tanh, R=[bpv], W=[bvvg])
                        sc, bsc = sc_r.next()
                        junk, bjunk = tf.next()
                        ACT(junk[:, 0:256], vvg[:], AF.Square, R=[bvvg], W=[bjunk, bsc], accum_out=sc[:, 0:1])
                        ACT(sc[:, 1:2], sc[:, 0:1], AF.Sqrt, R=[bsc], W=[bsc], scale=1.0 / 256, bias=EPS)
                        RCP(sc[:, 1:2], sc[:, 1:2], R=[bsc], W=[bsc])
                        vvn, bvvn = vvn_r.next()
                        STT(P.dve, vvn[:], vvg[:], sc[:, 1:2], gvn[:], ALU.mult, ALU.mult, R=[bvvg, bsc, b_gvn], W=[bvvn])
                        for k in range(2):
                            for half in range(2):
                                g_ = 2 * k + half
                                pm, bpm = pbank.next()
                                MM(pm[:, 0:128], vvn[:, k * 128:(k + 1) * 128], wsT[:, g_, :], R=[bvvn, b_w], W=[bpm])
                                lo, hi = half * 64, half * 64 + 64
                                tm, btm = tf.next()
                                TT(P.dve, tm[lo:hi, 0:128], pm[lo:hi, 0:128], biasT[lo:hi, k, :], ALU.add, R=[bpm, b_biasT], W=[btm])
                                TT(P.pool, yc[lo:hi, k, s_ * 128:(s_ + 1) * 128], tm[lo:hi, 0:128], u[lo:hi, k, s_ * 128:(s_ + 1) * 128], ALU.mult, R=[btm, bu], W=[byc])
                    ST(YT[6:8, :, t0:t0 + TS].rearrange("c p t -> p c t"), yc[:], R=[byc])
                P.barrier()
            keep_consts()
            if stop_after == "a%d" % l:
                P.emit()
                return nc

            def attention(group):
                pes, sb, ps = phase_scope()
                with pes:
                    if group == "a":
                        dk, nh, vw = 96, 6, 390
                        Vd = VA
                        scale = 96 ** -0.5
                    else:
                        dk, nh, vw = 64, 6, 130
                        Vd = VB
                        scale = 64 ** -0.5
                    v_sb = sb("v_sb", [128, 64, vw], BF16); b_v = P.buf()
                    LD(v_sb[:, 0:32, :], Vd[:, 0:32, :], W=[b_v])
                    LD(v_sb[:, 32:64, :], Vd[:, 32:64, :], W=[b_v])
                    dkp = 96 if group == "a" else 128
                    kt_r = Rot(P, sb, 2, "kt", [dkp, S], BF16)
                    q_r = Rot(P, sb, 3, "q", [dkp, TS], BF16)
                    if group == "b":
                        for k_ in range(2):
                            MSET(P.pool, kt_r.t[k_][64:128, :], 0.0, W=[kt_r.b[k_]])
                        for k_ in range(3):
                            MSET(P.pool, q_r.t[k_][64:128, :], 0.0, W=[q_r.b[k_]])
                    p_r = Rot(P, sb, 3, "p", [128, 2 * TS], BF16)
                    rd_r = Rot(P, sb, 2, "rd", [65, TS], F32)
                    bc_r = Rot(P, sb, 2, "bc", [64, TS], F32)
                    o_r = Rot(P, sb, 2, "o", [64, TS], F32)
                    s_r = Rot(P, ps, 3, "s", [128, 2 * TS], F32)
                    ops_r = Rot(P, ps, 2, "o_ps", [128, TS], F32)

                    def load_k(hh):
                        kt, bkt = kt_r.next()
                        if group == "a":
                            src = KN[hh // 2, (hh % 2) * 64:(hh % 2) * 64 + 64, :]
                            for q4 in range(4):
                                LD(kt[0:64, q4 * 2048:(q4 + 1) * 2048], src[:, q4 * 2048:(q4 + 1) * 2048], W=[bkt])
                                LD(kt[64:96, q4 * 2048:(q4 + 1) * 2048], KPE[:, q4 * 2048:(q4 + 1) * 2048], W=[bkt])
                        else:
                            src = KB[hh * 64:hh * 64 + 64, :]
                            for q4 in range(4):
                                LD(kt[0:64, q4 * 2048:(q4 + 1) * 2048], src[:, q4 * 2048:(q4 + 1) * 2048], W=[bkt])
                        return kt, bkt

                    def load_q(hh, qt):
                        q, bq = q_r.next()
                        if group == "a":
                            LD(q[:], QA[hh, :, qt * TS:(qt + 1) * TS], W=[bq])
                        else:
                            LD(q[0:64, :], QB[hh // 2, (hh % 2) * 64:(hh % 2) * 64 + 64, qt * TS:(qt + 1) * TS], W=[bq])
                        return q, bq

                    b_den = P.bufs_n(4, "den")
                    den_i = [0]
                    kcur = load_k(0)
                    for hh in range(nh):
                        kvh = hh if group == "a" else hh // 3
                        if group == "a":
                            knext = load_k(hh + 1) if hh + 1 < nh else None
                        else:
                            knext = load_k(1) if hh == 2 else None
                        kt, bkt = kcur
                        qn = load_q(hh, 0)
                        for qt in range(NT):
                            q, bq = qn
                            if qt + 1 < NT:
                                qn = load_q(hh, qt + 1)
                            NJ = 32
                            sb_list = {}
                            o_ps, b_ops = ops_r.next()

                            def do_s(jj):
                                st, bst = s_r.next()
                                for k2 in range(2):
                                    j = 2 * jj + k2
                                    MM(st[:, k2 * TS:(k2 + 1) * TS], kt[:, j * 128:(j + 1) * 128], q[:], R=[bkt, bq], W=[bst])
                                sb_list[jj] = (st, bst)

                            def do_exp(jj):
                                st, bst = sb_list[jj]
                                p, bp = p_r.next()
                                ACT(p[:], st[:], AF.Exp, R=[bst], W=[bp], scale=scale)
                                return p, bp

                            def do_pv(jj, p, bp):
                                for k2 in range(2):
                                    j = 2 * jj + k2
                                    MM(o_ps[0:65, :], v_sb[:, j, kvh * 65:(kvh + 1) * 65], p[:, k2 * TS:(k2 + 1) * TS],
                                       start=(j == 0), stop=(j == 63), R=[b_v, bp], W=[b_ops])

                            do_s(0)
                            do_s(1)
                            for jj in range(NJ):
                                p, bp = do_exp(jj)
                                if jj + 2 < NJ:
                                    do_s(jj + 2)
                                do_pv(jj, p, bp)
                            rd, brd = rd_r.next()
                            RCP(rd[64:65, :], o_ps[64:65, :], R=[b_ops], W=[brd])
                            bc, bbc = bc_r.next()
                            dslot = den_i[0] % 4
                            den_i[0] += 1
                            P.dma(P.sp, lambda e, o=DEN[dslot:dslot + 1, :], i_=rd[64:65, :]: e.dma_start(out=o, in_=i_), reads=[brd], writes=[b_den[dslot]])
                            LD(bc[:], DEN[dslot:dslot + 1, :].broadcast_to([64, TS]), W=[bbc], R=[b_den[dslot]])
                            o, bo = o_r.next()
                            TT(P.dve, o[:], o_ps[0:64, :], bc[:], ALU.mult, R=[b_ops, bbc], W=[bo])
                            chunk = (0 if group == "a" else 3) + hh // 2
                            ST(YT[chunk, (hh % 2) * 64:(hh % 2) * 64 + 64, qt * TS:(qt + 1) * TS], o[:], R=[bo])
                        if knext is not None:
                            kcur = knext
                    P.barrier()
                keep_consts()

            attention("a")
            if stop_after == "ba%d" % l:
                P.emit()
                return nc
            attention("b")
            if stop_after == "b%d" % l:
                P.emit()
                return nc

            pes, sb, ps = phase_scope()
            with pes:
                w_out = sb("w_out", [128, 8, D], BF16)
                w_q = sb("w_q", [128, 8, 512], BF16)
                w_kv = sb("w_kv", [128, 8, 1024], BF16)
                w_o = sb("w_o", [128, 4, D], BF16)
                km = sb("km", [128, 4, 256], BF16); b_km = P.buf()
                vm = sb("vm", [128, 2, 512], BF16); b_vm = P.buf()
                b_w = P.buf("weights")
                wes = ExitStack()
                with wes:
                    wsb = lambda n, sh, dt: wes.enter_context(nc.sbuf_tensor(U(n), list(sh), dt))
                    stage = Rot(P, wsb, 3, "stg", [128, 2048], F32)
                    for c in range(8):
                        load_weight(lambda a, b, c=c: w_out[:, c, a:b], w_out_d[l, c * 128:(c + 1) * 128, :], 128, D, stage, smallcol(l, O_OUTN + c))
                        load_weight(lambda a, b, c=c: w_q[:, c, a:b], mem_w_q_d[l, c * 128:(c + 1) * 128, :], 128, 512, stage, smallcol(l, O_MEMX + c))
                        load_weight(lambda a, b, c=c: w_kv[:, c, a:b], mem_w_kv_d[l, c * 128:(c + 1) * 128, :], 128, 1024, stage, smallcol(l, O_MEMKV + c))
                    for c in range(4):
                        load_weight(lambda a, b, c=c: w_o[:, c, a:b], mem_w_o_d[l, c * 128:(c + 1) * 128, :], 128, D, stage, None)
                    P.barrier()
                    keep_consts()
                    P.bufs.extend([b_km, b_vm, b_w])
                    msq = wsb("msq", [128, 8, 256], BF16); b_msq = P.buf()
                    memn = wsb("memn", [128, 8, 256], BF16); b_memn = P.buf()
                    mr = wsb("mr", [128, 256], F32); b_mr = P.buf()
                    pbk = Rot(P, lambda n, sh, dt: wes.enter_context(nc.psum_tensor(U(n), list(sh), dt)), 4, "pbm", [128, 512], F32)
                    ACT(msq[:], memT[:], AF.Square, R=[b_memT], W=[b_msq])
                    pb, bpb = pbk.next()
                    for c in range(8):
                        MM(pb[:, 0:256], ones_bf[:], msq[:, c, :], start=(c == 0), stop=(c == 7), R=[b_msq, b_ones], W=[bpb])
                    rsqrt_tile(mr[:], pb[:, 0:256], 1.0 / D, R=[bpb], W=[b_mr])
                    for c in range(8):
                        TT(P.dve, memn[:, c, :], memT[:, c, :], mr[:], ALU.mult, R=[b_memT, b_mr], W=[b_memn])
                    for hm in range(4):
                        pb, bpb = pbk.next()
                        for c in range(8):
                            MM(pb[:, 0:256], w_kv[:, c, hm * 128:(hm + 1) * 128], memn[:, c, :], start=(c == 0), stop=(c == 7), R=[b_memn], W=[bpb])
                        CP(P.dve, km[:, hm, :], pb[:, 0:256], R=[bpb], W=[b_km])
                    for kt_ in range(2):
                        pb, bpb = pbk.next()
                        for c in range(8):
                            MM(pb[:], memn[:, c, kt_ * 128:(kt_ + 1) * 128], w_kv[:, c, 512:1024], start=(c == 0), stop=(c == 7), R=[b_memn], W=[bpb])
                        CP(P.dve, vm[:, kt_, :], pb[:], R=[bpb], W=[b_vm])
                    P.barrier()
                keep_consts()
                P.bufs.extend([b_km, b_vm, b_w])

                xt_r = Rot(P, sb, 2, "xt", [128, 8, TS], F32)
                yt_r = Rot(P, sb, 2, "yt", [128, 8, TS], F32)
                sq_r = Rot(P, sb, 1, "sq", [128, 8, TS], BF16)
                yn_r = Rot(P, sb, 1, "yn", [128, 8, TS], BF16)
                h2_r = Rot(P, sb, 1, "h2", [128, 8, TS], BF16)
                qm_r = Rot(P, sb, 1, "qm", [128, 4, TS], BF16)
                om_r = Rot(P, sb, 1, "om", [128, 4, TS], BF16)
                pm_r = Rot(P, sb, 2, "pm", [128, 2 * TS], BF16)
                tf = Rot(P, sb, 6, "tf", [128, TS], F32)
                pbank = Rot(P, ps, 4, "pb", [128, 512], F32)
                s_r = Rot(P, ps, 2, "s", [128, 2 * TS], F32)
                mscale = 128 ** -0.5
                for i in range(NT):
                    t0 = i * TS
                    xt, bxt = xt_r.next()
                    LD(xt[:], xA[:, :, t0:t0 + TS].rearrange("c p t -> p c t"), W=[bxt])
                    yt, byt = yt_r.next()
                    LD(yt[:], YT[:, :, t0:t0 + TS].rearrange("c p t -> p c t"), W=[byt])
                    sq, bsq = sq_r.next()
                    ACT(sq[:], yt[:], AF.Square, R=[byt], W=[bsq])
                    yn, byn = yn_r.next()
                    for (c0, c1) in ((0, 3), (3, 6), (6, 8)):
                        pb, bpb = pbank.next()
                        for c in range(c0, c1):
                            MM(pb[:], ones_bf[:], sq[:, c, :], start=(c == c0), stop=(c == c1 - 1), R=[bsq, b_ones], W=[bpb])
                        rr, brr = tf.next()
                        rsqrt_tile(rr[:], pb[:], 1.0 / (128 * (c1 - c0)), R=[bpb], W=[brr])
                        for c in range(c0, c1):
                            TT(P.dve if c % 2 == 0 else P.pool, yn[:, c, :], yt[:, c, :], rr[:], ALU.mult, R=[byt, brr], W=[byn])
                    for nb in range(8):
                        pb, bpb = pbank.next()
                        for c in range(8):
                            MM(pb[:], w_out[:, c, nb * 128:(nb + 1) * 128], yn[:, c, :], start=(c == 0), stop=(c == 7), R=[byn, b_w], W=[bpb])
                        TT(P.dve, xt[:, nb, :], pb[:], xt[:, nb, :], ALU.add, R=[bpb], W=[bxt])
                    sq, bsq = sq_r.next()
                    ACT(sq[:], xt[:], AF.Square, R=[bxt], W=[bsq])
                    pb, bpb = pbank.next()
                    for c in range(8):
                        MM(pb[:], ones_bf[:], sq[:, c, :], start=(c == 0), stop=(c == 7), R=[bsq, b_ones], W=[bpb])
                    rr, brr = tf.next()
                    rsqrt_tile(rr[:], pb[:], 1.0 / D, R=[bpb], W=[brr])
                    h2, bh2 = h2_r.next()
                    for c in range(8):
                        TT(P.dve if c % 2 == 0 else P.pool, h2[:, c, :], xt[:, c, :], rr[:], ALU.mult, R=[bxt, brr], W=[bh2])
                    qm, bqm = qm_r.next()
                    for hm in range(4):
                        pb, bpb = pbank.next()
                        for c in range(8):
                            MM(pb[:], w_q[:, c, hm * 128:(hm + 1) * 128], h2[:, c, :], start=(c == 0), stop=(c == 7), R=[bh2, b_w], W=[bpb])
                        P.op(P.act, lambda e, o=qm[:, hm, :], i_=pb[:]: e.copy(out=o, in_=i_), reads=[bpb], writes=[bqm])
                    om, bom = om_r.next()
                    for hm in range(4):
                        st, bst = s_r.next()
                        for kt_ in range(2):
                            MM(st[:, kt_ * TS:(kt_ + 1) * TS], km[:, hm, kt_ * 128:(kt_ + 1) * 128], qm[:, hm, :], R=[b_km, bqm], W=[bst])
                        pm, bpm = pm_r.next()
                        ACT(pm[:], st[:], AF.Exp, R=[bst], W=[bpm], scale=mscale)
                        po, bpo = pbank.next()
                        for kt_ in range(2):
                            MM(po[:], vm[:, kt_, hm * 128:(hm + 1) * 128], pm[:, kt_ * TS:(kt_ + 1) * TS], start=(kt_ == 0), stop=(kt_ == 1), R=[b_vm, bpm], W=[bpo])
                        pd, bpd = pbank.next()
                        for kt_ in range(2):
                            MM(pd[:], ones_bf[:], pm[:, kt_ * TS:(kt_ + 1) * TS], start=(kt_ == 0), stop=(kt_ == 1), R=[b_ones, bpm], W=[bpd])
                        rd, brd = tf.next()
                        RCP(rd[:], pd[:], R=[bpd], W=[brd])
                        TT(P.dve, om[:, hm, :], po[:], rd[:], ALU.mult, R=[bpo, brd], W=[bom])
                    for nb in range(8):
                        pb, bpb = pbank.next()
                        for c in range(4):
                            MM(pb[:], w_o[:, c, nb * 128:(nb + 1) * 128], om[:, c, :], start=(c == 0), stop=(c == 3), R=[bom, b_w], W=[bpb])
                        TT(P.dve, xt[:, nb, :], pb[:], xt[:, nb, :], ALU.add, R=[bpb], W=[bxt])
                    ST(xB[:, :, t0:t0 + TS].rearrange("c p t -> p c t"), xt[:], R=[bxt])
                P.barrier()
            keep_consts()
            if stop_after == "c%d" % l:
                P.emit()
                return nc

            NF = 11
            TW = 510
            NTD = (S + TW - 1) // TW
            for half in range(2):
                pes, sb, ps = phase_scope()
                with pes:
                    w_up = sb("w_up", [128, 8, 2 * NF * 128], BF16)
                    w_dn = sb("w_dn", [128, NF, D], BF16)
                    b_w = P.buf("weights")
                    wes = ExitStack()
                    with wes:
                        stage = Rot(P, lambda n, sh, dt: wes.enter_context(nc.sbuf_tensor(U(n), list(sh), dt)), 3, "stg", [128, 2048], F32)
                        f0 = half * NF * 128
                        for c in range(8):
                            g = smallcol(l, O_FFN + c)
                            load_weight(lambda a, b, c=c: w_up[:, c, a:b], w_up_d[l, c * 128:(c + 1) * 128, f0:f0 + NF * 128], 128, NF * 128, stage, g)
                            load_weight(lambda a, b, c=c: w_up[:, c, NF * 128 + a:NF * 128 + b], w_up_d[l, c * 128:(c + 1) * 128, D_FF + f0:D_FF + f0 + NF * 128], 128, NF * 128, stage, g)
                        for f in range(NF):
                            load_weight(lambda a, b, f=f: w_dn[:, f, a:b], w_dn_d[l, f0 + f * 128:f0 + (f + 1) * 128, :], 128, D, stage, None)
                        P.barrier()
                    keep_consts()
                    P.bufs.append(b_w)
                    xt_r = Rot(P, sb, 2, "xt", [128, 8, TS], F32)
                    ac_r = Rot(P, sb, 2, "ac", [128, 8, TS], F32) if half == 1 else None
                    sq_r = Rot(P, sb, 1, "sq", [128, 8, TS], BF16)
                    h_r = Rot(P, sb, 1, "h", [128, 8, TS], BF16)
                    g_r = Rot(P, sb, 1, "g", [128, NF, TS], BF16)
                    tf = Rot(P, sb, 8, "tf", [128, TS], F32)
                    pbank = Rot(P, ps, 8, "pb", [128, 512], F32)
                    for i in range(NTD):
                        t0 = i * TW
                        a_lo = t0 - 1
                        n_out = min(TW, S - t0)
                        W_ = n_out + 2
                        lo_tok = max(a_lo, 0)
                        hi_tok = min(a_lo + W_, S)
                        c_lo = lo_tok - a_lo
                        c_hi = hi_tok - a_lo
                        xt, bxt = xt_r.next()
                        if c_lo > 0:
                            MSET(P.pool, xt[:, :, 0:c_lo], 0.0, W=[bxt])
                        if c_hi < W_:
                            MSET(P.pool, xt[:, :, c_hi:W_], 0.0, W=[bxt])
                        LD(xt[:, :, c_lo:c_hi], xB[:, :, lo_tok:hi_tok].rearrange("c p t -> p c t"), W=[bxt])
                        if half == 1:
                            ac, bac = ac_r.next()
                            LD(ac[:, :, 1:1 + n_out], xC[:, :, t0:t0 + n_out].rearrange("c p t -> p c t"), W=[bac])
                        else:
                            ac, bac = xt, bxt
                        sq, bsq = sq_r.next()
                        ACT(sq[:, :, 0:W_], xt[:, :, 0:W_], AF.Square, R=[bxt], W=[bsq])
                        pb, bpb = pbank.next()
                        for c in range(8):
                            MM(pb[:, 0:W_], ones_bf[:], sq[:, c, 0:W_], start=(c == 0), stop=(c == 7), R=[bsq, b_ones], W=[bpb])
                        rr, brr = tf.next()
                        rsqrt_tile(rr[:, 0:W_], pb[:, 0:W_], 1.0 / D, R=[bpb], W=[brr])
                        h, bh = h_r.next()
                        for c in range(8):
                            TT(P.dve if c % 2 == 0 else P.pool, h[:, c, 0:W_], xt[:, c, 0:W_], rr[:, 0:W_], ALU.mult, R=[bxt, brr], W=[bh])
                        g, bg = g_r.next()
                        for f in range(NF):
                            fg = half * NF + f
                            res = []
                            for part in range(2):
                                pb, bpb = pbank.next()
                                for c in range(8):
                                    MM(pb[:, 0:W_], w_up[:, c, (part * NF + f) * 128:(part * NF + f + 1) * 128], h[:, c, 0:W_],
                                       start=(c == 0), stop=(c == 7), R=[bh, b_w], W=[bpb])
                                blk = fg + part * 22
                                cw = O_CW + blk * 3
                                t_, bt_ = tf.next()
                                ACT(t_[:, 0:n_out], pb[:, 0:n_out], AF.Identity, R=[bpb, b_small], W=[bt_],
                                    scale=smallcol(l, cw), bias=smallcol(l, O_CB + blk))
                                STT(P.dve, t_[:, 0:n_out], pb[:, 1:1 + n_out], smallcol(l, cw + 1), t_[:, 0:n_out], ALU.mult, ALU.add, R=[bpb, b_small], W=[bt_])
                                STT(P.dve, t_[:, 0:n_out], pb[:, 2:2 + n_out], smallcol(l, cw + 2), t_[:, 0:n_out], ALU.mult, ALU.add, R=[bpb, b_small], W=[bt_])
                                res.append((t_, bt_))
                            ACT(res[0][0][:, 0:n_out], res[0][0][:, 0:n_out], AF.Silu, R=[res[0][1]], W=[res[0][1]])
                            TT(P.pool, g[:, f, 0:n_out], res[0][0][:, 0:n_out], res[1][0][:, 0:n_out], ALU.mult, R=[res[0][1], res[1][1]], W=[bg])
                        for nb in range(8):
                            pb, bpb = pbank.next()
                            for f in range(NF):
                                MM(pb[:, 0:n_out], w_dn[:, f, nb * 128:(nb + 1) * 128], g[:, f, 0:n_out], start=(f == 0), stop=(f == NF - 1), R=[bg, b_w], W=[bpb])
                            TT(P.dve, ac[:, nb, 1:1 + n_out], pb[:, 0:n_out], ac[:, nb, 1:1 + n_out], ALU.add, R=[bpb] + ([bxt] if half == 0 else []), W=[bac])
                        dstT = xC if half == 0 else xA
                        ST(dstT[:, :, t0:t0 + n_out].rearrange("c p t -> p c t"), ac[:, :, 1:1 + n_out], R=[bac])
                    P.barrier()
                keep_consts()
            if stop_after == "d%d" % l:
                P.emit()
                return nc

        pes, sb, ps = phase_scope()
        with pes:
            xt_r = Rot(P, sb, 2, "xt", [128, 8, TS], F32)
            sq_r = Rot(P, sb, 1, "sq", [128, 8, TS], BF16)
            xn_r = Rot(P, sb, 2, "xn", [128, 8, TS], F32)
            yo_r = Rot(P, sb, 2, "yo", [128, 4, D], F32)
            tf = Rot(P, sb, 2, "tf", [128, TS], F32)
            pbank = Rot(P, ps, 6, "pb", [128, 512], F32)
            for i in range(NT):
                t0 = i * TS
                xt, bxt = xt_r.next()
                LD(xt[:], xA[:, :, t0:t0 + TS].rearrange("c p t -> p c t"), W=[bxt])
                sq, bsq = sq_r.next()
                ACT(sq[:], xt[:], AF.Square, R=[bxt], W=[bsq])
                pb, bpb = pbank.next()
                for c in range(8):
                    MM(pb[:], ones_bf[:], sq[:, c, :], start=(c == 0), stop=(c == 7), R=[bsq, b_ones], W=[bpb])
                rr, brr = tf.next()
                rsqrt_tile(rr[:], pb[:], 1.0 / D, R=[bpb], W=[brr])
                xn, bxn = xn_r.next()
                for c in range(8):
                    STT(P.dve, xn[:, c, :], xt[:, c, :], fin[:, c:c + 1], rr[:], ALU.mult, ALU.mult, R=[bxt, brr, b_fin], W=[bxn])
                yo, byo = yo_r.next()
                for s_ in range(4):
                    for c4 in range(2):
                        pb, bpb = pbank.next()
                        for cc in range(4):
                            c = c4 * 4 + cc
                            TR(pb[:, cc * 128:(cc + 1) * 128], xn[:, c, s_ * 128:(s_ + 1) * 128], ident[:], R=[bxn, b_ident], W=[bpb])
                        if (s_ + c4) % 2 == 0:
                            CP(P.dve, yo[:, s_, c4 * 512:(c4 + 1) * 512], pb[:], R=[bpb], W=[byo])
                        else:
                            P.op(P.act, lambda e, o=yo[:, s_, c4 * 512:(c4 + 1) * 512], i_=pb[:]: e.copy(out=o, in_=i_), reads=[bpb], writes=[byo])
                ST(y_out[t0:t0 + TS, :].rearrange("(s p) d -> p s d", p=128), yo[:], R=[byo])
            P.barrier()
        P.emit()
    return nc


def _swap_pairs(w):
    idx = np.arange(w.shape[-1]).reshape(-1, 2)[:, ::-1].reshape(-1)
    return w[..., idx]


def _rope_tables():
    rows = S // 64
    row = np.repeat(np.arange(rows, dtype=np.float32), 64)
    col = np.tile(np.arange(64, dtype=np.float32), rows)

    def tab(d_rot):
        n = d_rot // 4
        inv = (np.float32(10000.0) ** (-np.arange(n, dtype=np.float32) / np.float32(n))).astype(np.float32)
        ang = np.concatenate([row[:, None] * inv, col[:, None] * inv], axis=-1).astype(np.float32)
        c = np.cos(ang).astype(np.float32)
        s = np.sin(ang).astype(np.float32)
        cf = np.repeat(c, 2, axis=1)
        sf = np.repeat(s, 2, axis=1)
        sign = np.tile(np.array([-1.0, 1.0], np.float32), d_rot // 2)
        return np.ascontiguousarray(cf.T), np.ascontiguousarray((sf * sign).T)

    ca, sa = tab(32)
    cb, sb_ = tab(64)
    ropeA = np.stack([ca, sa]).astype(np.float32)
    ropeB = np.stack([np.concatenate([cb, cb]), np.concatenate([sb_, sb_])]).astype(np.float32)
    return ropeA, ropeB


def _host_layout(inputs):
    f = lambda k: np.asarray(inputs[k], dtype=np.float32)
    L = DEPTH
    w_in = f("w_in")
    sw_src = np.concatenate([w_in[:, :, 320:416], w_in[:, :, 416:800], w_in[:, :, 800:928]], axis=-1)
    w_in_sw = _swap_pairs(sw_src)
    w_uq = f("mla_w_uq")
    w_uq_sw = _swap_pairs(w_uq)
    w_ukv = f("mla_w_ukv").reshape(L, 128, 6, 2, 64)
    w_ukv_p = np.concatenate([w_ukv[:, :, :, 0, :].reshape(L, 128, 384), w_ukv[:, :, :, 1, :].reshape(L, 128, 384)], axis=-1)
    wsT = np.ascontiguousarray(f("gmlp_w_s").transpose(0, 1, 3, 2))
    pc = lambda v: v.reshape(L, -1, 128).transpose(0, 2, 1)
    gq = f("gqa_q_norm"); gk = f("gqa_k_norm")
    sw64 = np.arange(64).reshape(-1, 2)[:, ::-1].reshape(-1)
    tile2 = lambda v: np.concatenate([v, v], axis=-1)[:, :, None]
    cw = f("ffn_conv_w")
    cwp = cw.reshape(L, 3, 44, 128).transpose(0, 3, 2, 1).reshape(L, 128, 132)
    cb = pc(f("ffn_conv_b"))
    small = np.concatenate([
        pc(f("mix_norm")), pc(f("out_norm")), pc(f("mem_x_norm")), pc(f("mem_kv_norm")), pc(f("ffn_norm")),
        pc(f("mla_q_norm")), pc(f("mla_kv_norm")),
        tile2(gq), tile2(gq[:, sw64]), tile2(gk), tile2(gk[:, sw64]),
        cwp, cb], axis=-1).astype(np.float32)
    ropeA, ropeB = _rope_tables()
    shared = {
        "ident": np.eye(128, dtype=np.float32),
        "ropeA": ropeA, "ropeB": ropeB,
        "w_in": w_in, "w_in_sw": np.ascontiguousarray(w_in_sw),
        "w_uq": w_uq, "w_uq_sw": np.ascontiguousarray(w_uq_sw),
        "w_ukv": np.ascontiguousarray(w_ukv_p),
        "wsT": wsT, "bs": f("gmlp_b_s"), "gvn": f("gmlp_v_norm"),
        "w_out": f("w_out"), "mem_w_q": f("mem_w_q"), "mem_w_kv": f("mem_w_kv"), "mem_w_o": f("mem_w_o"),
        "w_up": f("ffn_w_up"), "w_dn": f("ffn_w_down"),
        "small": np.ascontiguousarray(small),
        "fin": np.ascontiguousarray(f("final_norm").reshape(8, 128).T),
    }
    return shared


_NC_CACHE = {}


def kernel(**inputs):
    shared = _host_layout(inputs)
    x = np.asarray(inputs["x"], dtype=np.float32)
    mem = np.asarray(inputs["mem"], dtype=np.float32)
    if "nc" not in _NC_CACHE:
        _NC_CACHE["nc"] = build()
    nc = _NC_CACHE["nc"]
    in_maps = []
    for c in range(NCORES):
        m = dict(shared)
        m["x"] = np.ascontiguousarray(x[c])
        m["mem"] = np.ascontiguousarray(mem[c])
        in_maps.append(m)
    res = run_bass_kernel_spmd(nc, in_maps, core_ids=list(range(NCORES)))
    return np.stack([res.results[c]["y"] for c in range(NCORES)], axis=0).astype(np.float32)
```

```python
from contextlib import ExitStack
import numpy as np
import concourse.bass as bass
import concourse.mybir as mybir
from concourse.bass_utils import run_bass_kernel_spmd

F32 = mybir.dt.float32
BF16 = mybir.dt.bfloat16
AF = mybir.ActivationFunctionType
ALU = mybir.AluOpType

S = 8192
D = 1024
TS = 512
NT = S // TS
DEPTH = 2
D_IN = 1568
D_FF = 2816
EPS = 1e-6
NCORES = 8


class Buf:
    __slots__ = ("name", "w", "r", "sem", "sem2")

    def __init__(self, name):
        self.name = name
        self.w = []
        self.r = []
        self.sem = None
        self.sem2 = None


def _prune(ts):
    best = {}
    for s, v in ts:
        k = id(s)
        if k not in best or best[k][1] < v:
            best[k] = (s, v)
    return list(best.values())


class Eng:
    def __init__(self, prog, name):
        self.prog = prog
        self.name = name
        self.ops = []
        self.sem = prog.es.enter_context(prog.nc.semaphore("s_" + name))
        self.count = 0
        self.waited = {}
        self.needed = set()
        prog.sem2eng[id(self.sem)] = self

    def wait(self, tickets, skip_own=False):
        for s, v in _prune(tickets):
            if skip_own and s is self.sem:
                continue
            k = id(s)
            if self.waited.get(k, 0) < v:
                self.ops.append(("wait", s, v))
                self.waited[k] = v
                src = self.prog.sem2eng.get(k)
                if src is not None:
                    src.needed.add(v)

    def replay(self, e):
        ranks = {}
        for eng in self.prog.engs:
            ranks[id(eng.sem)] = {v: i + 1 for i, v in enumerate(sorted(eng.needed))}
        mine = ranks[id(self.sem)]
        for ent in self.ops:
            if ent[0] == "wait":
                _, s_, v = ent
                r = ranks.get(id(s_))
                e.wait_ge(s_, r[v] if r is not None else v)
            elif ent[0] == "op":
                ins = ent[1](e)
                if ent[2] in mine:
                    ins.then_inc(self.sem, 1)
            else:
                ent[1](e).then_inc(ent[2], 16)


class Prog:
    def __init__(self, nc, es, n_dma_sems=48):
        self.nc = nc
        self.es = es
        self.sem2eng = {}
        self.pe = Eng(self, "pe")
        self.act = Eng(self, "act")
        self.dve = Eng(self, "dve")
        self.pool = Eng(self, "pool")
        self.sp = Eng(self, "sp")
        self.engs = [self.pe, self.act, self.dve, self.pool, self.sp]
        self.dsems = [[es.enter_context(nc.semaphore("d%d" % i)), 0] for i in range(n_dma_sems)]
        self.free_dsems = list(range(n_dma_sems))
        self.bufs = []

    def buf(self, name="b"):
        b = Buf(name)
        self.bufs.append(b)
        return b

    def bufs_n(self, n, name="b"):
        return [self.buf(name + str(i)) for i in range(n)]

    def _deps(self, reads, writes):
        deps = []
        for b in reads:
            deps += b.w
        for b in writes:
            deps += b.w
            deps += b.r
        return deps

    def op(self, eng, fn, reads=(), writes=()):
        eng.wait(self._deps(reads, writes), skip_own=(eng is self.pe))
        eng.count += 1
        t = (eng.sem, eng.count)
        eng.ops.append(("op", fn, eng.count))
        for b in reads:
            b.r = _prune(b.r + [t])
        for b in writes:
            b.w = [t]
            b.r = []
        return t

    def dma(self, eng, fn, reads=(), writes=()):
        owner = (list(writes) + list(reads))[0]
        if eng is self.pool:
            if owner.sem2 is None:
                owner.sem2 = self.dsems[self.free_dsems.pop()]
            rec = owner.sem2
        else:
            if owner.sem is None:
                owner.sem = self.dsems[self.free_dsems.pop()]
            rec = owner.sem
        eng.wait(self._deps(reads, writes))
        rec[1] += 16
        t = (rec[0], rec[1])
        eng.ops.append(("dma", fn, rec[0]))
        for b in reads:
            b.r = _prune(b.r + [t])
        for b in writes:
            b.w = _prune([x for x in b.w if x[0] is rec[0]] + [t])
            b.r = []
        return t

    def barrier(self):
        ts = [(e.sem, e.count) for e in self.engs if e.count > 0]
        ts += [(r[0], r[1]) for r in self.dsems if r[1] > 0]
        for e in self.engs:
            e.wait(ts)
        for b in self.bufs:
            b.w = []
            b.r = []
            if b.sem is not None:
                self.free_dsems.append(self.dsems.index(b.sem))
                b.sem = None
            if b.sem2 is not None:
                self.free_dsems.append(self.dsems.index(b.sem2))
                b.sem2 = None
        self.bufs = []

    def emit(self):
        with self.nc.Block() as block:
            block.sync(self.sp.replay)
            block.tensor(self.pe.replay)
            block.scalar(self.act.replay)
            block.vector(self.dve.replay)
            block.gpsimd(self.pool.replay)


class Rot:
    def __init__(self, P, alloc, n, name, shape, dt):
        self.t = [alloc("%s%d" % (name, i), shape, dt) for i in range(n)]
        self.b = [P.buf("%s%d" % (name, i)) for i in range(n)]
        self.i = 0

    def next(self):
        k = self.i % len(self.t)
        self.i += 1
        return self.t[k], self.b[k]


def build(debug=False, stop_after=None, taps=()):
    nc = bass.Bass("TRN2", target_bir_lowering=False)
    dram_in = lambda n, sh, dt=F32: nc.dram_tensor(n, list(sh), dt, kind="ExternalInput").ap()
    dram_sc = lambda n, sh, dt=F32: nc.dram_tensor(n, list(sh), dt, kind=("ExternalOutput" if n in taps else "Internal")).ap()

    x_in = dram_in("x", [S, D])
    mem_in = dram_in("mem", [256, D])
    ident_in = dram_in("ident", [128, 128])
    ropeA_in = dram_in("ropeA", [2, 32, S])
    ropeB_in = dram_in("ropeB", [2, 128, S])
    w_in_d = dram_in("w_in", [DEPTH, D, D_IN])
    w_in_sw_d = dram_in("w_in_sw", [DEPTH, D, 608])
    w_uq_d = dram_in("w_uq", [DEPTH, 256, 576])
    w_uq_sw_d = dram_in("w_uq_sw", [DEPTH, 256, 576])
    w_ukv_d = dram_in("w_ukv", [DEPTH, 128, 768])
    wsT_d = dram_in("wsT", [DEPTH, 4, 128, 128])
    bs_d = dram_in("bs", [DEPTH, 4, 128])
    gvn_d = dram_in("gvn", [DEPTH, 256])
    w_out_d = dram_in("w_out", [DEPTH, D, D])
    mem_w_q_d = dram_in("mem_w_q", [DEPTH, D, 512])
    mem_w_kv_d = dram_in("mem_w_kv", [DEPTH, D, 1024])
    mem_w_o_d = dram_in("mem_w_o", [DEPTH, 512, D])
    w_up_d = dram_in("w_up", [DEPTH, D, 2 * D_FF])
    w_dn_d = dram_in("w_dn", [DEPTH, D_FF, D])
    NSM = 223
    small_d = dram_in("small", [DEPTH, 128, NSM])
    fin_d = dram_in("fin", [128, 8])
    y_out = nc.dram_tensor("y", [S, D], F32, kind="ExternalOutput").ap()

    xA = dram_sc("xA", [8, 128, S])
    xB = dram_sc("xB", [8, 128, S])
    xC = dram_sc("xC", [8, 128, S])
    QA = dram_sc("QA", [6, 96, S], BF16)
    KN = dram_sc("KN", [3, 128, S], BF16)
    KPE = dram_sc("KPE", [32, S], BF16)
    VA = dram_sc("VA", [128, 64, 390], BF16)
    QB = dram_sc("QB", [3, 128, S], BF16)
    KB = dram_sc("KB", [128, S], BF16)
    VB = dram_sc("VB", [128, 64, 130], BF16)
    YT = dram_sc("YT", [8, 128, S])
    DEN = dram_sc("DEN", [4, TS])

    O_MIX, O_OUTN, O_MEMX, O_MEMKV, O_FFN = 0, 8, 16, 24, 32
    O_QN, O_KVN = 40, 42
    O_GQ1, O_GQ2, O_GK1, O_GK2 = 43, 44, 45, 46
    O_CW, O_CB = 47, 47 + 132

    with ExitStack() as es:
        P = Prog(nc, es)
        uid = [0]

        def U(n):
            uid[0] += 1
            return "%s_u%d" % (n, uid[0])

        SB = lambda n, sh, dt: es.enter_context(nc.sbuf_tensor(U(n), list(sh), dt))

        def MM(out, lhsT, rhs, start=True, stop=True, R=(), W=()):
            return P.op(P.pe, lambda e: e.matmul(out, lhsT=lhsT, rhs=rhs, start=start, stop=stop), reads=R, writes=W)

        def TR(out, in_, ident, R=(), W=()):
            return P.op(P.pe, lambda e: e.transpose(out, in_, ident), reads=R, writes=W)

        def ACT(out, in_, func, R=(), W=(), **kw):
            return P.op(P.act, lambda e: e.activation(out=out, in_=in_, func=func, **kw), reads=R, writes=W)

        def TS_(eng, out, in0, s1, s2, op0, op1=None, R=(), W=()):
            if op1 is None:
                return P.op(eng, lambda e: e.tensor_scalar(out=out, in0=in0, scalar1=s1, scalar2=None, op0=op0), reads=R, writes=W)
            return P.op(eng, lambda e: e.tensor_scalar(out=out, in0=in0, scalar1=s1, scalar2=s2, op0=op0, op1=op1), reads=R, writes=W)

        def STT(eng, out, in0, scalar, in1, op0, op1, R=(), W=()):
            return P.op(eng, lambda e: e.scalar_tensor_tensor(out=out, in0=in0, scalar=scalar, in1=in1, op0=op0, op1=op1), reads=R, writes=W)

        def TT(eng, out, in0, in1, op, R=(), W=()):
            return P.op(eng, lambda e: e.tensor_tensor(out=out, in0=in0, in1=in1, op=op), reads=R, writes=W)

        def CP(eng, out, in_, R=(), W=()):
            return P.op(eng, lambda e: e.tensor_copy(out=out, in_=in_), reads=R, writes=W)

        def RCP(out, in_, R=(), W=()):
            return P.op(P.dve, lambda e: e.reciprocal(out=out, in_=in_), reads=R, writes=W)

        def MSET(eng, ap, val, W=()):
            return P.op(eng, lambda e: e.memset(ap, val), writes=W)

        def LD(out, in_, W, R=()):
            return P.dma(P.sp, lambda e: e.dma_start(out=out, in_=in_), reads=R, writes=W)

        def ST(out, in_, R):
            return P.dma(P.pool, lambda e: e.dma_start(out=out, in_=in_), reads=R)

        ident = SB("ident", [128, 128], F32); b_ident = P.buf()
        ones_bf = SB("ones_bf", [128, 128], BF16); b_ones = P.buf()
        blk_bf = SB("blk_bf", [128, 128], BF16); b_blk = P.buf()
        ones_f = SB("ones_f", [128, 64], F32); b_onesf = P.buf()
        small = SB("small", [128, DEPTH, NSM], F32); b_small = P.buf()
        fin = SB("fin", [128, 8], F32); b_fin = P.buf()
        LD(ident[:], ident_in[:, :], W=[b_ident])
        LD(small[:], small_d.rearrange("l p n -> p l n"), W=[b_small])
        LD(fin[:], fin_d[:, :], W=[b_fin])
        MSET(P.dve, ones_bf[:], 1.0, W=[b_ones])
        MSET(P.dve, blk_bf[:], 0.0, W=[b_blk])
        MSET(P.dve, blk_bf[0:64, 0:64], 1.0, W=[b_blk])
        MSET(P.dve, blk_bf[64:128, 64:128], 1.0, W=[b_blk])
        MSET(P.dve, ones_f[:], 1.0, W=[b_onesf])
        constbufs = [b_ident, b_ones, b_blk, b_onesf, b_small, b_fin]

        def keep_consts():
            P.bufs.extend(constbufs)

        def smallcol(l, off, n=1):
            return small[:, l, off:off + n]

        def phase_scope():
            pes = ExitStack()
            sb = lambda n, sh, dt: pes.enter_context(nc.sbuf_tensor(U(n), list(sh), dt))
            ps = lambda n, sh, dt: pes.enter_context(nc.psum_tensor(U(n), list(sh), dt))
            return pes, sb, ps

        def rsqrt_tile(dst, src_ps, inv_n, R, W):
            ACT(dst, src_ps, AF.Sqrt, R=R, W=W, scale=inv_n, bias=EPS)
            RCP(dst, dst, R=W, W=W)

        def load_weight(dst_bf, src_dram, rows, cols, stage, gain_ap=None, col_chunk=2048):
            for c0 in range(0, cols, col_chunk):
                c1 = min(cols, c0 + col_chunk)
                st, bst = stage.next()
                LD(st[0:rows, 0:c1 - c0], src_dram[:, c0:c1], W=[bst])
                if gain_ap is None:
                    CP(P.dve, dst_bf(c0, c1), st[0:rows, 0:c1 - c0], R=[bst], W=[])
                else:
                    TS_(P.dve, dst_bf(c0, c1), st[0:rows, 0:c1 - c0], gain_ap, None, ALU.mult, R=[bst, b_small], W=[])

        memT = SB("memT", [128, 8, 256], F32); b_memT = P.buf()
        constbufs.append(b_memT)
        pes, sb, ps = phase_scope()
        with pes:
            xin = Rot(P, sb, 2, "xin", [128, 4, D], F32)
            xo = Rot(P, sb, 2, "xo", [128, 8, TS], F32)
            pbank = Rot(P, ps, 4, "pb", [128, 512], F32)
            mt, bmt = xin.next()
            LD(mt[:, 0:2, :], mem_in.rearrange("(s p) d -> p s d", p=128), W=[bmt])
            for c in range(8):
                pb, bpb = pbank.next()
                for s_ in range(2):
                    TR(pb[:, s_ * 128:(s_ + 1) * 128], mt[:, s_, c * 128:(c + 1) * 128], ident[:], R=[bmt, b_ident], W=[bpb])
                CP(P.dve, memT[:, c, :], pb[:, 0:256], R=[bpb], W=[b_memT])
            for i in range(NT):
                xt, bxt = xin.next()
                LD(xt[:], x_in[i * TS:(i + 1) * TS, :].rearrange("(s p) d -> p s d", p=128), W=[bxt])
                xo_t, bxo = xo.next()
                for c in range(8):
                    pb, bpb = pbank.next()
                    for s_ in range(4):
                        TR(pb[:, s_ * 128:(s_ + 1) * 128], xt[:, s_, c * 128:(c + 1) * 128], ident[:], R=[bxt, b_ident], W=[bpb])
                    if c % 2 == 0:
                        CP(P.dve, xo_t[:, c, :], pb[:], R=[bpb], W=[bxo])
                    else:
                        P.op(P.act, lambda e, o=xo_t[:, c, :], i_=pb[:]: e.copy(out=o, in_=i_), reads=[bpb], writes=[bxo])
                ST(xA[:, :, i * TS:(i + 1) * TS].rearrange("c p t -> p c t"), xo_t[:], R=[bxo])
            P.barrier()
        keep_consts()
        if stop_after == "p0":
            P.emit()
            return nc

        for l in range(DEPTH):
            pes, sb, ps = phase_scope()
            with pes:
                w_in = sb("w_in", [128, 8, D_IN], BF16)
                w_sw = sb("w_sw", [128, 8, 608], BF16)
                w_uq = sb("w_uq", [128, 2, 576], BF16)
                w_uqs = sb("w_uqs", [128, 2, 576], BF16)
                w_ukv = sb("w_ukv", [128, 768], BF16)
                wsT = sb("wsT", [128, 4, 128], BF16)
                gvn = sb("gvn", [128, 256], F32); b_gvn = P.buf()
                biasT = sb("biasT", [128, 2, 128], F32); b_biasT = P.buf()
                b_w = P.buf("weights")
                wes = ExitStack()
                with wes:
                    stage = Rot(P, lambda n, sh, dt: wes.enter_context(nc.sbuf_tensor(U(n), list(sh), dt)), 3, "stg", [128, 2048], F32)
                    for c in range(8):
                        g = smallcol(l, O_MIX + c)
                        load_weight(lambda a, b, c=c: w_in[:, c, a:b], w_in_d[l, c * 128:(c + 1) * 128, :], 128, D_IN, stage, g)
                        load_weight(lambda a, b, c=c: w_sw[:, c, a:b], w_in_sw_d[l, c * 128:(c + 1) * 128, :], 128, 608, stage, g)
                    for c in range(2):
                        g = smallcol(l, O_QN + c)
                        load_weight(lambda a, b, c=c: w_uq[:, c, a:b], w_uq_d[l, c * 128:(c + 1) * 128, :], 128, 576, stage, g)
                        load_weight(lambda a, b, c=c: w_uqs[:, c, a:b], w_uq_sw_d[l, c * 128:(c + 1) * 128, :], 128, 576, stage, g)
                    load_weight(lambda a, b: w_ukv[:, a:b], w_ukv_d[l, :, :], 128, 768, stage, smallcol(l, O_KVN))
                    for g_ in range(4):
                        load_weight(lambda a, b, g_=g_: wsT[:, g_, a:b], wsT_d[l, g_, :, :], 128, 128, stage, None)
                    LD(gvn[:], gvn_d[l:l + 1, :].broadcast_to([128, 256]), W=[b_gvn])
                    for g_ in range(4):
                        k_, half = g_ // 2, g_ % 2
                        LD(biasT[half * 64:(half + 1) * 64, k_, :], bs_d[l, g_:g_ + 1, :].broadcast_to([64, 128]), W=[b_biasT])
                    P.barrier()
                keep_consts()
                P.bufs.extend([b_w, b_gvn, b_biasT])

                xt_r = Rot(P, sb, 2, "xt", [128, 8, TS], F32)
                sq_r = Rot(P, sb, 1, "sq", [128, 8, TS], BF16)
                h_r = Rot(P, sb, 2, "h", [128, 8, TS], BF16)
                tabA_r = Rot(P, sb, 2, "tabA", [96, 2, TS], F32)
                tabB_r = Rot(P, sb, 2, "tabB", [128, 2, TS], F32)
                tf = Rot(P, sb, 8, "tf", [128, TS], F32)
                tb = Rot(P, sb, 4, "tb", [128, TS], BF16)
                cqn_r = Rot(P, sb, 1, "cqn", [128, 2, TS], BF16)
                qa_r = Rot(P, sb, 2, "qa", [96, 6, TS], BF16)
                kn_r = Rot(P, sb, 2, "kn", [128, 3, TS], BF16)
                kpe_r = Rot(P, sb, 2, "kpe", [96, TS], BF16)
                va_r = Rot(P, sb, 2, "va", [128, 4, 390], BF16)
                qb_r = Rot(P, sb, 2, "qb", [128, 3, TS], BF16)
                kb_r = Rot(P, sb, 2, "kb", [128, TS], BF16)
                vb_r = Rot(P, sb, 2, "vb", [128, 4, 130], BF16)
                u_r = Rot(P, sb, 1, "u", [128, 2, TS], F32)
                yc_r = Rot(P, sb, 2, "yc", [128, 2, TS], F32)
                vvg_r = Rot(P, sb, 2, "vvg", [128, 256], F32)
                vvn_r = Rot(P, sb, 2, "vvn", [128, 256], BF16)
                sc_r = Rot(P, sb, 4, "sc", [128, 2], F32)
                pbank = Rot(P, ps, 8, "pb", [128, 512], F32)
                for k in range(2):
                    for s_ in range(4):
                        MSET(P.pool, va_r.t[k][:, s_, :], 1.0, W=[va_r.b[k]])
                        MSET(P.pool, vb_r.t[k][:, s_, :], 1.0, W=[vb_r.b[k]])

                for i in range(NT):
                    t0 = i * TS
                    xt, bxt = xt_r.next()
                    LD(xt[:], xA[:, :, t0:t0 + TS].rearrange("c p t -> p c t"), W=[bxt])
                    tabA, btA = tabA_r.next()
                    LD(tabA[64:96, :, :], ropeA_in[:, :, t0:t0 + TS].rearrange("k p t -> p k t"), W=[btA])
                    tabB, btB = tabB_r.next()
                    LD(tabB[:], ropeB_in[:, :, t0:t0 + TS].rearrange("k p t -> p k t"), W=[btB])
                    sq, bsq = sq_r.next()
                    ACT(sq[:], xt[:], AF.Square, R=[bxt], W=[bsq])
                    pb, bpb = pbank.next()
                    for c in range(8):
                        MM(pb[:], ones_bf[:], sq[:, c, :], start=(c == 0), stop=(c == 7), R=[bsq, b_ones], W=[bpb])
                    r0, br0 = tf.next()
                    rsqrt_tile(r0[:], pb[:], 1.0 / D, R=[bpb], W=[br0])
                    h, bh = h_r.next()
                    for c in range(8):
                        TT(P.dve if c % 2 == 0 else P.pool, h[:, c, :], xt[:, c, :], r0[:], ALU.mult, R=[bxt, br0], W=[bh])

                    def proj(wt, col0, ncol, R=(bh, b_w)):
                        pb_, bpb_ = pbank.next()
                        for c in range(8):
                            MM(pb_[0:ncol, :], wt[:, c, col0:col0 + ncol], h[:, c, :], start=(c == 0), stop=(c == 7), R=list(R), W=[bpb_])
                        return pb_, bpb_

                    cq = [proj(w_in, 0, 128), proj(w_in, 128, 128)]
                    sqc, bsqc = [], []
                    for k in range(2):
                        t_, b_ = tb.next()
                        ACT(t_[:], cq[k][0][:], AF.Square, R=[cq[k][1]], W=[b_])
                        sqc.append(t_); bsqc.append(b_)
                    pss, bpss = pbank.next()
                    for k in range(2):
                        MM(pss[:], ones_bf[:], sqc[k][:], start=(k == 0), stop=(k == 1), R=[bsqc[k], b_ones], W=[bpss])
                    rq, brq = tf.next()
                    rsqrt_tile(rq[:], pss[:], 1.0 / 256, R=[bpss], W=[brq])
                    cqn, bcqn = cqn_r.next()
                    for k in range(2):
                        TT(P.dve, cqn[:, k, :], cq[k][0][:], rq[:], ALU.mult, R=[cq[k][1], brq], W=[bcqn])
                    qa, bqa = qa_r.next()
                    for hh in range(6):
                        pq, bpq = pbank.next()
                        pqs, bpqs = pbank.next()
                        for k in range(2):
                            MM(pq[0:96, :], w_uq[:, k, hh * 96:(hh + 1) * 96], cqn[:, k, :], start=(k == 0), stop=(k == 1), R=[bcqn, b_w], W=[bpq])
                        for k in range(2):
                            MM(pqs[0:96, :], w_uqs[:, k, hh * 96:(hh + 1) * 96], cqn[:, k, :], start=(k == 0), stop=(k == 1), R=[bcqn, b_w], W=[bpqs])
                        P.op(P.act, lambda e, o=qa[0:64, hh, :], i_=pq[0:64, :]: e.copy(out=o, in_=i_), reads=[bpq], writes=[bqa])
                        t1, bt1 = tf.next()
                        t2, bt2 = tf.next()
                        TT(P.dve, t1[64:96, :], pq[64:96, :], tabA[64:96, 0, :], ALU.mult, R=[bpq, btA], W=[bt1])
                        TT(P.dve, t2[64:96, :], pqs[64:96, :], tabA[64:96, 1, :], ALU.mult, R=[bpqs, btA], W=[bt2])
                        TT(P.pool, qa[64:96, hh, :], t1[64:96, :], t2[64:96, :], ALU.add, R=[bt1, bt2], W=[bqa])
                    ST(QA[:, :, t0:t0 + TS].rearrange("h p t -> p h t"), qa[:], R=[bqa])

                    ckv = proj(w_in, 256, 128)
                    kr = proj(w_in, 320, 96)
                    krs = proj(w_sw, 0, 96)
                    t_, b_ = tb.next()
                    ACT(t_[:], ckv[0][:], AF.Square, R=[ckv[1]], W=[b_])
                    pss, bpss = pbank.next()
                    MM(pss[:], ones_bf[:], t_[:], R=[b_, b_ones], W=[bpss])
                    rk, brk = tf.next()
                    rsqrt_tile(rk[:], pss[:], 1.0 / 128, R=[bpss], W=[brk])
                    ckvn, bckvn = tb.next()
                    TT(P.dve, ckvn[:], ckv[0][:], rk[:], ALU.mult, R=[ckv[1], brk], W=[bckvn])
                    kn, bkn = kn_r.next()
                    for b3 in range(3):
                        pk, bpk = pbank.next()
                        MM(pk[:], w_ukv[:, b3 * 128:(b3 + 1) * 128], ckvn[:], R=[bckvn, b_w], W=[bpk])
                        if b3 % 2 == 0:
                            P.op(P.act, lambda e, o=kn[:, b3, :], i_=pk[:]: e.copy(out=o, in_=i_), reads=[bpk], writes=[bkn])
                        else:
                            CP(P.dve, kn[:, b3, :], pk[:], R=[bpk], W=[bkn])
                    ST(KN[:, :, t0:t0 + TS].rearrange("c p t -> p c t"), kn[:], R=[bkn])
                    kpe, bkpe = kpe_r.next()
                    t1, bt1 = tf.next()
                    t2, bt2 = tf.next()
                    TT(P.dve, t1[64:96, :], kr[0][64:96, :], tabA[64:96, 0, :], ALU.mult, R=[kr[1], btA], W=[bt1])
                    TT(P.dve, t2[64:96, :], krs[0][64:96, :], tabA[64:96, 1, :], ALU.mult, R=[krs[1], btA], W=[bt2])
                    TT(P.pool, kpe[64:96, :], t1[64:96, :], t2[64:96, :], ALU.add, R=[bt1, bt2], W=[bkpe])
                    ST(KPE[:, t0:t0 + TS], kpe[64:96, :], R=[bkpe])
                    va, bva = va_r.next()
                    for s_ in range(4):
                        pv, bpv = pbank.next()
                        MM(pv[:, 0:384], ckvn[:, s_ * 128:(s_ + 1) * 128], w_ukv[:, 384:768], R=[bckvn, b_w], W=[bpv])
                        dst = va[:, s_, :].rearrange("p (h c) -> p h c", c=65)[:, :, 0:64]
                        src = pv[:, 0:384].rearrange("p (h c) -> p h c", c=64)
                        if s_ % 2 == 0:
                            CP(P.dve, dst, src, R=[bpv], W=[bva])
                        else:
                            P.op(P.act, lambda e, o=dst, i_=src: e.copy(out=o, in_=i_), reads=[bpv], writes=[bva])
                    ST(VA[:, 4 * i:4 * i + 4, :], va[:], R=[bva])

                    qb, bqb = qb_r.next()
                    kb, bkb = kb_r.next()
                    for cc in range(4):
                        if cc < 3:
                            raw = proj(w_in, 416 + 128 * cc, 128)
                            swp = proj(w_sw, 96 + 128 * cc, 128)
                            og1, og2 = O_GQ1, O_GQ2
                            dst, bdst = qb[:, cc, :], bqb
                        else:
                            raw = proj(w_in, 800, 128)
                            swp = proj(w_sw, 96 + 384, 128)
                            og1, og2 = O_GK1, O_GK2
                            dst, bdst = kb[:], bkb
                        t_, b_ = tb.next()
                        ACT(t_[:], raw[0][:], AF.Square, R=[raw[1]], W=[b_])
                        pss, bpss = pbank.next()
                        MM(pss[:], blk_bf[:], t_[:], R=[b_, b_blk], W=[bpss])
                        rr, brr = tf.next()
                        rsqrt_tile(rr[:], pss[:], 1.0 / 64, R=[bpss], W=[brr])
                        t1, bt1 = tf.next()
                        t2, bt2 = tf.next()
                        STT(P.dve, t1[:], raw[0][:], smallcol(l, og1), tabB[:, 0, :], ALU.mult, ALU.mult, R=[raw[1], btB, b_small], W=[bt1])
                        STT(P.dve, t2[:], swp[0][:], smallcol(l, og2), tabB[:, 1, :], ALU.mult, ALU.mult, R=[swp[1], btB, b_small], W=[bt2])
                        TT(P.pool, t1[:], t1[:], t2[:], ALU.add, R=[bt2], W=[bt1])
                        TT(P.pool, dst, t1[:], rr[:], ALU.mult, R=[bt1, brr], W=[bdst])
                    ST(QB[:, :, t0:t0 + TS].rearrange("c p t -> p c t"), qb[:], R=[bqb])
                    ST(KB[:, t0:t0 + TS], kb[:], R=[bkb])
                    vb, bvb = vb_r.next()
                    for s_ in range(4):
                        pv, bpv = pbank.next()
                        for c in range(8):
                            MM(pv[:, 0:128], h[:, c, s_ * 128:(s_ + 1) * 128], w_in[:, c, 928:1056], start=(c == 0), stop=(c == 7), R=[bh, b_w], W=[bpv])
                        dst = vb[:, s_, :].rearrange("p (h c) -> p h c", c=65)[:, :, 0:64]
                        src = pv[:, 0:128].rearrange("p (h c) -> p h c", c=64)
                        P.op(P.act, lambda e, o=dst, i_=src: e.copy(out=o, in_=i_), reads=[bpv], writes=[bvb])
                    ST(VB[:, 4 * i:4 * i + 4, :], vb[:], R=[bvb])

                    u, bu = u_r.next()
                    for k in range(2):
                        pu = proj(w_in, 1056 + 128 * k, 128)
                        ACT(u[:, k, :], pu[0][:], AF.Gelu_apprx_tanh, R=[pu[1]], W=[bu])
                    yc, byc = yc_r.next()
                    for s_ in range(4):
                        pv, bpv = pbank.next()
                        for c in range(8):
                            MM(pv[:, 0:256], h[:, c, s_ * 128:(s_ + 1) * 128], w_in[:, c, 1312:1568], start=(c == 0), stop=(c == 7), R=[bh, b_w], W=[bpv])
                        vvg, bvvg = vvg_r.next()
                        ACT(vvg[:], pv[:, 0:256], AF.Gelu_apprx_tanh, R=[bpv], W=[bvvg])
                        sc, bsc = sc_r.next()
                        junk, bjunk = tf.next()
                        ACT(junk[:, 0:256], vvg[:], AF.Square, R=[bvvg], W=[bjunk, bsc], accum_out=sc[:, 0:1])
                        ACT(sc[:, 1:2], sc[:, 0:1], AF.Sqrt, R=[bsc], W=[bsc], scale=1.0 / 256, bias=EPS)
                        RCP(sc[:, 1:2], sc[:, 1:2], R=[bsc], W=[bsc])
                        vvn, bvvn = vvn_r.next()
                        STT(P.dve, vvn[:], vvg[:], sc[:, 1:2], gvn[:], ALU.mult, ALU.mult, R=[bvvg, bsc, b_gvn], W=[bvvn])
                        for k in range(2):
                            for half in range(2):
                                g_ = 2 * k + half
                                pm, bpm = pbank.next()
                                MM(pm[:, 0:128], vvn[:, k * 128:(k + 1) * 128], wsT[:, g_, :], R=[bvvn, b_w], W=[bpm])
                                lo, hi = half * 64, half * 64 + 64
                                tm, btm = tf.next()
                                TT(P.dve, tm[lo:hi, 0:128], pm[lo:hi, 0:128], biasT[lo:hi, k, :], ALU.add, R=[bpm, b_biasT], W=[btm])
                                TT(P.pool, yc[lo:hi, k, s_ * 128:(s_ + 1) * 128], tm[lo:hi, 0:128], u[lo:hi, k, s_ * 128:(s_ + 1) * 128], ALU.mult, R=[btm, bu], W=[byc])
                    ST(YT[6:8, :, t0:t0 + TS].rearrange("c p t -> p c t"), yc[:], R=[byc])
                P.barrier()
            keep_consts()
            if stop_after == "a%d" % l:
                P.emit()
                return nc

            def attention(group):
                pes, sb, ps = phase_scope()
                with pes:
                    if group == "a":
                        dk, nh, vw = 96, 6, 390
                        Vd = VA
                        scale = 96 ** -0.5
                    else:
                        dk, nh, vw = 64, 6, 130
                        Vd = VB
                        scale = 64 ** -0.5
                    v_sb = sb("v_sb", [128, 64, vw], BF16); b_v = P.buf()
                    LD(v_sb[:, 0:32, :], Vd[:, 0:32, :], W=[b_v])
                    LD(v_sb[:, 32:64, :], Vd[:, 32:64, :], W=[b_v])
                    dkp = 96 if group == "a" else 128
                    kt_r = Rot(P, sb, 2, "kt", [dkp, S], BF16)
                    q_r = Rot(P, sb, 3, "q", [dkp, TS], BF16)
                    if group == "b":
                        for k_ in range(2):
                            MSET(P.pool, kt_r.t[k_][64:128, :], 0.0, W=[kt_r.b[k_]])
                        for k_ in range(3):
                            MSET(P.pool, q_r.t[k_][64:128, :], 0.0, W=[q_r.b[k_]])
                    p_r = Rot(P, sb, 3, "p", [128, 2 * TS], BF16)
                    rd_r = Rot(P, sb, 2, "rd", [65, TS], F32)
                    bc_r = Rot(P, sb, 2, "bc", [64, TS], F32)
                    o_r = Rot(P, sb, 2, "o", [64, TS], F32)
                    s_r = Rot(P, ps, 3, "s", [128, 2 * TS], F32)
                    ops_r = Rot(P, ps, 2, "o_ps", [128, TS], F32)

                    def load_k(hh):
                        kt, bkt = kt_r.next()
                        if group == "a":
                            src = KN[hh // 2, (hh % 2) * 64:(hh % 2) * 64 + 64, :]
                            for q4 in range(4):
                                LD(kt[0:64, q4 * 2048:(q4 + 1) * 2048], src[:, q4 * 2048:(q4 + 1) * 2048], W=[bkt])
                                LD(kt[64:96, q4 * 2048:(q4 + 1) * 2048], KPE[:, q4 * 2048:(q4 + 1) * 2048], W=[bkt])
                        else:
                            src = KB[hh * 64:hh * 64 + 64, :]
                            for q4 in range(4):
                                LD(kt[0:64, q4 * 2048:(q4 + 1) * 2048], src[:, q4 * 2048:(q4 + 1) * 2048], W=[bkt])
                        return kt, bkt

                    def load_q(hh, qt):
                        q, bq = q_r.next()
                        if group == "a":
                            LD(q[:], QA[hh, :, qt * TS:(qt + 1) * TS], W=[bq])
                        else:
                            LD(q[0:64, :], QB[hh // 2, (hh % 2) * 64:(hh % 2) * 64 + 64, qt * TS:(qt + 1) * TS], W=[bq])
                        return q, bq

                    b_den = P.bufs_n(4, "den")
                    den_i = [0]
                    kcur = load_k(0)
                    for hh in range(nh):
                        kvh = hh if group == "a" else hh // 3
                        if group == "a":
                            knext = load_k(hh + 1) if hh + 1 < nh else None
                        else:
                            knext = load_k(1) if hh == 2 else None
                        kt, bkt = kcur
                        if hh == 0:
                            qn = load_q(0, 0)
                        for qt in range(NT):
                            q, bq = qn
                            if qt + 1 < NT:
                                qn = load_q(hh, qt + 1)
                            elif hh + 1 < nh:
                                qn = load_q(hh + 1, 0)
                            NJ = 32
                            sb_list = {}
                            o_ps, b_ops = ops_r.next()

                            def do_s(jj):
                                st, bst = s_r.next()
                                for k2 in range(2):
                                    j = 2 * jj + k2
                                    MM(st[:, k2 * TS:(k2 + 1) * TS], kt[:, j * 128:(j + 1) * 128], q[:], R=[bkt, bq], W=[bst])
                                sb_list[jj] = (st, bst)

                            def do_exp(jj):
                                st, bst = sb_list[jj]
                                p, bp = p_r.next()
                                ACT(p[:], st[:], AF.Exp, R=[bst], W=[bp], scale=scale)
                                return p, bp

                            def do_pv(jj, p, bp):
                                for k2 in range(2):
                                    j = 2 * jj + k2
                                    MM(o_ps[0:65, :], v_sb[:, j, kvh * 65:(kvh + 1) * 65], p[:, k2 * TS:(k2 + 1) * TS],
                                       start=(j == 0), stop=(j == 63), R=[b_v, bp], W=[b_ops])

                            do_s(0)
                            do_s(1)
                            for jj in range(NJ):
                                p, bp = do_exp(jj)
                                if jj + 2 < NJ:
                                    do_s(jj + 2)
                                do_pv(jj, p, bp)
                            rd, brd = rd_r.next()
                            RCP(rd[64:65, :], o_ps[64:65, :], R=[b_ops], W=[brd])
                            bc, bbc = bc_r.next()
                            dslot = den_i[0] % 4
                            den_i[0] += 1
                            P.dma(P.sp, lambda e, o=DEN[dslot:dslot + 1, :], i_=rd[64:65, :]: e.dma_start(out=o, in_=i_), reads=[brd], writes=[b_den[dslot]])
                            LD(bc[:], DEN[dslot:dslot + 1, :].broadcast_to([64, TS]), W=[bbc], R=[b_den[dslot]])
                            o, bo = o_r.next()
                            TT(P.dve, o[:], o_ps[0:64, :], bc[:], ALU.mult, R=[b_ops, bbc], W=[bo])
                            chunk = (0 if group == "a" else 3) + hh // 2
                            ST(YT[chunk, (hh % 2) * 64:(hh % 2) * 64 + 64, qt * TS:(qt + 1) * TS], o[:], R=[bo])
                        if knext is not None:
                            kcur = knext
                    P.barrier()
                keep_consts()

            attention("a")
            if stop_after == "ba%d" % l:
                P.emit()
                return nc
            attention("b")
            if stop_after == "b%d" % l:
                P.emit()
                return nc

            pes, sb, ps = phase_scope()
            with pes:
                w_out = sb("w_out", [128, 8, D], BF16)
                w_q = sb("w_q", [128, 8, 512], BF16)
                w_kv = sb("w_kv", [128, 8, 1024], BF16)
                w_o = sb("w_o", [128, 4, D], BF16)
                km = sb("km", [128, 4, 256], BF16); b_km = P.buf()
                vm = sb("vm", [128, 2, 512], BF16); b_vm = P.buf()
                b_w = P.buf("weights")
                wes = ExitStack()
                with wes:
                    wsb = lambda n, sh, dt: wes.enter_context(nc.sbuf_tensor(U(n), list(sh), dt))
                    stage = Rot(P, wsb, 3, "stg", [128, 2048], F32)
                    for c in range(8):
                        load_weight(lambda a, b, c=c: w_out[:, c, a:b], w_out_d[l, c * 128:(c + 1) * 128, :], 128, D, stage, smallcol(l, O_OUTN + c))
                        load_weight(lambda a, b, c=c: w_q[:, c, a:b], mem_w_q_d[l, c * 128:(c + 1) * 128, :], 128, 512, stage, smallcol(l, O_MEMX + c))
                        load_weight(lambda a, b, c=c: w_kv[:, c, a:b], mem_w_kv_d[l, c * 128:(c + 1) * 128, :], 128, 1024, stage, smallcol(l, O_MEMKV + c))
                    for c in range(4):
                        load_weight(lambda a, b, c=c: w_o[:, c, a:b], mem_w_o_d[l, c * 128:(c + 1) * 128, :], 128, D, stage, None)
                    P.barrier()
                    keep_consts()
                    P.bufs.extend([b_km, b_vm, b_w])
                    msq = wsb("msq", [128, 8, 256], BF16); b_msq = P.buf()
                    memn = wsb("memn", [128, 8, 256], BF16); b_memn = P.buf()
                    mr = wsb("mr", [128, 256], F32); b_mr = P.buf()
                    pbk = Rot(P, lambda n, sh, dt: wes.enter_context(nc.psum_tensor(U(n), list(sh), dt)), 4, "pbm", [128, 512], F32)
                    ACT(msq[:], memT[:], AF.Square, R=[b_memT], W=[b_msq])
                    pb, bpb = pbk.next()
                    for c in range(8):
                        MM(pb[:, 0:256], ones_bf[:], msq[:, c, :], start=(c == 0), stop=(c == 7), R=[b_msq, b_ones], W=[bpb])
                    rsqrt_tile(mr[:], pb[:, 0:256], 1.0 / D, R=[bpb], W=[b_mr])
                    for c in range(8):
                        TT(P.dve, memn[:, c, :], memT[:, c, :], mr[:], ALU.mult, R=[b_memT, b_mr], W=[b_memn])
                    for hm in range(4):
                        pb, bpb = pbk.next()
                        for c in range(8):
                            MM(pb[:, 0:256], w_kv[:, c, hm * 128:(hm + 1) * 128], memn[:, c, :], start=(c == 0), stop=(c == 7), R=[b_memn], W=[bpb])
                        CP(P.dve, km[:, hm, :], pb[:, 0:256], R=[bpb], W=[b_km])
                    for kt_ in range(2):
                        pb, bpb = pbk.next()
                        for c in range(8):
                            MM(pb[:], memn[:, c, kt_ * 128:(kt_ + 1) * 128], w_kv[:, c, 512:1024], start=(c == 0), stop=(c == 7), R=[b_memn], W=[bpb])
                        CP(P.dve, vm[:, kt_, :], pb[:], R=[bpb], W=[b_vm])
                    P.barrier()
                keep_consts()
                P.bufs.extend([b_km, b_vm, b_w])

                xt_r = Rot(P, sb, 2, "xt", [128, 8, TS], F32)
                yt_r = Rot(P, sb, 2, "yt", [128, 8, TS], F32)
                sqy_r = Rot(P, sb, 1, "sqy", [128, 8, TS], BF16)
                sqx_r = Rot(P, sb, 1, "sqx", [128, 8, TS], BF16)
                yn_r = Rot(P, sb, 2, "yn", [128, 8, TS], BF16)
                h2_r = Rot(P, sb, 1, "h2", [128, 8, TS], BF16)
                qm_r = Rot(P, sb, 1, "qm", [128, 4, TS], BF16)
                om_r = Rot(P, sb, 1, "om", [128, 4, TS], BF16)
                pm_r = Rot(P, sb, 2, "pm", [128, 2 * TS], BF16)
                tf = Rot(P, sb, 6, "tf", [128, TS], F32)
                pbank = Rot(P, ps, 3, "pb", [128, 512], F32)
                ssm_ps = ps("ssm", [128, 512], F32); b_ssm = P.buf()
                s_r = Rot(P, ps, 2, "s", [128, 2 * TS], F32)
                mscale = 128 ** -0.5
                xt_cb = [P.bufs_n(8, "xtc") for _ in range(2)]
                yn_cb = [P.bufs_n(8, "ync") for _ in range(2)]
                sqx_cb = P.bufs_n(8, "sqxc")
                h2_cb = P.bufs_n(8, "h2c")
                qm_cb = P.bufs_n(4, "qmc")
                om_cb = P.bufs_n(4, "omc")

                def c_stage0(i):
                    st = {"t0": i * TS, "k": i % 2}
                    k = i % 2
                    st["yt"], st["byt"] = yt_r.next()
                    LD(st["yt"][:], YT[:, :, i * TS:(i + 1) * TS].rearrange("c p t -> p c t"), W=[st["byt"]])
                    st["xt"], st["bx"] = xt_r.t[k], xt_cb[k]
                    LD(st["xt"][:], xA[:, :, i * TS:(i + 1) * TS].rearrange("c p t -> p c t"), W=st["bx"])
                    st["yn"], st["byn"] = yn_r.t[k], yn_cb[k]
                    return st

                def c_stage_y1(st):
                    sq, bsq = sqy_r.next()
                    ACT(sq[:], st["yt"][:], AF.Square, R=[st["byt"]], W=[bsq])
                    st["sqy"], st["bsqy"] = sq, bsq

                def c_stage_y2(st):
                    sq, bsq, yt, byt, yn, byn = st["sqy"], st["bsqy"], st["yt"], st["byt"], st["yn"], st["byn"]
                    for (c0, c1) in ((0, 3), (3, 6), (6, 8)):
                        pb, bpb = pbank.next()
                        for c in range(c0, c1):
                            MM(pb[:], ones_bf[:], sq[:, c, :], start=(c == c0), stop=(c == c1 - 1), R=[bsq, b_ones], W=[bpb])
                        rr, brr = tf.next()
                        rsqrt_tile(rr[:], pb[:], 1.0 / (128 * (c1 - c0)), R=[bpb], W=[brr])
                        for c in range(c0, c1):
                            TT(P.dve if c % 2 == 0 else P.pool, yn[:, c, :], yt[:, c, :], rr[:], ALU.mult, R=[byt, brr], W=[byn[c]])

                def c_body(st, nxt):
                    xt, bx, yn, byn, t0 = st["xt"], st["bx"], st["yn"], st["byn"], st["t0"]
                    sq = sqx_r.t[0]
                    for nb in range(8):
                        pb, bpb = pbank.next()
                        for c in range(8):
                            MM(pb[:], w_out[:, c, nb * 128:(nb + 1) * 128], yn[:, c, :], start=(c == 0), stop=(c == 7), R=[byn[c], b_w], W=[bpb])
                        TT(P.dve, xt[:, nb, :], pb[:], xt[:, nb, :], ALU.add, R=[bpb], W=[bx[nb]])
                        ACT(sq[:, nb, :], xt[:, nb, :], AF.Square, R=[bx[nb]], W=[sqx_cb[nb]])
                        MM(ssm_ps[:], ones_bf[:], sq[:, nb, :], start=(nb == 0), stop=(nb == 7), R=[sqx_cb[nb], b_ones], W=[b_ssm])
                    rr, brr = tf.next()
                    rsqrt_tile(rr[:], ssm_ps[:], 1.0 / D, R=[b_ssm], W=[brr])
                    h2 = h2_r.t[0]
                    for c in range(8):
                        TT(P.dve if c % 2 == 0 else P.pool, h2[:, c, :], xt[:, c, :], rr[:], ALU.mult, R=[bx[c], brr], W=[h2_cb[c]])
                    qm = qm_r.t[0]
                    for hm in range(4):
                        pb, bpb = pbank.next()
                        for c in range(8):
                            MM(pb[:], w_q[:, c, hm * 128:(hm + 1) * 128], h2[:, c, :], start=(c == 0), stop=(c == 7), R=[h2_cb[c], b_w], W=[bpb])
                        P.op(P.act, lambda e, o=qm[:, hm, :], i_=pb[:]: e.copy(out=o, in_=i_), reads=[bpb], writes=[qm_cb[hm]])
                    if nxt is not None:
                        c_stage_y1(nxt)
                    om = om_r.t[0]
                    sts = {}

                    def c_s(hm):
                        st_, bst_ = s_r.next()
                        for kt_ in range(2):
                            MM(st_[:, kt_ * TS:(kt_ + 1) * TS], km[:, hm, kt_ * 128:(kt_ + 1) * 128], qm[:, hm, :], R=[b_km, qm_cb[hm]], W=[bst_])
                        sts[hm] = (st_, bst_)

                    c_s(0)
                    c_s(1)
                    for hm in range(4):
                        st_, bst_ = sts[hm]
                        pm, bpm = pm_r.next()
                        ACT(pm[:], st_[:], AF.Exp, R=[bst_], W=[bpm], scale=mscale)
                        if hm + 2 < 4:
                            c_s(hm + 2)
                        po, bpo = pbank.next()
                        for kt_ in range(2):
                            MM(po[:], vm[:, kt_, hm * 128:(hm + 1) * 128], pm[:, kt_ * TS:(kt_ + 1) * TS], start=(kt_ == 0), stop=(kt_ == 1), R=[b_vm, bpm], W=[bpo])
                        pd, bpd = pbank.next()
                        for kt_ in range(2):
                            MM(pd[:], ones_bf[:], pm[:, kt_ * TS:(kt_ + 1) * TS], start=(kt_ == 0), stop=(kt_ == 1), R=[b_ones, bpm], W=[bpd])
                        rd, brd = tf.next()
                        RCP(rd[:], pd[:], R=[bpd], W=[brd])
                        TT(P.dve, om[:, hm, :], po[:], rd[:], ALU.mult, R=[bpo, brd], W=[om_cb[hm]])
                    if nxt is not None:
                        c_stage_y2(nxt)
                    for nb in range(8):
                        pb, bpb = pbank.next()
                        for c in range(4):
                            MM(pb[:], w_o[:, c, nb * 128:(nb + 1) * 128], om[:, c, :], start=(c == 0), stop=(c == 3), R=[om_cb[c], b_w], W=[bpb])
                        TT(P.dve, xt[:, nb, :], pb[:], xt[:, nb, :], ALU.add, R=[bpb], W=[bx[nb]])
                    ST(xB[:, :, t0:t0 + TS].rearrange("c p t -> p c t"), xt[:], R=bx)

                cur = c_stage0(0)
                c_stage_y1(cur)
                c_stage_y2(cur)
                for i in range(NT):
                    nxt = c_stage0(i + 1) if i + 1 < NT else None
                    c_body(cur, nxt)
                    cur = nxt
                P.barrier()
            keep_consts()
            if stop_after == "c%d" % l:
                P.emit()
                return nc

            NF = 11
            TW = 510
            NTD = (S + TW - 1) // TW
            for half in range(2):
                pes, sb, ps = phase_scope()
                with pes:
                    w_up = sb("w_up", [128, 8, 2 * NF * 128], BF16)
                    w_dn = sb("w_dn", [128, NF, D], BF16)
                    b_w = P.buf("weights")
                    wes = ExitStack()
                    with wes:
                        stage = Rot(P, lambda n, sh, dt: wes.enter_context(nc.sbuf_tensor(U(n), list(sh), dt)), 3, "stg", [128, 2048], F32)
                        f0 = half * NF * 128
                        for c in range(8):
                            g = smallcol(l, O_FFN + c)
                            load_weight(lambda a, b, c=c: w_up[:, c, a:b], w_up_d[l, c * 128:(c + 1) * 128, f0:f0 + NF * 128], 128, NF * 128, stage, g)
                            load_weight(lambda a, b, c=c: w_up[:, c, NF * 128 + a:NF * 128 + b], w_up_d[l, c * 128:(c + 1) * 128, D_FF + f0:D_FF + f0 + NF * 128], 128, NF * 128, stage, g)
                        for f in range(NF):
                            load_weight(lambda a, b, f=f: w_dn[:, f, a:b], w_dn_d[l, f0 + f * 128:f0 + (f + 1) * 128, :], 128, D, stage, None)
                        P.barrier()
                    keep_consts()
                    P.bufs.append(b_w)
                    xt_r = Rot(P, sb, 2, "xt", [128, 8, TS], F32)
                    ac_r = Rot(P, sb, 2, "ac", [128, 8, TS], F32)
                    sq_r = Rot(P, sb, 1, "sq", [128, 8, TS], BF16)
                    h_r = Rot(P, sb, 2, "h", [128, 8, TS], BF16)
                    g_r = Rot(P, sb, 1, "g", [128, NF, TS], BF16)
                    tf = Rot(P, sb, 8, "tf", [128, TS], F32)
                    rr_r = Rot(P, sb, 2, "rr", [128, TS], F32)
                    pbank = Rot(P, ps, 8, "pb", [128, 512], F32)
                    xt_cb = [P.bufs_n(8, "xtc") for _ in range(2)]
                    ac_cb = [P.bufs_n(8, "acc") for _ in range(2)]
                    h_cb = [P.bufs_n(8, "hc") for _ in range(2)]
                    g_cb = P.bufs_n(NF, "gc")
                    g = g_r.t[0]

                    def stage0(i):
                        st = {}
                        t0 = i * TW
                        a_lo = t0 - 1
                        n_out = min(TW, S - t0)
                        W_ = n_out + 2
                        lo_tok = max(a_lo, 0)
                        hi_tok = min(a_lo + W_, S)
                        c_lo = lo_tok - a_lo
                        c_hi = hi_tok - a_lo
                        k = i % 2
                        xt, bx = xt_r.t[k], xt_cb[k]
                        if c_lo > 0:
                            MSET(P.pool, xt[:, :, 0:c_lo], 0.0, W=bx)
                        if c_hi < W_:
                            MSET(P.pool, xt[:, :, c_hi:W_], 0.0, W=bx)
                        LD(xt[:, :, c_lo:c_hi], xB[:, :, lo_tok:hi_tok].rearrange("c p t -> p c t"), W=bx)
                        ac, bac = ac_r.t[k], ac_cb[k]
                        if half == 1:
                            LD(ac[:, :, 1:1 + n_out], xC[:, :, t0:t0 + n_out].rearrange("c p t -> p c t"), W=bac)
                        st.update(t0=t0, n_out=n_out, W_=W_, xt=xt, bx=bx, ac=ac, bac=bac, h=h_r.t[k], bh=h_cb[k])
                        return st

                    def stage1(st):
                        W_ = st["W_"]
                        sq, bsq = sq_r.next()
                        ACT(sq[:, :, 0:W_], st["xt"][:, :, 0:W_], AF.Square, R=st["bx"], W=[bsq])
                        st["sq"], st["bsq"] = sq, bsq

                    def stage2(st):
                        W_ = st["W_"]
                        sq, bsq = st["sq"], st["bsq"]
                        pb, bpb = pbank.next()
                        for c in range(8):
                            MM(pb[:, 0:W_], ones_bf[:], sq[:, c, 0:W_], start=(c == 0), stop=(c == 7), R=[bsq, b_ones], W=[bpb])
                        rr, brr = rr_r.next()
                        rsqrt_tile(rr[:, 0:W_], pb[:, 0:W_], 1.0 / D, R=[bpb], W=[brr])
                        for c in range(8):
                            TT(P.dve if c % 2 == 0 else P.pool, st["h"][:, c, 0:W_], st["xt"][:, c, 0:W_], rr[:, 0:W_], ALU.mult,
                               R=[st["bx"][c], brr], W=[st["bh"][c]])

                    def body(st, nxt):
                        n_out, W_, h, bh, ac, bac, t0 = st["n_out"], st["W_"], st["h"], st["bh"], st["ac"], st["bac"], st["t0"]
                        for f in range(NF):
                            if nxt is not None and f == 0:
                                stage1(nxt)
                            if nxt is not None and f == 3:
                                stage2(nxt)
                            fg = half * NF + f
                            res = []
                            for part in range(2):
                                pb, bpb = pbank.next()
                                for c in range(8):
                                    MM(pb[:, 0:W_], w_up[:, c, (part * NF + f) * 128:(part * NF + f + 1) * 128], h[:, c, 0:W_],
                                       start=(c == 0), stop=(c == 7), R=[bh[c], b_w], W=[bpb])
                                blk = fg + part * 22
                                cw = O_CW + blk * 3
                                t_, bt_ = tf.next()
                                ACT(t_[:, 0:n_out], pb[:, 0:n_out], AF.Identity, R=[bpb, b_small], W=[bt_],
                                    scale=smallcol(l, cw), bias=smallcol(l, O_CB + blk))
                                STT(P.dve, t_[:, 0:n_out], pb[:, 1:1 + n_out], smallcol(l, cw + 1), t_[:, 0:n_out], ALU.mult, ALU.add, R=[bpb, b_small], W=[bt_])
                                STT(P.dve, t_[:, 0:n_out], pb[:, 2:2 + n_out], smallcol(l, cw + 2), t_[:, 0:n_out], ALU.mult, ALU.add, R=[bpb, b_small], W=[bt_])
                                res.append((t_, bt_))
                            ACT(res[0][0][:, 0:n_out], res[0][0][:, 0:n_out], AF.Silu, R=[res[0][1]], W=[res[0][1]])
                            TT(P.pool, g[:, f, 0:n_out], res[0][0][:, 0:n_out], res[1][0][:, 0:n_out], ALU.mult, R=[res[0][1], res[1][1]], W=[g_cb[f]])
                        for nb in range(8):
                            pb, bpb = pbank.next()
                            for f in range(NF):
                                MM(pb[:, 0:n_out], w_dn[:, f, nb * 128:(nb + 1) * 128], g[:, f, 0:n_out], start=(f == 0), stop=(f == NF - 1), R=[g_cb[f], b_w], W=[bpb])
                            if half == 0:
                                TT(P.dve, ac[:, nb, 1:1 + n_out], pb[:, 0:n_out], st["xt"][:, nb, 1:1 + n_out], ALU.add, R=[bpb, st["bx"][nb]], W=[bac[nb]])
                            else:
                                TT(P.dve, ac[:, nb, 1:1 + n_out], pb[:, 0:n_out], ac[:, nb, 1:1 + n_out], ALU.add, R=[bpb], W=[bac[nb]])
                        dstT = xC if half == 0 else xA
                        ST(dstT[:, :, t0:t0 + n_out].rearrange("c p t -> p c t"), ac[:, :, 1:1 + n_out], R=bac)

                    cur = stage0(0)
                    stage1(cur)
                    stage2(cur)
                    for i in range(NTD):
                        nxt = stage0(i + 1) if i + 1 < NTD else None
                        body(cur, nxt)
                        cur = nxt
                    P.barrier()
                keep_consts()
            if stop_after == "d%d" % l:
                P.emit()
                return nc

        pes, sb, ps = phase_scope()
        with pes:
            xt_r = Rot(P, sb, 2, "xt", [128, 8, TS], F32)
            sq_r = Rot(P, sb, 1, "sq", [128, 8, TS], BF16)
            xn_r = Rot(P, sb, 2, "xn", [128, 8, TS], F32)
            yo_r = Rot(P, sb, 2, "yo", [128, 4, D], F32)
            tf = Rot(P, sb, 2, "tf", [128, TS], F32)
            pbank = Rot(P, ps, 6, "pb", [128, 512], F32)
            for i in range(NT):
                t0 = i * TS
                xt, bxt = xt_r.next()
                LD(xt[:], xA[:, :, t0:t0 + TS].rearrange("c p t -> p c t"), W=[bxt])
                sq, bsq = sq_r.next()
                ACT(sq[:], xt[:], AF.Square, R=[bxt], W=[bsq])
                pb, bpb = pbank.next()
                for c in range(8):
                    MM(pb[:], ones_bf[:], sq[:, c, :], start=(c == 0), stop=(c == 7), R=[bsq, b_ones], W=[bpb])
                rr, brr = tf.next()
                rsqrt_tile(rr[:], pb[:], 1.0 / D, R=[bpb], W=[brr])
                xn, bxn = xn_r.next()
                for c in range(8):
                    STT(P.dve, xn[:, c, :], xt[:, c, :], fin[:, c:c + 1], rr[:], ALU.mult, ALU.mult, R=[bxt, brr, b_fin], W=[bxn])
                yo, byo = yo_r.next()
                for s_ in range(4):
                    for c4 in range(2):
                        pb, bpb = pbank.next()
                        for cc in range(4):
                            c = c4 * 4 + cc
                            TR(pb[:, cc * 128:(cc + 1) * 128], xn[:, c, s_ * 128:(s_ + 1) * 128], ident[:], R=[bxn, b_ident], W=[bpb])
                        if (s_ + c4) % 2 == 0:
                            CP(P.dve, yo[:, s_, c4 * 512:(c4 + 1) * 512], pb[:], R=[bpb], W=[byo])
                        else:
                            P.op(P.act, lambda e, o=yo[:, s_, c4 * 512:(c4 + 1) * 512], i_=pb[:]: e.copy(out=o, in_=i_), reads=[bpb], writes=[byo])
                ST(y_out[t0:t0 + TS, :].rearrange("(s p) d -> p s d", p=128), yo[:], R=[byo])
            P.barrier()
        P.emit()
    return nc


def _swap_pairs(w):
    idx = np.arange(w.shape[-1]).reshape(-1, 2)[:, ::-1].reshape(-1)
    return w[..., idx]


def _rope_tables():
    rows = S // 64
    row = np.repeat(np.arange(rows, dtype=np.float32), 64)
    col = np.tile(np.arange(64, dtype=np.float32), rows)

    def tab(d_rot):
        n = d_rot // 4
        inv = (np.float32(10000.0) ** (-np.arange(n, dtype=np.float32) / np.float32(n))).astype(np.float32)
        ang = np.concatenate([row[:, None] * inv, col[:, None] * inv], axis=-1).astype(np.float32)
        c = np.cos(ang).astype(np.float32)
        s = np.sin(ang).astype(np.float32)
        cf = np.repeat(c, 2, axis=1)
        sf = np.repeat(s, 2, axis=1)
        sign = np.tile(np.array([-1.0, 1.0], np.float32), d_rot // 2)
        return np.ascontiguousarray(cf.T), np.ascontiguousarray((sf * sign).T)

    ca, sa = tab(32)
    cb, sb_ = tab(64)
    ropeA = np.stack([ca, sa]).astype(np.float32)
    ropeB = np.stack([np.concatenate([cb, cb]), np.concatenate([sb_, sb_])]).astype(np.float32)
    return ropeA, ropeB


def _host_layout(inputs):
    f = lambda k: np.asarray(inputs[k], dtype=np.float32)
    L = DEPTH
    w_in = f("w_in")
    sw_src = np.concatenate([w_in[:, :, 320:416], w_in[:, :, 416:800], w_in[:, :, 800:928]], axis=-1)
    w_in_sw = _swap_pairs(sw_src)
    w_uq = f("mla_w_uq")
    w_uq_sw = _swap_pairs(w_uq)
    w_ukv = f("mla_w_ukv").reshape(L, 128, 6, 2, 64)
    w_ukv_p = np.concatenate([w_ukv[:, :, :, 0, :].reshape(L, 128, 384), w_ukv[:, :, :, 1, :].reshape(L, 128, 384)], axis=-1)
    wsT = np.ascontiguousarray(f("gmlp_w_s").transpose(0, 1, 3, 2))
    pc = lambda v: v.reshape(L, -1, 128).transpose(0, 2, 1)
    gq = f("gqa_q_norm"); gk = f("gqa_k_norm")
    sw64 = np.arange(64).reshape(-1, 2)[:, ::-1].reshape(-1)
    tile2 = lambda v: np.concatenate([v, v], axis=-1)[:, :, None]
    cw = f("ffn_conv_w")
    cwp = cw.reshape(L, 3, 44, 128).transpose(0, 3, 2, 1).reshape(L, 128, 132)
    cb = pc(f("ffn_conv_b"))
    small = np.concatenate([
        pc(f("mix_norm")), pc(f("out_norm")), pc(f("mem_x_norm")), pc(f("mem_kv_norm")), pc(f("ffn_norm")),
        pc(f("mla_q_norm")), pc(f("mla_kv_norm")),
        tile2(gq), tile2(gq[:, sw64]), tile2(gk), tile2(gk[:, sw64]),
        cwp, cb], axis=-1).astype(np.float32)
    ropeA, ropeB = _rope_tables()
    shared = {
        "ident": np.eye(128, dtype=np.float32),
        "ropeA": ropeA, "ropeB": ropeB,
        "w_in": w_in, "w_in_sw": np.ascontiguousarray(w_in_sw),
        "w_uq": w_uq, "w_uq_sw": np.ascontiguousarray(w_uq_sw),
        "w_ukv": np.ascontiguousarray(w_ukv_p),
        "wsT": wsT, "bs": f("gmlp_b_s"), "gvn": f("gmlp_v_norm"),
        "w_out": f("w_out"), "mem_w_q": f("mem_w_q"), "mem_w_kv": f("mem_w_kv"), "mem_w_o": f("mem_w_o"),
        "w_up": f("ffn_w_up"), "w_dn": f("ffn_w_down"),
        "small": np.ascontiguousarray(small),
        "fin": np.ascontiguousarray(f("final_norm").reshape(8, 128).T),
    }
    return shared


_NC_CACHE = {}


def kernel(**inputs):
    shared = _host_layout(inputs)
    x = np.asarray(inputs["x"], dtype=np.float32)
    mem = np.asarray(inputs["mem"], dtype=np.float32)
    if "nc" not in _NC_CACHE:
        _NC_CACHE["nc"] = build()
    nc = _NC_CACHE["nc"]
    in_maps = []
    for c in range(NCORES):
        m = dict(shared)
        m["x"] = np.ascontiguousarray(x[c])
        m["mem"] = np.ascontiguousarray(mem[c])
        in_maps.append(m)
    res = run_bass_kernel_spmd(nc, in_maps, core_ids=list(range(NCORES)))
    return np.stack([res.results[c]["y"] for c in range(NCORES)], axis=0).astype(np.float32)
```

```python
from contextlib import ExitStack
import numpy as np
import concourse.bass as bass
import concourse.mybir as mybir
from concourse.bass_utils import run_bass_kernel_spmd

F32 = mybir.dt.float32
BF16 = mybir.dt.bfloat16
AF = mybir.ActivationFunctionType
ALU = mybir.AluOpType

S = 8192
D = 1024
TS = 512
NT = S // TS
DEPTH = 2
D_IN = 1568
D_FF = 2816
EPS = 1e-6
NCORES = 8


class Buf:
    __slots__ = ("name", "w", "r", "sem", "sem2")

    def __init__(self, name):
        self.name = name
        self.w = []
        self.r = []
        self.sem = None
        self.sem2 = None


def _prune(ts):
    best = {}
    for s, v in ts:
        k = id(s)
        if k not in best or best[k][1] < v:
            best[k] = (s, v)
    return list(best.values())


class Eng:
    def __init__(self, prog, name):
        self.prog = prog
        self.name = name
        self.ops = []
        self.sem = prog.es.enter_context(prog.nc.semaphore("s_" + name))
        self.count = 0
        self.waited = {}
        self.needed = set()
        prog.sem2eng[id(self.sem)] = self

    def wait(self, tickets, skip_own=False):
        for s, v in _prune(tickets):
            if skip_own and s is self.sem:
                continue
            k = id(s)
            if self.waited.get(k, 0) < v:
                self.ops.append(("wait", s, v))
                self.waited[k] = v
                src = self.prog.sem2eng.get(k)
                if src is not None:
                    src.needed.add(v)

    def replay(self, e):
        ranks = {}
        for eng in self.prog.engs:
            ranks[id(eng.sem)] = {v: i + 1 for i, v in enumerate(sorted(eng.needed))}
        mine = ranks[id(self.sem)]
        for ent in self.ops:
            if ent[0] == "wait":
                _, s_, v = ent
                r = ranks.get(id(s_))
                e.wait_ge(s_, r[v] if r is not None else v)
            elif ent[0] == "op":
                ins = ent[1](e)
                if ent[2] in mine:
                    ins.then_inc(self.sem, 1)
            else:
                ent[1](e).then_inc(ent[2], 16)


class Prog:
    def __init__(self, nc, es, n_dma_sems=48):
        self.nc = nc
        self.es = es
        self.sem2eng = {}
        self.pe = Eng(self, "pe")
        self.act = Eng(self, "act")
        self.dve = Eng(self, "dve")
        self.pool = Eng(self, "pool")
        self.sp = Eng(self, "sp")
        self.engs = [self.pe, self.act, self.dve, self.pool, self.sp]
        self.dsems = [[es.enter_context(nc.semaphore("d%d" % i)), 0] for i in range(n_dma_sems)]
        self.free_dsems = list(range(n_dma_sems))
        self.bufs = []

    def buf(self, name="b"):
        b = Buf(name)
        self.bufs.append(b)
        return b

    def bufs_n(self, n, name="b"):
        return [self.buf(name + str(i)) for i in range(n)]

    def _deps(self, reads, writes):
        deps = []
        for b in reads:
            deps += b.w
        for b in writes:
            deps += b.w
            deps += b.r
        return deps

    def op(self, eng, fn, reads=(), writes=()):
        eng.wait(self._deps(reads, writes), skip_own=(eng is self.pe))
        eng.count += 1
        t = (eng.sem, eng.count)
        eng.ops.append(("op", fn, eng.count))
        for b in reads:
            b.r = _prune(b.r + [t])
        for b in writes:
            b.w = [t]
            b.r = []
        return t

    def dma(self, eng, fn, reads=(), writes=()):
        owner = (list(writes) + list(reads))[0]
        if eng is self.pool:
            if owner.sem2 is None:
                owner.sem2 = self.dsems[self.free_dsems.pop()]
            rec = owner.sem2
        else:
            if owner.sem is None:
                owner.sem = self.dsems[self.free_dsems.pop()]
            rec = owner.sem
        eng.wait(self._deps(reads, writes))
        rec[1] += 16
        t = (rec[0], rec[1])
        eng.ops.append(("dma", fn, rec[0]))
        for b in reads:
            b.r = _prune(b.r + [t])
        for b in writes:
            b.w = _prune([x for x in b.w if x[0] is rec[0]] + [t])
            b.r = []
        return t

    def barrier(self):
        ts = [(e.sem, e.count) for e in self.engs if e.count > 0]
        ts += [(r[0], r[1]) for r in self.dsems if r[1] > 0]
        for e in self.engs:
            e.wait(ts)
        for b in self.bufs:
            b.w = []
            b.r = []
            if b.sem is not None:
                self.free_dsems.append(self.dsems.index(b.sem))
                b.sem = None
            if b.sem2 is not None:
                self.free_dsems.append(self.dsems.index(b.sem2))
                b.sem2 = None
        self.bufs = []

    def emit(self):
        with self.nc.Block() as block:
            block.sync(self.sp.replay)
            block.tensor(self.pe.replay)
            block.scalar(self.act.replay)
            block.vector(self.dve.replay)
            block.gpsimd(self.pool.replay)


class Rot:
    def __init__(self, P, alloc, n, name, shape, dt):
        self.t = [alloc("%s%d" % (name, i), shape, dt) for i in range(n)]
        self.b = [P.buf("%s%d" % (name, i)) for i in range(n)]
        self.i = 0

    def next(self):
        k = self.i % len(self.t)
        self.i += 1
        return self.t[k], self.b[k]


def build(debug=False, stop_after=None, taps=()):
    nc = bass.Bass("TRN2", target_bir_lowering=False)
    dram_in = lambda n, sh, dt=F32: nc.dram_tensor(n, list(sh), dt, kind="ExternalInput").ap()
    dram_sc = lambda n, sh, dt=F32: nc.dram_tensor(n, list(sh), dt, kind=("ExternalOutput" if n in taps else "Internal")).ap()

    x_in = dram_in("x", [S, D])
    mem_in = dram_in("mem", [256, D])
    ident_in = dram_in("ident", [128, 128])
    ropeA_in = dram_in("ropeA", [2, 32, S])
    ropeB_in = dram_in("ropeB", [2, 128, S])
    w_in_d = dram_in("w_in", [DEPTH, D, D_IN])
    w_in_sw_d = dram_in("w_in_sw", [DEPTH, D, 608])
    w_uq_d = dram_in("w_uq", [DEPTH, 256, 576])
    w_uq_sw_d = dram_in("w_uq_sw", [DEPTH, 256, 576])
    w_ukv_d = dram_in("w_ukv", [DEPTH, 128, 768])
    wsT_d = dram_in("wsT", [DEPTH, 4, 128, 128])
    bs_d = dram_in("bs", [DEPTH, 4, 128])
    gvn_d = dram_in("gvn", [DEPTH, 256])
    w_out_d = dram_in("w_out", [DEPTH, D, D])
    mem_w_q_d = dram_in("mem_w_q", [DEPTH, D, 512])
    mem_w_kv_d = dram_in("mem_w_kv", [DEPTH, D, 1024])
    mem_w_o_d = dram_in("mem_w_o", [DEPTH, 512, D])
    w_up_d = dram_in("w_up", [DEPTH, D, 2 * D_FF])
    w_dn_d = dram_in("w_dn", [DEPTH, D_FF, D])
    NSM = 223
    small_d = dram_in("small", [DEPTH, 128, NSM])
    fin_d = dram_in("fin", [128, 8])
    y_out = nc.dram_tensor("y", [S, D], F32, kind="ExternalOutput").ap()

    xA = dram_sc("xA", [8, 128, S])
    xB = dram_sc("xB", [8, 128, S])
    xC = dram_sc("xC", [8, 128, S])
    QA = dram_sc("QA", [6, 96, S], BF16)
    KN = dram_sc("KN", [3, 128, S], BF16)
    KPE = dram_sc("KPE", [32, S], BF16)
    VA = dram_sc("VA", [128, 64, 390], BF16)
    QB = dram_sc("QB", [3, 128, S], BF16)
    KB = dram_sc("KB", [128, S], BF16)
    VB = dram_sc("VB", [128, 64, 130], BF16)
    YT = dram_sc("YT", [8, 128, S])
    DEN = dram_sc("DEN", [4, TS])

    O_MIX, O_OUTN, O_MEMX, O_MEMKV, O_FFN = 0, 8, 16, 24, 32
    O_QN, O_KVN = 40, 42
    O_GQ1, O_GQ2, O_GK1, O_GK2 = 43, 44, 45, 46
    O_CW, O_CB = 47, 47 + 132

    with ExitStack() as es:
        P = Prog(nc, es)
        uid = [0]

        def U(n):
            uid[0] += 1
            return "%s_u%d" % (n, uid[0])

        SB = lambda n, sh, dt: es.enter_context(nc.sbuf_tensor(U(n), list(sh), dt))

        def MM(out, lhsT, rhs, start=True, stop=True, R=(), W=()):
            return P.op(P.pe, lambda e: e.matmul(out, lhsT=lhsT, rhs=rhs, start=start, stop=stop), reads=R, writes=W)

        def TR(out, in_, ident, R=(), W=()):
            return P.op(P.pe, lambda e: e.transpose(out, in_, ident), reads=R, writes=W)

        def ACT(out, in_, func, R=(), W=(), **kw):
            return P.op(P.act, lambda e: e.activation(out=out, in_=in_, func=func, **kw), reads=R, writes=W)

        def TS_(eng, out, in0, s1, s2, op0, op1=None, R=(), W=()):
            if op1 is None:
                return P.op(eng, lambda e: e.tensor_scalar(out=out, in0=in0, scalar1=s1, scalar2=None, op0=op0), reads=R, writes=W)
            return P.op(eng, lambda e: e.tensor_scalar(out=out, in0=in0, scalar1=s1, scalar2=s2, op0=op0, op1=op1), reads=R, writes=W)

        def STT(eng, out, in0, scalar, in1, op0, op1, R=(), W=()):
            return P.op(eng, lambda e: e.scalar_tensor_tensor(out=out, in0=in0, scalar=scalar, in1=in1, op0=op0, op1=op1), reads=R, writes=W)

        def TT(eng, out, in0, in1, op, R=(), W=()):
            return P.op(eng, lambda e: e.tensor_tensor(out=out, in0=in0, in1=in1, op=op), reads=R, writes=W)

        def CP(eng, out, in_, R=(), W=()):
            return P.op(eng, lambda e: e.tensor_copy(out=out, in_=in_), reads=R, writes=W)

        def RCP(out, in_, R=(), W=()):
            return P.op(P.dve, lambda e: e.reciprocal(out=out, in_=in_), reads=R, writes=W)

        def MSET(eng, ap, val, W=()):
            return P.op(eng, lambda e: e.memset(ap, val), writes=W)

        def LD(out, in_, W, R=()):
            return P.dma(P.sp, lambda e: e.dma_start(out=out, in_=in_), reads=R, writes=W)

        def ST(out, in_, R):
            return P.dma(P.pool, lambda e: e.dma_start(out=out, in_=in_), reads=R)

        ident = SB("ident", [128, 128], F32); b_ident = P.buf()
        ones_bf = SB("ones_bf", [128, 128], BF16); b_ones = P.buf()
        blk_bf = SB("blk_bf", [128, 128], BF16); b_blk = P.buf()
        ones_f = SB("ones_f", [128, 64], F32); b_onesf = P.buf()
        small = SB("small", [128, DEPTH, NSM], F32); b_small = P.buf()
        fin = SB("fin", [128, 8], F32); b_fin = P.buf()
        LD(ident[:], ident_in[:, :], W=[b_ident])
        LD(small[:], small_d.rearrange("l p n -> p l n"), W=[b_small])
        LD(fin[:], fin_d[:, :], W=[b_fin])
        MSET(P.dve, ones_bf[:], 1.0, W=[b_ones])
        MSET(P.dve, blk_bf[:], 0.0, W=[b_blk])
        MSET(P.dve, blk_bf[0:64, 0:64], 1.0, W=[b_blk])
        MSET(P.dve, blk_bf[64:128, 64:128], 1.0, W=[b_blk])
        MSET(P.dve, ones_f[:], 1.0, W=[b_onesf])
        constbufs = [b_ident, b_ones, b_blk, b_onesf, b_small, b_fin]

        def keep_consts():
            P.bufs.extend(constbufs)

        def smallcol(l, off, n=1):
            return small[:, l, off:off + n]

        def phase_scope():
            pes = ExitStack()
            sb = lambda n, sh, dt: pes.enter_context(nc.sbuf_tensor(U(n), list(sh), dt))
            ps = lambda n, sh, dt: pes.enter_context(nc.psum_tensor(U(n), list(sh), dt))
            return pes, sb, ps

        def rsqrt_tile(dst, src_ps, inv_n, R, W):
            ACT(dst, src_ps, AF.Sqrt, R=R, W=W, scale=inv_n, bias=EPS)
            RCP(dst, dst, R=W, W=W)

        def load_weight(dst_bf, src_dram, rows, cols, stage, gain_ap=None, col_chunk=2048):
            for c0 in range(0, cols, col_chunk):
                c1 = min(cols, c0 + col_chunk)
                st, bst = stage.next()
                LD(st[0:rows, 0:c1 - c0], src_dram[:, c0:c1], W=[bst])
                if gain_ap is None:
                    CP(P.dve, dst_bf(c0, c1), st[0:rows, 0:c1 - c0], R=[bst], W=[])
                else:
                    TS_(P.dve, dst_bf(c0, c1), st[0:rows, 0:c1 - c0], gain_ap, None, ALU.mult, R=[bst, b_small], W=[])

        memT = SB("memT", [128, 8, 256], F32); b_memT = P.buf()
        constbufs.append(b_memT)
        pes, sb, ps = phase_scope()
        with pes:
            xin = Rot(P, sb, 2, "xin", [128, 4, D], F32)
            xo = Rot(P, sb, 2, "xo", [128, 8, TS], F32)
            pbank = Rot(P, ps, 4, "pb", [128, 512], F32)
            mt, bmt = xin.next()
            LD(mt[:, 0:2, :], mem_in.rearrange("(s p) d -> p s d", p=128), W=[bmt])
            for c in range(8):
                pb, bpb = pbank.next()
                for s_ in range(2):
                    TR(pb[:, s_ * 128:(s_ + 1) * 128], mt[:, s_, c * 128:(c + 1) * 128], ident[:], R=[bmt, b_ident], W=[bpb])
                CP(P.dve, memT[:, c, :], pb[:, 0:256], R=[bpb], W=[b_memT])
            for i in range(NT):
                xt, bxt = xin.next()
                LD(xt[:], x_in[i * TS:(i + 1) * TS, :].rearrange("(s p) d -> p s d", p=128), W=[bxt])
                xo_t, bxo = xo.next()
                for c in range(8):
                    pb, bpb = pbank.next()
                    for s_ in range(4):
                        TR(pb[:, s_ * 128:(s_ + 1) * 128], xt[:, s_, c * 128:(c + 1) * 128], ident[:], R=[bxt, b_ident], W=[bpb])
                    if c % 2 == 0:
                        CP(P.dve, xo_t[:, c, :], pb[:], R=[bpb], W=[bxo])
                    else:
                        P.op(P.act, lambda e, o=xo_t[:, c, :], i_=pb[:]: e.copy(out=o, in_=i_), reads=[bpb], writes=[bxo])
                ST(xA[:, :, i * TS:(i + 1) * TS].rearrange("c p t -> p c t"), xo_t[:], R=[bxo])
            P.barrier()
        keep_consts()
        if stop_after == "p0":
            P.emit()
            return nc

        for l in range(DEPTH):
            pes, sb, ps = phase_scope()
            with pes:
                w_in = sb("w_in", [128, 8, D_IN], BF16)
                w_sw = sb("w_sw", [128, 8, 608], BF16)
                w_uq = sb("w_uq", [128, 2, 576], BF16)
                w_uqs = sb("w_uqs", [128, 2, 576], BF16)
                w_ukv = sb("w_ukv", [128, 768], BF16)
                wsT = sb("wsT", [128, 4, 128], BF16)
                gvn = sb("gvn", [128, 256], F32); b_gvn = P.buf()
                biasT = sb("biasT", [128, 2, 128], F32); b_biasT = P.buf()
                b_w = P.buf("weights")
                wes = ExitStack()
                with wes:
                    stage = Rot(P, lambda n, sh, dt: wes.enter_context(nc.sbuf_tensor(U(n), list(sh), dt)), 3, "stg", [128, 2048], F32)
                    for c in range(8):
                        g = smallcol(l, O_MIX + c)
                        load_weight(lambda a, b, c=c: w_in[:, c, a:b], w_in_d[l, c * 128:(c + 1) * 128, :], 128, D_IN, stage, g)
                        load_weight(lambda a, b, c=c: w_sw[:, c, a:b], w_in_sw_d[l, c * 128:(c + 1) * 128, :], 128, 608, stage, g)
                    for c in range(2):
                        g = smallcol(l, O_QN + c)
                        load_weight(lambda a, b, c=c: w_uq[:, c, a:b], w_uq_d[l, c * 128:(c + 1) * 128, :], 128, 576, stage, g)
                        load_weight(lambda a, b, c=c: w_uqs[:, c, a:b], w_uq_sw_d[l, c * 128:(c + 1) * 128, :], 128, 576, stage, g)
                    load_weight(lambda a, b: w_ukv[:, a:b], w_ukv_d[l, :, :], 128, 768, stage, smallcol(l, O_KVN))
                    for g_ in range(4):
                        load_weight(lambda a, b, g_=g_: wsT[:, g_, a:b], wsT_d[l, g_, :, :], 128, 128, stage, None)
                    LD(gvn[:], gvn_d[l:l + 1, :].broadcast_to([128, 256]), W=[b_gvn])
                    for g_ in range(4):
                        k_, half = g_ // 2, g_ % 2
                        LD(biasT[half * 64:(half + 1) * 64, k_, :], bs_d[l, g_:g_ + 1, :].broadcast_to([64, 128]), W=[b_biasT])
                    P.barrier()
                keep_consts()
                P.bufs.extend([b_w, b_gvn, b_biasT])

                xt_r = Rot(P, sb, 2, "xt", [128, 8, TS], F32)
                sq_r = Rot(P, sb, 1, "sq", [128, 8, TS], BF16)
                h_r = Rot(P, sb, 2, "h", [128, 8, TS], BF16)
                tabA_r = Rot(P, sb, 2, "tabA", [96, 2, TS], F32)
                tabB_r = Rot(P, sb, 2, "tabB", [128, 2, TS], F32)
                tf = Rot(P, sb, 8, "tf", [128, TS], F32)
                tb = Rot(P, sb, 4, "tb", [128, TS], BF16)
                cqn_r = Rot(P, sb, 1, "cqn", [128, 2, TS], BF16)
                qa_r = Rot(P, sb, 2, "qa", [96, 6, TS], BF16)
                kn_r = Rot(P, sb, 2, "kn", [128, 3, TS], BF16)
                kpe_r = Rot(P, sb, 2, "kpe", [96, TS], BF16)
                va_r = Rot(P, sb, 2, "va", [128, 4, 390], BF16)
                qb_r = Rot(P, sb, 2, "qb", [128, 3, TS], BF16)
                kb_r = Rot(P, sb, 2, "kb", [128, TS], BF16)
                vb_r = Rot(P, sb, 2, "vb", [128, 4, 130], BF16)
                u_r = Rot(P, sb, 1, "u", [128, 2, TS], F32)
                yc_r = Rot(P, sb, 2, "yc", [128, 2, TS], F32)
                vvg_r = Rot(P, sb, 2, "vvg", [128, 256], F32)
                vvn_r = Rot(P, sb, 2, "vvn", [128, 256], BF16)
                sc_r = Rot(P, sb, 4, "sc", [128, 2], F32)
                pbank = Rot(P, ps, 8, "pb", [128, 512], F32)
                for k in range(2):
                    for s_ in range(4):
                        MSET(P.pool, va_r.t[k][:, s_, :], 1.0, W=[va_r.b[k]])
                        MSET(P.pool, vb_r.t[k][:, s_, :], 1.0, W=[vb_r.b[k]])

                for i in range(NT):
                    t0 = i * TS
                    xt, bxt = xt_r.next()
                    LD(xt[:], xA[:, :, t0:t0 + TS].rearrange("c p t -> p c t"), W=[bxt])
                    tabA, btA = tabA_r.next()
                    LD(tabA[64:96, :, :], ropeA_in[:, :, t0:t0 + TS].rearrange("k p t -> p k t"), W=[btA])
                    tabB, btB = tabB_r.next()
                    LD(tabB[:], ropeB_in[:, :, t0:t0 + TS].rearrange("k p t -> p k t"), W=[btB])
                    sq, bsq = sq_r.next()
                    ACT(sq[:], xt[:], AF.Square, R=[bxt], W=[bsq])
                    pb, bpb = pbank.next()
                    for c in range(8):
                        MM(pb[:], ones_bf[:], sq[:, c, :], start=(c == 0), stop=(c == 7), R=[bsq, b_ones], W=[bpb])
                    r0, br0 = tf.next()
                    rsqrt_tile(r0[:], pb[:], 1.0 / D, R=[bpb], W=[br0])
                    h, bh = h_r.next()
                    for c in range(8):
                        TT(P.dve if c % 2 == 0 else P.pool, h[:, c, :], xt[:, c, :], r0[:], ALU.mult, R=[bxt, br0], W=[bh])

                    def proj(wt, col0, ncol, R=(bh, b_w)):
                        pb_, bpb_ = pbank.next()
                        for c in range(8):
                            MM(pb_[0:ncol, :], wt[:, c, col0:col0 + ncol], h[:, c, :], start=(c == 0), stop=(c == 7), R=list(R), W=[bpb_])
                        return pb_, bpb_

                    cq = [proj(w_in, 0, 128), proj(w_in, 128, 128)]
                    sqc, bsqc = [], []
                    for k in range(2):
                        t_, b_ = tb.next()
                        ACT(t_[:], cq[k][0][:], AF.Square, R=[cq[k][1]], W=[b_])
                        sqc.append(t_); bsqc.append(b_)
                    pss, bpss = pbank.next()
                    for k in range(2):
                        MM(pss[:], ones_bf[:], sqc[k][:], start=(k == 0), stop=(k == 1), R=[bsqc[k], b_ones], W=[bpss])
                    rq, brq = tf.next()
                    rsqrt_tile(rq[:], pss[:], 1.0 / 256, R=[bpss], W=[brq])
                    cqn, bcqn = cqn_r.next()
                    for k in range(2):
                        TT(P.dve, cqn[:, k, :], cq[k][0][:], rq[:], ALU.mult, R=[cq[k][1], brq], W=[bcqn])
                    qa, bqa = qa_r.next()
                    for hh in range(6):
                        pq, bpq = pbank.next()
                        pqs, bpqs = pbank.next()
                        for k in range(2):
                            MM(pq[0:96, :], w_uq[:, k, hh * 96:(hh + 1) * 96], cqn[:, k, :], start=(k == 0), stop=(k == 1), R=[bcqn, b_w], W=[bpq])
                        for k in range(2):
                            MM(pqs[0:96, :], w_uqs[:, k, hh * 96:(hh + 1) * 96], cqn[:, k, :], start=(k == 0), stop=(k == 1), R=[bcqn, b_w], W=[bpqs])
                        P.op(P.act, lambda e, o=qa[0:64, hh, :], i_=pq[0:64, :]: e.copy(out=o, in_=i_), reads=[bpq], writes=[bqa])
                        t1, bt1 = tf.next()
                        t2, bt2 = tf.next()
                        TT(P.dve, t1[64:96, :], pq[64:96, :], tabA[64:96, 0, :], ALU.mult, R=[bpq, btA], W=[bt1])
                        TT(P.dve, t2[64:96, :], pqs[64:96, :], tabA[64:96, 1, :], ALU.mult, R=[bpqs, btA], W=[bt2])
                        TT(P.pool, qa[64:96, hh, :], t1[64:96, :], t2[64:96, :], ALU.add, R=[bt1, bt2], W=[bqa])
                    ST(QA[:, :, t0:t0 + TS].rearrange("h p t -> p h t"), qa[:], R=[bqa])

                    ckv = proj(w_in, 256, 128)
                    kr = proj(w_in, 320, 96)
                    krs = proj(w_sw, 0, 96)
                    t_, b_ = tb.next()
                    ACT(t_[:], ckv[0][:], AF.Square, R=[ckv[1]], W=[b_])
                    pss, bpss = pbank.next()
                    MM(pss[:], ones_bf[:], t_[:], R=[b_, b_ones], W=[bpss])
                    rk, brk = tf.next()
                    rsqrt_tile(rk[:], pss[:], 1.0 / 128, R=[bpss], W=[brk])
                    ckvn, bckvn = tb.next()
                    TT(P.dve, ckvn[:], ckv[0][:], rk[:], ALU.mult, R=[ckv[1], brk], W=[bckvn])
                    kn, bkn = kn_r.next()
                    for b3 in range(3):
                        pk, bpk = pbank.next()
                        MM(pk[:], w_ukv[:, b3 * 128:(b3 + 1) * 128], ckvn[:], R=[bckvn, b_w], W=[bpk])
                        if b3 % 2 == 0:
                            P.op(P.act, lambda e, o=kn[:, b3, :], i_=pk[:]: e.copy(out=o, in_=i_), reads=[bpk], writes=[bkn])
                        else:
                            CP(P.dve, kn[:, b3, :], pk[:], R=[bpk], W=[bkn])
                    ST(KN[:, :, t0:t0 + TS].rearrange("c p t -> p c t"), kn[:], R=[bkn])
                    kpe, bkpe = kpe_r.next()
                    t1, bt1 = tf.next()
                    t2, bt2 = tf.next()
                    TT(P.dve, t1[64:96, :], kr[0][64:96, :], tabA[64:96, 0, :], ALU.mult, R=[kr[1], btA], W=[bt1])
                    TT(P.dve, t2[64:96, :], krs[0][64:96, :], tabA[64:96, 1, :], ALU.mult, R=[krs[1], btA], W=[bt2])
                    TT(P.pool, kpe[64:96, :], t1[64:96, :], t2[64:96, :], ALU.add, R=[bt1, bt2], W=[bkpe])
                    ST(KPE[:, t0:t0 + TS], kpe[64:96, :], R=[bkpe])
                    va, bva = va_r.next()
                    for s_ in range(4):
                        pv, bpv = pbank.next()
                        MM(pv[:, 0:384], ckvn[:, s_ * 128:(s_ + 1) * 128], w_ukv[:, 384:768], R=[bckvn, b_w], W=[bpv])
                        dst = va[:, s_, :].rearrange("p (h c) -> p h c", c=65)[:, :, 0:64]
                        src = pv[:, 0:384].rearrange("p (h c) -> p h c", c=64)
                        if s_ % 2 == 0:
                            CP(P.dve, dst, src, R=[bpv], W=[bva])
                        else:
                            P.op(P.act, lambda e, o=dst, i_=src: e.copy(out=o, in_=i_), reads=[bpv], writes=[bva])
                    ST(VA[:, 4 * i:4 * i + 4, :], va[:], R=[bva])

                    qb, bqb = qb_r.next()
                    kb, bkb = kb_r.next()
                    for cc in range(4):
                        if cc < 3:
                            raw = proj(w_in, 416 + 128 * cc, 128)
                            swp = proj(w_sw, 96 + 128 * cc, 128)
                            og1, og2 = O_GQ1, O_GQ2
                            dst, bdst = qb[:, cc, :], bqb
                        else:
                            raw = proj(w_in, 800, 128)
                            swp = proj(w_sw, 96 + 384, 128)
                            og1, og2 = O_GK1, O_GK2
                            dst, bdst = kb[:], bkb
                        t_, b_ = tb.next()
                        ACT(t_[:], raw[0][:], AF.Square, R=[raw[1]], W=[b_])
                        pss, bpss = pbank.next()
                        MM(pss[:], blk_bf[:], t_[:], R=[b_, b_blk], W=[bpss])
                        rr, brr = tf.next()
                        rsqrt_tile(rr[:], pss[:], 1.0 / 64, R=[bpss], W=[brr])
                        t1, bt1 = tf.next()
                        t2, bt2 = tf.next()
                        STT(P.dve, t1[:], raw[0][:], smallcol(l, og1), tabB[:, 0, :], ALU.mult, ALU.mult, R=[raw[1], btB, b_small], W=[bt1])
                        STT(P.dve, t2[:], swp[0][:], smallcol(l, og2), tabB[:, 1, :], ALU.mult, ALU.mult, R=[swp[1], btB, b_small], W=[bt2])
                        TT(P.pool, t1[:], t1[:], t2[:], ALU.add, R=[bt2], W=[bt1])
                        TT(P.pool, dst, t1[:], rr[:], ALU.mult, R=[bt1, brr], W=[bdst])
                    ST(QB[:, :, t0:t0 + TS].rearrange("c p t -> p c t"), qb[:], R=[bqb])
                    ST(KB[:, t0:t0 + TS], kb[:], R=[bkb])
                    vb, bvb = vb_r.next()
                    for s_ in range(4):
                        pv, bpv = pbank.next()
                        for c in range(8):
                            MM(pv[:, 0:128], h[:, c, s_ * 128:(s_ + 1) * 128], w_in[:, c, 928:1056], start=(c == 0), stop=(c == 7), R=[bh, b_w], W=[bpv])
                        dst = vb[:, s_, :].rearrange("p (h c) -> p h c", c=65)[:, :, 0:64]
                        src = pv[:, 0:128].rearrange("p (h c) -> p h c", c=64)
                        P.op(P.act, lambda e, o=dst, i_=src: e.copy(out=o, in_=i_), reads=[bpv], writes=[bvb])
                    ST(VB[:, 4 * i:4 * i + 4, :], vb[:], R=[bvb])

                    u, bu = u_r.next()
                    for k in range(2):
                        pu = proj(w_in, 1056 + 128 * k, 128)
                        ACT(u[:, k, :], pu[0][:], AF.Gelu_apprx_tanh, R=[pu[1]], W=[bu])
                    yc, byc = yc_r.next()
                    for s_ in range(4):
                        pv, bpv = pbank.next()
                        for c in range(8):
                            MM(pv[:, 0:256], h[:, c, s_ * 128:(s_ + 1) * 128], w_in[:, c, 1312:1568], start=(c == 0), stop=(c == 7), R=[bh, b_w], W=[bpv])
                        vvg, bvvg = vvg_r.next()
                        ACT(vvg[:], pv[:, 0:256], AF.Gelu_apprx_tanh, R=[bpv], W=[bvvg])
                        sc, bsc = sc_r.next()
                        junk, bjunk = tf.next()
                        ACT(junk[:, 0:256], vvg[:], AF.Square, R=[bvvg], W=[bjunk, bsc], accum_out=sc[:, 0:1])
                        ACT(sc[:, 1:2], sc[:, 0:1], AF.Sqrt, R=[bsc], W=[bsc], scale=1.0 / 256, bias=EPS)
                        RCP(sc[:, 1:2], sc[:, 1:2], R=[bsc], W=[bsc])
                        vvn, bvvn = vvn_r.next()
                        STT(P.dve, vvn[:], vvg[:], sc[:, 1:2], gvn[:], ALU.mult, ALU.mult, R=[bvvg, bsc, b_gvn], W=[bvvn])
                        for k in range(2):
                            for half in range(2):
                                g_ = 2 * k + half
                                pm, bpm = pbank.next()
                                MM(pm[:, 0:128], vvn[:, k * 128:(k + 1) * 128], wsT[:, g_, :], R=[bvvn, b_w], W=[bpm])
                                lo, hi = half * 64, half * 64 + 64
                                tm, btm = tf.next()
                                TT(P.dve, tm[lo:hi, 0:128], pm[lo:hi, 0:128], biasT[lo:hi, k, :], ALU.add, R=[bpm, b_biasT], W=[btm])
                                TT(P.pool, yc[lo:hi, k, s_ * 128:(s_ + 1) * 128], tm[lo:hi, 0:128], u[lo:hi, k, s_ * 128:(s_ + 1) * 128], ALU.mult, R=[btm, bu], W=[byc])
                    ST(YT[6:8, :, t0:t0 + TS].rearrange("c p t -> p c t"), yc[:], R=[byc])
                P.barrier()
            keep_consts()
            if stop_after == "a%d" % l:
                P.emit()
                return nc

            def attention(group):
                pes, sb, ps = phase_scope()
                with pes:
                    if group == "a":
                        dk, nh, vw = 96, 6, 390
                        Vd = VA
                        scale = 96 ** -0.5
                    else:
                        dk, nh, vw = 64, 6, 130
                        Vd = VB
                        scale = 64 ** -0.5
                    v_sb = sb("v_sb", [128, 64, vw], BF16); b_v = P.buf()
                    LD(v_sb[:, 0:32, :], Vd[:, 0:32, :], W=[b_v])
                    LD(v_sb[:, 32:64, :], Vd[:, 32:64, :], W=[b_v])
                    dkp = 96 if group == "a" else 128
                    kt_r = Rot(P, sb, 2, "kt", [dkp, S], BF16)
                    q_r = Rot(P, sb, 3, "q", [dkp, TS], BF16)
                    if group == "b":
                        for k_ in range(2):
                            MSET(P.pool, kt_r.t[k_][64:128, :], 0.0, W=[kt_r.b[k_]])
                        for k_ in range(3):
                            MSET(P.pool, q_r.t[k_][64:128, :], 0.0, W=[q_r.b[k_]])
                    p_r = Rot(P, sb, 3, "p", [128, 2 * TS], BF16)
                    rd_r = Rot(P, sb, 2, "rd", [65, TS], F32)
                    bc_r = Rot(P, sb, 2, "bc", [64, TS], F32)
                    o_r = Rot(P, sb, 2, "o", [64, TS], F32)
                    s_r = Rot(P, ps, 3, "s", [128, 2 * TS], F32)
                    ops_r = Rot(P, ps, 2, "o_ps", [128, TS], F32)

                    def load_k(hh):
                        kt, bkt = kt_r.next()
                        if group == "a":
                            src = KN[hh // 2, (hh % 2) * 64:(hh % 2) * 64 + 64, :]
                            for q4 in range(4):
                                LD(kt[0:64, q4 * 2048:(q4 + 1) * 2048], src[:, q4 * 2048:(q4 + 1) * 2048], W=[bkt])
                                LD(kt[64:96, q4 * 2048:(q4 + 1) * 2048], KPE[:, q4 * 2048:(q4 + 1) * 2048], W=[bkt])
                        else:
                            src = KB[hh * 64:hh * 64 + 64, :]
                            for q4 in range(4):
                                LD(kt[0:64, q4 * 2048:(q4 + 1) * 2048], src[:, q4 * 2048:(q4 + 1) * 2048], W=[bkt])
                        return kt, bkt

                    def load_q(hh, qt):
                        q, bq = q_r.next()
                        if group == "a":
                            LD(q[:], QA[hh, :, qt * TS:(qt + 1) * TS], W=[bq])
                        else:
                            LD(q[0:64, :], QB[hh // 2, (hh % 2) * 64:(hh % 2) * 64 + 64, qt * TS:(qt + 1) * TS], W=[bq])
                        return q, bq

                    b_den = P.bufs_n(4, "den")
                    den_i = [0]
                    kcur = load_k(0)
                    for hh in range(nh):
                        kvh = hh if group == "a" else hh // 3
                        if group == "a":
                            knext = load_k(hh + 1) if hh + 1 < nh else None
                        else:
                            knext = load_k(1) if hh == 2 else None
                        kt, bkt = kcur
                        if hh == 0:
                            qn = load_q(0, 0)
                        for qt in range(NT):
                            q, bq = qn
                            if qt + 1 < NT:
                                qn = load_q(hh, qt + 1)
                            elif hh + 1 < nh:
                                qn = load_q(hh + 1, 0)
                            NJ = 32
                            sb_list = {}
                            o_ps, b_ops = ops_r.next()

                            def do_s(jj):
                                st, bst = s_r.next()
                                for k2 in range(2):
                                    j = 2 * jj + k2
                                    MM(st[:, k2 * TS:(k2 + 1) * TS], kt[:, j * 128:(j + 1) * 128], q[:], R=[bkt, bq], W=[bst])
                                sb_list[jj] = (st, bst)

                            def do_exp(jj):
                                st, bst = sb_list[jj]
                                p, bp = p_r.next()
                                ACT(p[:], st[:], AF.Exp, R=[bst], W=[bp], scale=scale)
                                return p, bp

                            def do_pv(jj, p, bp):
                                for k2 in range(2):
                                    j = 2 * jj + k2
                                    MM(o_ps[0:65, :], v_sb[:, j, kvh * 65:(kvh + 1) * 65], p[:, k2 * TS:(k2 + 1) * TS],
                                       start=(j == 0), stop=(j == 63), R=[b_v, bp], W=[b_ops])

                            do_s(0)
                            do_s(1)
                            for jj in range(NJ):
                                p, bp = do_exp(jj)
                                if jj + 2 < NJ:
                                    do_s(jj + 2)
                                do_pv(jj, p, bp)
                            rd, brd = rd_r.next()
                            RCP(rd[64:65, :], o_ps[64:65, :], R=[b_ops], W=[brd])
                            bc, bbc = bc_r.next()
                            dslot = den_i[0] % 4
                            den_i[0] += 1
                            P.dma(P.sp, lambda e, o=DEN[dslot:dslot + 1, :], i_=rd[64:65, :]: e.dma_start(out=o, in_=i_), reads=[brd], writes=[b_den[dslot]])
                            LD(bc[:], DEN[dslot:dslot + 1, :].broadcast_to([64, TS]), W=[bbc], R=[b_den[dslot]])
                            o, bo = o_r.next()
                            TT(P.dve, o[:], o_ps[0:64, :], bc[:], ALU.mult, R=[b_ops, bbc], W=[bo])
                            chunk = (0 if group == "a" else 3) + hh // 2
                            ST(YT[chunk, (hh % 2) * 64:(hh % 2) * 64 + 64, qt * TS:(qt + 1) * TS], o[:], R=[bo])
                        if knext is not None:
                            kcur = knext
                    P.barrier()
                keep_consts()

            attention("a")
            if stop_after == "ba%d" % l:
                P.emit()
                return nc
            attention("b")
            if stop_after == "b%d" % l:
                P.emit()
                return nc

            pes, sb, ps = phase_scope()
            with pes:
                w_out = sb("w_out", [128, 8, D], BF16)
                w_q = sb("w_q", [128, 8, 512], BF16)
                w_kv = sb("w_kv", [128, 8, 1024], BF16)
                w_o = sb("w_o", [128, 4, D], BF16)
                km = sb("km", [128, 4, 256], BF16); b_km = P.buf()
                vm = sb("vm", [128, 2, 512], BF16); b_vm = P.buf()
                b_w = P.buf("weights")
                wes = ExitStack()
                with wes:
                    wsb = lambda n, sh, dt: wes.enter_context(nc.sbuf_tensor(U(n), list(sh), dt))
                    stage = Rot(P, wsb, 3, "stg", [128, 2048], F32)
                    for c in range(8):
                        load_weight(lambda a, b, c=c: w_out[:, c, a:b], w_out_d[l, c * 128:(c + 1) * 128, :], 128, D, stage, smallcol(l, O_OUTN + c))
                        load_weight(lambda a, b, c=c: w_q[:, c, a:b], mem_w_q_d[l, c * 128:(c + 1) * 128, :], 128, 512, stage, smallcol(l, O_MEMX + c))
                        load_weight(lambda a, b, c=c: w_kv[:, c, a:b], mem_w_kv_d[l, c * 128:(c + 1) * 128, :], 128, 1024, stage, smallcol(l, O_MEMKV + c))
                    for c in range(4):
                        load_weight(lambda a, b, c=c: w_o[:, c, a:b], mem_w_o_d[l, c * 128:(c + 1) * 128, :], 128, D, stage, None)
                    P.barrier()
                    keep_consts()
                    P.bufs.extend([b_km, b_vm, b_w])
                    msq = wsb("msq", [128, 8, 256], BF16); b_msq = P.buf()
                    memn = wsb("memn", [128, 8, 256], BF16); b_memn = P.buf()
                    mr = wsb("mr", [128, 256], F32); b_mr = P.buf()
                    pbk = Rot(P, lambda n, sh, dt: wes.enter_context(nc.psum_tensor(U(n), list(sh), dt)), 4, "pbm", [128, 512], F32)
                    ACT(msq[:], memT[:], AF.Square, R=[b_memT], W=[b_msq])
                    pb, bpb = pbk.next()
                    for c in range(8):
                        MM(pb[:, 0:256], ones_bf[:], msq[:, c, :], start=(c == 0), stop=(c == 7), R=[b_msq, b_ones], W=[bpb])
                    rsqrt_tile(mr[:], pb[:, 0:256], 1.0 / D, R=[bpb], W=[b_mr])
                    for c in range(8):
                        TT(P.dve, memn[:, c, :], memT[:, c, :], mr[:], ALU.mult, R=[b_memT, b_mr], W=[b_memn])
                    for hm in range(4):
                        pb, bpb = pbk.next()
                        for c in range(8):
                            MM(pb[:, 0:256], w_kv[:, c, hm * 128:(hm + 1) * 128], memn[:, c, :], start=(c == 0), stop=(c == 7), R=[b_memn], W=[bpb])
                        CP(P.dve, km[:, hm, :], pb[:, 0:256], R=[bpb], W=[b_km])
                    for kt_ in range(2):
                        pb, bpb = pbk.next()
                        for c in range(8):
                            MM(pb[:], memn[:, c, kt_ * 128:(kt_ + 1) * 128], w_kv[:, c, 512:1024], start=(c == 0), stop=(c == 7), R=[b_memn], W=[bpb])
                        CP(P.dve, vm[:, kt_, :], pb[:], R=[bpb], W=[b_vm])
                    P.barrier()
                keep_consts()
                P.bufs.extend([b_km, b_vm, b_w])

                xt_r = Rot(P, sb, 2, "xt", [128, 8, TS], F32)
                yt_r = Rot(P, sb, 2, "yt", [128, 8, TS], F32)
                sqy_r = Rot(P, sb, 1, "sqy", [128, 8, TS], BF16)
                sqx_r = Rot(P, sb, 1, "sqx", [128, 8, TS], BF16)
                yn_r = Rot(P, sb, 2, "yn", [128, 8, TS], BF16)
                h2_r = Rot(P, sb, 1, "h2", [128, 8, TS], BF16)
                qm_r = Rot(P, sb, 1, "qm", [128, 4, TS], BF16)
                om_r = Rot(P, sb, 1, "om", [128, 4, TS], BF16)
                pm_r = Rot(P, sb, 2, "pm", [128, 2 * TS], BF16)
                tf = Rot(P, sb, 6, "tf", [128, TS], F32)
                pbank = Rot(P, ps, 3, "pb", [128, 512], F32)
                ssm_ps = ps("ssm", [128, 512], F32); b_ssm = P.buf()
                s_r = Rot(P, ps, 2, "s", [128, 2 * TS], F32)
                mscale = 128 ** -0.5
                xt_cb = [P.bufs_n(8, "xtc") for _ in range(2)]
                yn_cb = [P.bufs_n(8, "ync") for _ in range(2)]
                sqx_cb = P.bufs_n(8, "sqxc")
                h2_cb = P.bufs_n(8, "h2c")
                qm_cb = P.bufs_n(4, "qmc")
                om_cb = P.bufs_n(4, "omc")

                def c_stage0(i):
                    st = {"t0": i * TS, "k": i % 2}
                    k = i % 2
                    st["yt"], st["byt"] = yt_r.next()
                    LD(st["yt"][:], YT[:, :, i * TS:(i + 1) * TS].rearrange("c p t -> p c t"), W=[st["byt"]])
                    st["xt"], st["bx"] = xt_r.t[k], xt_cb[k]
                    LD(st["xt"][:], xA[:, :, i * TS:(i + 1) * TS].rearrange("c p t -> p c t"), W=st["bx"])
                    st["yn"], st["byn"] = yn_r.t[k], yn_cb[k]
                    return st

                def c_stage_y1(st):
                    sq, bsq = sqy_r.next()
                    ACT(sq[:], st["yt"][:], AF.Square, R=[st["byt"]], W=[bsq])
                    st["sqy"], st["bsqy"] = sq, bsq

                def c_stage_y2(st):
                    sq, bsq, yt, byt, yn, byn = st["sqy"], st["bsqy"], st["yt"], st["byt"], st["yn"], st["byn"]
                    for (c0, c1) in ((0, 3), (3, 6), (6, 8)):
                        pb, bpb = pbank.next()
                        for c in range(c0, c1):
                            MM(pb[:], ones_bf[:], sq[:, c, :], start=(c == c0), stop=(c == c1 - 1), R=[bsq, b_ones], W=[bpb])
                        rr, brr = tf.next()
                        rsqrt_tile(rr[:], pb[:], 1.0 / (128 * (c1 - c0)), R=[bpb], W=[brr])
                        for c in range(c0, c1):
                            TT(P.dve if c % 2 == 0 else P.pool, yn[:, c, :], yt[:, c, :], rr[:], ALU.mult, R=[byt, brr], W=[byn[c]])

                def c_body(st, nxt):
                    xt, bx, yn, byn, t0 = st["xt"], st["bx"], st["yn"], st["byn"], st["t0"]
                    sq = sqx_r.t[0]
                    for nb in range(8):
                        pb, bpb = pbank.next()
                        for c in range(8):
                            MM(pb[:], w_out[:, c, nb * 128:(nb + 1) * 128], yn[:, c, :], start=(c == 0), stop=(c == 7), R=[byn[c], b_w], W=[bpb])
                        TT(P.dve, xt[:, nb, :], pb[:], xt[:, nb, :], ALU.add, R=[bpb], W=[bx[nb]])
                        ACT(sq[:, nb, :], xt[:, nb, :], AF.Square, R=[bx[nb]], W=[sqx_cb[nb]])
                        if nb >= 1:
                            MM(ssm_ps[:], ones_bf[:], sq[:, nb - 1, :], start=(nb == 1), stop=False, R=[sqx_cb[nb - 1], b_ones], W=[b_ssm])
                    MM(ssm_ps[:], ones_bf[:], sq[:, 7, :], start=False, stop=True, R=[sqx_cb[7], b_ones], W=[b_ssm])
                    rr, brr = tf.next()
                    rsqrt_tile(rr[:], ssm_ps[:], 1.0 / D, R=[b_ssm], W=[brr])
                    h2 = h2_r.t[0]
                    for c in range(8):
                        TT(P.dve if c % 2 == 0 else P.pool, h2[:, c, :], xt[:, c, :], rr[:], ALU.mult, R=[bx[c], brr], W=[h2_cb[c]])
                    qm = qm_r.t[0]
                    for hm in range(4):
                        pb, bpb = pbank.next()
                        for c in range(8):
                            MM(pb[:], w_q[:, c, hm * 128:(hm + 1) * 128], h2[:, c, :], start=(c == 0), stop=(c == 7), R=[h2_cb[c], b_w], W=[bpb])
                        P.op(P.act, lambda e, o=qm[:, hm, :], i_=pb[:]: e.copy(out=o, in_=i_), reads=[bpb], writes=[qm_cb[hm]])
                    if nxt is not None:
                        c_stage_y1(nxt)
                    om = om_r.t[0]
                    sts = {}

                    def c_s(hm):
                        st_, bst_ = s_r.next()
                        for kt_ in range(2):
                            MM(st_[:, kt_ * TS:(kt_ + 1) * TS], km[:, hm, kt_ * 128:(kt_ + 1) * 128], qm[:, hm, :], R=[b_km, qm_cb[hm]], W=[bst_])
                        sts[hm] = (st_, bst_)

                    c_s(0)
                    c_s(1)
                    for hm in range(4):
                        st_, bst_ = sts[hm]
                        pm, bpm = pm_r.next()
                        ACT(pm[:], st_[:], AF.Exp, R=[bst_], W=[bpm], scale=mscale)
                        if hm + 2 < 4:
                            c_s(hm + 2)
                        po, bpo = pbank.next()
                        for kt_ in range(2):
                            MM(po[:], vm[:, kt_, hm * 128:(hm + 1) * 128], pm[:, kt_ * TS:(kt_ + 1) * TS], start=(kt_ == 0), stop=(kt_ == 1), R=[b_vm, bpm], W=[bpo])
                        pd, bpd = pbank.next()
                        for kt_ in range(2):
                            MM(pd[:], ones_bf[:], pm[:, kt_ * TS:(kt_ + 1) * TS], start=(kt_ == 0), stop=(kt_ == 1), R=[b_ones, bpm], W=[bpd])
                        rd, brd = tf.next()
                        RCP(rd[:], pd[:], R=[bpd], W=[brd])
                        TT(P.dve, om[:, hm, :], po[:], rd[:], ALU.mult, R=[bpo, brd], W=[om_cb[hm]])
                    if nxt is not None:
                        c_stage_y2(nxt)
                    for nb in range(8):
                        pb, bpb = pbank.next()
                        for c in range(4):
                            MM(pb[:], w_o[:, c, nb * 128:(nb + 1) * 128], om[:, c, :], start=(c == 0), stop=(c == 3), R=[om_cb[c], b_w], W=[bpb])
                        TT(P.dve, xt[:, nb, :], pb[:], xt[:, nb, :], ALU.add, R=[bpb], W=[bx[nb]])
                    ST(xB[:, :, t0:t0 + TS].rearrange("c p t -> p c t"), xt[:], R=bx)

                cur = c_stage0(0)
                c_stage_y1(cur)
                c_stage_y2(cur)
                for i in range(NT):
                    nxt = c_stage0(i + 1) if i + 1 < NT else None
                    c_body(cur, nxt)
                    cur = nxt
                P.barrier()
            keep_consts()
            if stop_after == "c%d" % l:
                P.emit()
                return nc

            NF = 11
            TW = 510
            NTD = (S + TW - 1) // TW
            for half in range(2):
                pes, sb, ps = phase_scope()
                with pes:
                    w_up = sb("w_up", [128, 8, 2 * NF * 128], BF16)
                    w_dn = sb("w_dn", [128, NF, D], BF16)
                    b_w = P.buf("weights")
                    wes = ExitStack()
                    with wes:
                        stage = Rot(P, lambda n, sh, dt: wes.enter_context(nc.sbuf_tensor(U(n), list(sh), dt)), 3, "stg", [128, 2048], F32)
                        f0 = half * NF * 128
                        for c in range(8):
                            g = smallcol(l, O_FFN + c)
                            load_weight(lambda a, b, c=c: w_up[:, c, a:b], w_up_d[l, c * 128:(c + 1) * 128, f0:f0 + NF * 128], 128, NF * 128, stage, g)
                            load_weight(lambda a, b, c=c: w_up[:, c, NF * 128 + a:NF * 128 + b], w_up_d[l, c * 128:(c + 1) * 128, D_FF + f0:D_FF + f0 + NF * 128], 128, NF * 128, stage, g)
                        for f in range(NF):
                            load_weight(lambda a, b, f=f: w_dn[:, f, a:b], w_dn_d[l, f0 + f * 128:f0 + (f + 1) * 128, :], 128, D, stage, None)
                        P.barrier()
                    keep_consts()
                    P.bufs.append(b_w)
                    xt_r = Rot(P, sb, 2, "xt", [128, 8, TS], F32)
                    ac_r = Rot(P, sb, 2, "ac", [128, 8, TS], F32)
                    sq_r = Rot(P, sb, 1, "sq", [128, 8, TS], BF16)
                    h_r = Rot(P, sb, 2, "h", [128, 8, TS], BF16)
                    g_r = Rot(P, sb, 1, "g", [128, NF, TS], BF16)
                    tf = Rot(P, sb, 8, "tf", [128, TS], F32)
                    rr_r = Rot(P, sb, 2, "rr", [128, TS], F32)
                    pbank = Rot(P, ps, 8, "pb", [128, 512], F32)
                    xt_cb = [P.bufs_n(8, "xtc") for _ in range(2)]
                    ac_cb = [P.bufs_n(8, "acc") for _ in range(2)]
                    h_cb = [P.bufs_n(8, "hc") for _ in range(2)]
                    g_cb = P.bufs_n(NF, "gc")
                    g = g_r.t[0]

                    def stage0(i):
                        st = {}
                        t0 = i * TW
                        a_lo = t0 - 1
                        n_out = min(TW, S - t0)
                        W_ = n_out + 2
                        lo_tok = max(a_lo, 0)
                        hi_tok = min(a_lo + W_, S)
                        c_lo = lo_tok - a_lo
                        c_hi = hi_tok - a_lo
                        k = i % 2
                        xt, bx = xt_r.t[k], xt_cb[k]
                        if c_lo > 0:
                            MSET(P.pool, xt[:, :, 0:c_lo], 0.0, W=bx)
                        if c_hi < W_:
                            MSET(P.pool, xt[:, :, c_hi:W_], 0.0, W=bx)
                        LD(xt[:, :, c_lo:c_hi], xB[:, :, lo_tok:hi_tok].rearrange("c p t -> p c t"), W=bx)
                        ac, bac = ac_r.t[k], ac_cb[k]
                        if half == 1:
                            LD(ac[:, :, 1:1 + n_out], xC[:, :, t0:t0 + n_out].rearrange("c p t -> p c t"), W=bac)
                        st.update(t0=t0, n_out=n_out, W_=W_, xt=xt, bx=bx, ac=ac, bac=bac, h=h_r.t[k], bh=h_cb[k])
                        return st

                    def stage1(st):
                        W_ = st["W_"]
                        sq, bsq = sq_r.next()
                        ACT(sq[:, :, 0:W_], st["xt"][:, :, 0:W_], AF.Square, R=st["bx"], W=[bsq])
                        st["sq"], st["bsq"] = sq, bsq

                    def stage2(st):
                        W_ = st["W_"]
                        sq, bsq = st["sq"], st["bsq"]
                        pb, bpb = pbank.next()
                        for c in range(8):
                            MM(pb[:, 0:W_], ones_bf[:], sq[:, c, 0:W_], start=(c == 0), stop=(c == 7), R=[bsq, b_ones], W=[bpb])
                        rr, brr = rr_r.next()
                        rsqrt_tile(rr[:, 0:W_], pb[:, 0:W_], 1.0 / D, R=[bpb], W=[brr])
                        for c in range(8):
                            TT(P.dve if c % 2 == 0 else P.pool, st["h"][:, c, 0:W_], st["xt"][:, c, 0:W_], rr[:, 0:W_], ALU.mult,
                               R=[st["bx"][c], brr], W=[st["bh"][c]])

                    def body(st, nxt):
                        n_out, W_, h, bh, ac, bac, t0 = st["n_out"], st["W_"], st["h"], st["bh"], st["ac"], st["bac"], st["t0"]
                        for f in range(NF):
                            if nxt is not None and f == 4:
                                stage1(nxt)
                            if nxt is not None and f == 7:
                                stage2(nxt)
                            fg = half * NF + f
                            res = []
                            for part in range(2):
                                pb, bpb = pbank.next()
                                for c in range(8):
                                    MM(pb[:, 0:W_], w_up[:, c, (part * NF + f) * 128:(part * NF + f + 1) * 128], h[:, c, 0:W_],
                                       start=(c == 0), stop=(c == 7), R=[bh[c], b_w], W=[bpb])
                                blk = fg + part * 22
                                cw = O_CW + blk * 3
                                t_, bt_ = tf.next()
                                ACT(t_[:, 0:n_out], pb[:, 0:n_out], AF.Identity, R=[bpb, b_small], W=[bt_],
                                    scale=smallcol(l, cw), bias=smallcol(l, O_CB + blk))
                                STT(P.dve, t_[:, 0:n_out], pb[:, 1:1 + n_out], smallcol(l, cw + 1), t_[:, 0:n_out], ALU.mult, ALU.add, R=[bpb, b_small], W=[bt_])
                                STT(P.dve, t_[:, 0:n_out], pb[:, 2:2 + n_out], smallcol(l, cw + 2), t_[:, 0:n_out], ALU.mult, ALU.add, R=[bpb, b_small], W=[bt_])
                                res.append((t_, bt_))
                            ACT(res[0][0][:, 0:n_out], res[0][0][:, 0:n_out], AF.Silu, R=[res[0][1]], W=[res[0][1]])
                            TT(P.pool, g[:, f, 0:n_out], res[0][0][:, 0:n_out], res[1][0][:, 0:n_out], ALU.mult, R=[res[0][1], res[1][1]], W=[g_cb[f]])
                        for nb in range(8):
                            pb, bpb = pbank.next()
                            for f in range(NF):
                                MM(pb[:, 0:n_out], w_dn[:, f, nb * 128:(nb + 1) * 128], g[:, f, 0:n_out], start=(f == 0), stop=(f == NF - 1), R=[g_cb[f], b_w], W=[bpb])
                            if half == 0:
                                TT(P.dve, ac[:, nb, 1:1 + n_out], pb[:, 0:n_out], st["xt"][:, nb, 1:1 + n_out], ALU.add, R=[bpb, st["bx"][nb]], W=[bac[nb]])
                            else:
                                TT(P.dve, ac[:, nb, 1:1 + n_out], pb[:, 0:n_out], ac[:, nb, 1:1 + n_out], ALU.add, R=[bpb], W=[bac[nb]])
                        dstT = xC if half == 0 else xA
                        ST(dstT[:, :, t0:t0 + n_out].rearrange("c p t -> p c t"), ac[:, :, 1:1 + n_out], R=bac)

                    cur = stage0(0)
                    stage1(cur)
                    stage2(cur)
                    for i in range(NTD):
                        nxt = stage0(i + 1) if i + 1 < NTD else None
                        body(cur, nxt)
                        cur = nxt
                    P.barrier()
                keep_consts()
            if stop_after == "d%d" % l:
                P.emit()
                return nc

        pes, sb, ps = phase_scope()
        with pes:
            xt_r = Rot(P, sb, 2, "xt", [128, 8, TS], F32)
            sq_r = Rot(P, sb, 1, "sq", [128, 8, TS], BF16)
            xn_r = Rot(P, sb, 2, "xn", [128, 8, TS], F32)
            yo_r = Rot(P, sb, 2, "yo", [128, 4, D], F32)
            tf = Rot(P, sb, 2, "tf", [128, TS], F32)
            pbank = Rot(P, ps, 6, "pb", [128, 512], F32)
            for i in range(NT):
                t0 = i * TS
                xt, bxt = xt_r.next()
                LD(xt[:], xA[:, :, t0:t0 + TS].rearrange("c p t -> p c t"), W=[bxt])
                sq, bsq = sq_r.next()
                ACT(sq[:], xt[:], AF.Square, R=[bxt], W=[bsq])
                pb, bpb = pbank.next()
                for c in range(8):
                    MM(pb[:], ones_bf[:], sq[:, c, :], start=(c == 0), stop=(c == 7), R=[bsq, b_ones], W=[bpb])
                rr, brr = tf.next()
                rsqrt_tile(rr[:], pb[:], 1.0 / D, R=[bpb], W=[brr])
                xn, bxn = xn_r.next()
                for c in range(8):
                    STT(P.dve, xn[:, c, :], xt[:, c, :], fin[:, c:c + 1], rr[:], ALU.mult, ALU.mult, R=[bxt, brr, b_fin], W=[bxn])
                yo, byo = yo_r.next()
                for s_ in range(4):
                    for c4 in range(2):
                        pb, bpb = pbank.next()
                        for cc in range(4):
                            c = c4 * 4 + cc
                            TR(pb[:, cc * 128:(cc + 1) * 128], xn[:, c, s_ * 128:(s_ + 1) * 128], ident[:], R=[bxn, b_ident], W=[bpb])
                        if (s_ + c4) % 2 == 0:
                            CP(P.dve, yo[:, s_, c4 * 512:(c4 + 1) * 512], pb[:], R=[bpb], W=[byo])
                        else:
                            P.op(P.act, lambda e, o=yo[:, s_, c4 * 512:(c4 + 1) * 512], i_=pb[:]: e.copy(out=o, in_=i_), reads=[bpb], writes=[byo])
                ST(y_out[t0:t0 + TS, :].rearrange("(s p) d -> p s d", p=128), yo[:], R=[byo])
            P.barrier()
        P.emit()
    return nc


def _swap_pairs(w):
    idx = np.arange(w.shape[-1]).reshape(-1, 2)[:, ::-1].reshape(-1)
    return w[..., idx]


def _rope_tables():
    rows = S // 64
    row = np.repeat(np.arange(rows, dtype=np.float32), 64)
    col = np.tile(np.arange(64, dtype=np.float32), rows)

    def tab(d_rot):
        n = d_rot // 4
        inv = (np.float32(10000.0) ** (-np.arange(n, dtype=np.float32) / np.float32(n))).astype(np.float32)
        ang = np.concatenate([row[:, None] * inv, col[:, None] * inv], axis=-1).astype(np.float32)
        c = np.cos(ang).astype(np.float32)
        s = np.sin(ang).astype(np.float32)
        cf = np.repeat(c, 2, axis=1)
        sf = np.repeat(s, 2, axis=1)
        sign = np.tile(np.array([-1.0, 1.0], np.float32), d_rot // 2)
        return np.ascontiguousarray(cf.T), np.ascontiguousarray((sf * sign).T)

    ca, sa = tab(32)
    cb, sb_ = tab(64)
    ropeA = np.stack([ca, sa]).astype(np.float32)
    ropeB = np.stack([np.concatenate([cb, cb]), np.concatenate([sb_, sb_])]).astype(np.float32)
    return ropeA, ropeB


def _host_layout(inputs):
    f = lambda k: np.asarray(inputs[k], dtype=np.float32)
    L = DEPTH
    w_in = f("w_in")
    sw_src = np.concatenate([w_in[:, :, 320:416], w_in[:, :, 416:800], w_in[:, :, 800:928]], axis=-1)
    w_in_sw = _swap_pairs(sw_src)
    w_uq = f("mla_w_uq")
    w_uq_sw = _swap_pairs(w_uq)
    w_ukv = f("mla_w_ukv").reshape(L, 128, 6, 2, 64)
    w_ukv_p = np.concatenate([w_ukv[:, :, :, 0, :].reshape(L, 128, 384), w_ukv[:, :, :, 1, :].reshape(L, 128, 384)], axis=-1)
    wsT = np.ascontiguousarray(f("gmlp_w_s").transpose(0, 1, 3, 2))
    pc = lambda v: v.reshape(L, -1, 128).transpose(0, 2, 1)
    gq = f("gqa_q_norm"); gk = f("gqa_k_norm")
    sw64 = np.arange(64).reshape(-1, 2)[:, ::-1].reshape(-1)
    tile2 = lambda v: np.concatenate([v, v], axis=-1)[:, :, None]
    cw = f("ffn_conv_w")
    cwp = cw.reshape(L, 3, 44, 128).transpose(0, 3, 2, 1).reshape(L, 128, 132)
    cb = pc(f("ffn_conv_b"))
    small = np.concatenate([
        pc(f("mix_norm")), pc(f("out_norm")), pc(f("mem_x_norm")), pc(f("mem_kv_norm")), pc(f("ffn_norm")),
        pc(f("mla_q_norm")), pc(f("mla_kv_norm")),
        tile2(gq), tile2(gq[:, sw64]), tile2(gk), tile2(gk[:, sw64]),
        cwp, cb], axis=-1).astype(np.float32)
    ropeA, ropeB = _rope_tables()
    shared = {
        "ident": np.eye(128, dtype=np.float32),
        "ropeA": ropeA, "ropeB": ropeB,
        "w_in": w_in, "w_in_sw": np.ascontiguousarray(w_in_sw),
        "w_uq": w_uq, "w_uq_sw": np.ascontiguousarray(w_uq_sw),
        "w_ukv": np.ascontiguousarray(w_ukv_p),
        "wsT": wsT, "bs": f("gmlp_b_s"), "gvn": f("gmlp_v_norm"),
        "w_out": f("w_out"), "mem_w_q": f("mem_w_q"), "mem_w_kv": f("mem_w_kv"), "mem_w_o": f("mem_w_o"),
        "w_up": f("ffn_w_up"), "w_dn": f("ffn_w_down"),
        "small": np.ascontiguousarray(small),
        "fin": np.ascontiguousarray(f("final_norm").reshape(8, 128).T),
    }
    return shared


_NC_CACHE = {}


def kernel(**inputs):
    shared = _host_layout(inputs)
    x = np.asarray(inputs["x"], dtype=np.float32)
    mem = np.asarray(inputs["mem"], dtype=np.float32)
    if "nc" not in _NC_CACHE:
        _NC_CACHE["nc"] = build()
    nc = _NC_CACHE["nc"]
    in_maps = []
    for c in range(NCORES):
        m = dict(shared)
        m["x"] = np.ascontiguousarray(x[c])
        m["mem"] = np.ascontiguousarray(mem[c])
        in_maps.append(m)
    res = run_bass_kernel_spmd(nc, in_maps, core_ids=list(range(NCORES)))
    return np.stack([res.results[c]["y"] for c in range(NCORES)], axis=0).astype(np.float32)
```

```python
from contextlib import ExitStack
import numpy as np
import concourse.bass as bass
import concourse.mybir as mybir
from concourse.bass_utils import run_bass_kernel_spmd

F32 = mybir.dt.float32
BF16 = mybir.dt.bfloat16
AF = mybir.ActivationFunctionType
ALU = mybir.AluOpType

S = 8192
D = 1024
TS = 512
NT = S // TS
DEPTH = 2
D_IN = 1568
D_FF = 2816
EPS = 1e-6
NCORES = 8


class Buf:
    __slots__ = ("name", "w", "r", "sem", "sem2")

    def __init__(self, name):
        self.name = name
        self.w = []
        self.r = []
        self.sem = None
        self.sem2 = None


def _prune(ts):
    best = {}
    for s, v in ts:
        k = id(s)
        if k not in best or best[k][1] < v:
            best[k] = (s, v)
    return list(best.values())


class Eng:
    def __init__(self, prog, name):
        self.prog = prog
        self.name = name
        self.ops = []
        self.sem = prog.es.enter_context(prog.nc.semaphore("s_" + name))
        self.count = 0
        self.waited = {}
        self.needed = set()
        prog.sem2eng[id(self.sem)] = self

    def wait(self, tickets, skip_own=False):
        for s, v in _prune(tickets):
            if skip_own and s is self.sem:
                continue
            k = id(s)
            if self.waited.get(k, 0) < v:
                self.ops.append(("wait", s, v))
                self.waited[k] = v
                src = self.prog.sem2eng.get(k)
                if src is not None:
                    src.needed.add(v)

    def replay(self, e):
        ranks = {}
        for eng in self.prog.engs:
            ranks[id(eng.sem)] = {v: i + 1 for i, v in enumerate(sorted(eng.needed))}
        mine = ranks[id(self.sem)]
        for ent in self.ops:
            if ent[0] == "wait":
                _, s_, v = ent
                r = ranks.get(id(s_))
                e.wait_ge(s_, r[v] if r is not None else v)
            elif ent[0] == "op":
                ins = ent[1](e)
                if ent[2] in mine:
                    ins.then_inc(self.sem, 1)
            else:
                ent[1](e).then_inc(ent[2], 16)


class Prog:
    def __init__(self, nc, es, n_dma_sems=48):
        self.nc = nc
        self.es = es
        self.sem2eng = {}
        self.pe = Eng(self, "pe")
        self.act = Eng(self, "act")
        self.dve = Eng(self, "dve")
        self.pool = Eng(self, "pool")
        self.sp = Eng(self, "sp")
        self.engs = [self.pe, self.act, self.dve, self.pool, self.sp]
        self.dsems = [[es.enter_context(nc.semaphore("d%d" % i)), 0] for i in range(n_dma_sems)]
        self.free_dsems = list(range(n_dma_sems))
        self.bufs = []

    def buf(self, name="b"):
        b = Buf(name)
        self.bufs.append(b)
        return b

    def bufs_n(self, n, name="b"):
        return [self.buf(name + str(i)) for i in range(n)]

    def _deps(self, reads, writes):
        deps = []
        for b in reads:
            deps += b.w
        for b in writes:
            deps += b.w
            deps += b.r
        return deps

    def op(self, eng, fn, reads=(), writes=()):
        eng.wait(self._deps(reads, writes), skip_own=(eng is self.pe))
        eng.count += 1
        t = (eng.sem, eng.count)
        eng.ops.append(("op", fn, eng.count))
        for b in reads:
            b.r = _prune(b.r + [t])
        for b in writes:
            b.w = [t]
            b.r = []
        return t

    def dma(self, eng, fn, reads=(), writes=()):
        owner = (list(writes) + list(reads))[0]
        if eng is self.pool:
            if owner.sem2 is None:
                owner.sem2 = self.dsems[self.free_dsems.pop()]
            rec = owner.sem2
        else:
            if owner.sem is None:
                owner.sem = self.dsems[self.free_dsems.pop()]
            rec = owner.sem
        eng.wait(self._deps(reads, writes))
        rec[1] += 16
        t = (rec[0], rec[1])
        eng.ops.append(("dma", fn, rec[0]))
        for b in reads:
            b.r = _prune(b.r + [t])
        for b in writes:
            b.w = _prune([x for x in b.w if x[0] is rec[0]] + [t])
            b.r = []
        return t

    def barrier(self):
        ts = [(e.sem, e.count) for e in self.engs if e.count > 0]
        ts += [(r[0], r[1]) for r in self.dsems if r[1] > 0]
        for e in self.engs:
            e.wait(ts)
        for b in self.bufs:
            b.w = []
            b.r = []
            if b.sem is not None:
                self.free_dsems.append(self.dsems.index(b.sem))
                b.sem = None
            if b.sem2 is not None:
                self.free_dsems.append(self.dsems.index(b.sem2))
                b.sem2 = None
        self.bufs = []

    def emit(self):
        with self.nc.Block() as block:
            block.sync(self.sp.replay)
            block.tensor(self.pe.replay)
            block.scalar(self.act.replay)
            block.vector(self.dve.replay)
            block.gpsimd(self.pool.replay)


class Rot:
    def __init__(self, P, alloc, n, name, shape, dt):
        self.t = [alloc("%s%d" % (name, i), shape, dt) for i in range(n)]
        self.b = [P.buf("%s%d" % (name, i)) for i in range(n)]
        self.i = 0

    def next(self):
        k = self.i % len(self.t)
        self.i += 1
        return self.t[k], self.b[k]


def build(debug=False, stop_after=None, taps=()):
    nc = bass.Bass("TRN2", target_bir_lowering=False)
    dram_in = lambda n, sh, dt=F32: nc.dram_tensor(n, list(sh), dt, kind="ExternalInput").ap()
    dram_sc = lambda n, sh, dt=F32: nc.dram_tensor(n, list(sh), dt, kind=("ExternalOutput" if n in taps else "Internal")).ap()

    x_in = dram_in("x", [S, D])
    mem_in = dram_in("mem", [256, D])
    ident_in = dram_in("ident", [128, 128])
    ropeA_in = dram_in("ropeA", [2, 32, S])
    ropeB_in = dram_in("ropeB", [2, 128, S])
    w_in_d = dram_in("w_in", [DEPTH, D, D_IN])
    w_in_sw_d = dram_in("w_in_sw", [DEPTH, D, 608])
    w_uq_d = dram_in("w_uq", [DEPTH, 256, 576])
    w_uq_sw_d = dram_in("w_uq_sw", [DEPTH, 256, 576])
    w_ukv_d = dram_in("w_ukv", [DEPTH, 128, 768])
    wsT_d = dram_in("wsT", [DEPTH, 4, 128, 128])
    bs_d = dram_in("bs", [DEPTH, 4, 128])
    gvn_d = dram_in("gvn", [DEPTH, 256])
    w_out_d = dram_in("w_out", [DEPTH, D, D])
    mem_w_q_d = dram_in("mem_w_q", [DEPTH, D, 512])
    mem_w_kv_d = dram_in("mem_w_kv", [DEPTH, D, 1024])
    mem_w_o_d = dram_in("mem_w_o", [DEPTH, 512, D])
    w_up_d = dram_in("w_up", [DEPTH, D, 2 * D_FF])
    w_dn_d = dram_in("w_dn", [DEPTH, D_FF, D])
    NSM = 223
    small_d = dram_in("small", [DEPTH, 128, NSM])
    fin_d = dram_in("fin", [128, 8])
    y_out = nc.dram_tensor("y", [S, D], F32, kind="ExternalOutput").ap()

    xA = dram_sc("xA", [8, 128, S])
    xB = dram_sc("xB", [8, 128, S])
    xC = dram_sc("xC", [8, 128, S])
    QA = dram_sc("QA", [6, 96, S], BF16)
    KN = dram_sc("KN", [3, 128, S], BF16)
    KPE = dram_sc("KPE", [32, S], BF16)
    VA = dram_sc("VA", [128, 64, 390], BF16)
    QB = dram_sc("QB", [3, 128, S], BF16)
    KB = dram_sc("KB", [128, S], BF16)
    VB = dram_sc("VB", [128, 64, 130], BF16)
    YT = dram_sc("YT", [8, 128, S])
    DEN = dram_sc("DEN", [4, TS])

    O_MIX, O_OUTN, O_MEMX, O_MEMKV, O_FFN = 0, 8, 16, 24, 32
    O_QN, O_KVN = 40, 42
    O_GQ1, O_GQ2, O_GK1, O_GK2 = 43, 44, 45, 46
    O_CW, O_CB = 47, 47 + 132

    with ExitStack() as es:
        P = Prog(nc, es)
        uid = [0]

        def U(n):
            uid[0] += 1
            return "%s_u%d" % (n, uid[0])

        SB = lambda n, sh, dt: es.enter_context(nc.sbuf_tensor(U(n), list(sh), dt))

        def MM(out, lhsT, rhs, start=True, stop=True, R=(), W=()):
            return P.op(P.pe, lambda e: e.matmul(out, lhsT=lhsT, rhs=rhs, start=start, stop=stop), reads=R, writes=W)

        def TR(out, in_, ident, R=(), W=()):
            return P.op(P.pe, lambda e: e.transpose(out, in_, ident), reads=R, writes=W)

        def ACT(out, in_, func, R=(), W=(), **kw):
            return P.op(P.act, lambda e: e.activation(out=out, in_=in_, func=func, **kw), reads=R, writes=W)

        def TS_(eng, out, in0, s1, s2, op0, op1=None, R=(), W=()):
            if op1 is None:
                return P.op(eng, lambda e: e.tensor_scalar(out=out, in0=in0, scalar1=s1, scalar2=None, op0=op0), reads=R, writes=W)
            return P.op(eng, lambda e: e.tensor_scalar(out=out, in0=in0, scalar1=s1, scalar2=s2, op0=op0, op1=op1), reads=R, writes=W)

        def STT(eng, out, in0, scalar, in1, op0, op1, R=(), W=()):
            return P.op(eng, lambda e: e.scalar_tensor_tensor(out=out, in0=in0, scalar=scalar, in1=in1, op0=op0, op1=op1), reads=R, writes=W)

        def TT(eng, out, in0, in1, op, R=(), W=()):
            return P.op(eng, lambda e: e.tensor_tensor(out=out, in0=in0, in1=in1, op=op), reads=R, writes=W)

        def CP(eng, out, in_, R=(), W=()):
            return P.op(eng, lambda e: e.tensor_copy(out=out, in_=in_), reads=R, writes=W)

        def RCP(out, in_, R=(), W=()):
            return P.op(P.dve, lambda e: e.reciprocal(out=out, in_=in_), reads=R, writes=W)

        def MSET(eng, ap, val, W=()):
            return P.op(eng, lambda e: e.memset(ap, val), writes=W)

        def LD(out, in_, W, R=()):
            return P.dma(P.sp, lambda e: e.dma_start(out=out, in_=in_), reads=R, writes=W)

        def ST(out, in_, R):
            return P.dma(P.pool, lambda e: e.dma_start(out=out, in_=in_), reads=R)

        ident = SB("ident", [128, 128], F32); b_ident = P.buf()
        ones_bf = SB("ones_bf", [128, 128], BF16); b_ones = P.buf()
        blk_bf = SB("blk_bf", [128, 128], BF16); b_blk = P.buf()
        ones_f = SB("ones_f", [128, 64], F32); b_onesf = P.buf()
        small = SB("small", [128, DEPTH, NSM], F32); b_small = P.buf()
        fin = SB("fin", [128, 8], F32); b_fin = P.buf()
        LD(ident[:], ident_in[:, :], W=[b_ident])
        LD(small[:], small_d.rearrange("l p n -> p l n"), W=[b_small])
        LD(fin[:], fin_d[:, :], W=[b_fin])
        MSET(P.dve, ones_bf[:], 1.0, W=[b_ones])
        MSET(P.dve, blk_bf[:], 0.0, W=[b_blk])
        MSET(P.dve, blk_bf[0:64, 0:64], 1.0, W=[b_blk])
        MSET(P.dve, blk_bf[64:128, 64:128], 1.0, W=[b_blk])
        MSET(P.dve, ones_f[:], 1.0, W=[b_onesf])
        constbufs = [b_ident, b_ones, b_blk, b_onesf, b_small, b_fin]

        def keep_consts():
            P.bufs.extend(constbufs)

        def smallcol(l, off, n=1):
            return small[:, l, off:off + n]

        def phase_scope():
            pes = ExitStack()
            sb = lambda n, sh, dt: pes.enter_context(nc.sbuf_tensor(U(n), list(sh), dt))
            ps = lambda n, sh, dt: pes.enter_context(nc.psum_tensor(U(n), list(sh), dt))
            return pes, sb, ps

        def rsqrt_tile(dst, src_ps, inv_n, R, W):
            ACT(dst, src_ps, AF.Sqrt, R=R, W=W, scale=inv_n, bias=EPS)
            RCP(dst, dst, R=W, W=W)

        def load_weight(dst_bf, src_dram, rows, cols, stage, gain_ap=None, col_chunk=2048):
            for c0 in range(0, cols, col_chunk):
                c1 = min(cols, c0 + col_chunk)
                st, bst = stage.next()
                LD(st[0:rows, 0:c1 - c0], src_dram[:, c0:c1], W=[bst])
                if gain_ap is None:
                    CP(P.dve, dst_bf(c0, c1), st[0:rows, 0:c1 - c0], R=[bst], W=[])
                else:
                    TS_(P.dve, dst_bf(c0, c1), st[0:rows, 0:c1 - c0], gain_ap, None, ALU.mult, R=[bst, b_small], W=[])

        memT = SB("memT", [128, 8, 256], F32); b_memT = P.buf()
        constbufs.append(b_memT)
        pes, sb, ps = phase_scope()
        with pes:
            xin = Rot(P, sb, 2, "xin", [128, 4, D], F32)
            xo = Rot(P, sb, 2, "xo", [128, 8, TS], F32)
            pbank = Rot(P, ps, 4, "pb", [128, 512], F32)
            mt, bmt = xin.next()
            LD(mt[:, 0:2, :], mem_in.rearrange("(s p) d -> p s d", p=128), W=[bmt])
            for c in range(8):
                pb, bpb = pbank.next()
                for s_ in range(2):
                    TR(pb[:, s_ * 128:(s_ + 1) * 128], mt[:, s_, c * 128:(c + 1) * 128], ident[:], R=[bmt, b_ident], W=[bpb])
                CP(P.dve, memT[:, c, :], pb[:, 0:256], R=[bpb], W=[b_memT])
            for i in range(NT):
                xt, bxt = xin.next()
                LD(xt[:], x_in[i * TS:(i + 1) * TS, :].rearrange("(s p) d -> p s d", p=128), W=[bxt])
                xo_t, bxo = xo.next()
                for c in range(8):
                    pb, bpb = pbank.next()
                    for s_ in range(4):
                        TR(pb[:, s_ * 128:(s_ + 1) * 128], xt[:, s_, c * 128:(c + 1) * 128], ident[:], R=[bxt, b_ident], W=[bpb])
                    if c % 2 == 0:
                        CP(P.dve, xo_t[:, c, :], pb[:], R=[bpb], W=[bxo])
                    else:
                        P.op(P.act, lambda e, o=xo_t[:, c, :], i_=pb[:]: e.copy(out=o, in_=i_), reads=[bpb], writes=[bxo])
                ST(xA[:, :, i * TS:(i + 1) * TS].rearrange("c p t -> p c t"), xo_t[:], R=[bxo])
            P.barrier()
        keep_consts()
        if stop_after == "p0":
            P.emit()
            return nc

        for l in range(DEPTH):
            pes, sb, ps = phase_scope()
            with pes:
                w_in = sb("w_in", [128, 8, D_IN], BF16)
                w_sw = sb("w_sw", [128, 8, 608], BF16)
                w_uq = sb("w_uq", [128, 2, 576], BF16)
                w_uqs = sb("w_uqs", [128, 2, 576], BF16)
                w_ukv = sb("w_ukv", [128, 768], BF16)
                wsT = sb("wsT", [128, 4, 128], BF16)
                gvn = sb("gvn", [128, 256], F32); b_gvn = P.buf()
                biasT = sb("biasT", [128, 2, 128], F32); b_biasT = P.buf()
                b_w = P.buf("weights")
                wes = ExitStack()
                with wes:
                    stage = Rot(P, lambda n, sh, dt: wes.enter_context(nc.sbuf_tensor(U(n), list(sh), dt)), 3, "stg", [128, 2048], F32)
                    for c in range(8):
                        g = smallcol(l, O_MIX + c)
                        load_weight(lambda a, b, c=c: w_in[:, c, a:b], w_in_d[l, c * 128:(c + 1) * 128, :], 128, D_IN, stage, g)
                        load_weight(lambda a, b, c=c: w_sw[:, c, a:b], w_in_sw_d[l, c * 128:(c + 1) * 128, :], 128, 608, stage, g)
                    for c in range(2):
                        g = smallcol(l, O_QN + c)
                        load_weight(lambda a, b, c=c: w_uq[:, c, a:b], w_uq_d[l, c * 128:(c + 1) * 128, :], 128, 576, stage, g)
                        load_weight(lambda a, b, c=c: w_uqs[:, c, a:b], w_uq_sw_d[l, c * 128:(c + 1) * 128, :], 128, 576, stage, g)
                    load_weight(lambda a, b: w_ukv[:, a:b], w_ukv_d[l, :, :], 128, 768, stage, smallcol(l, O_KVN))
                    for g_ in range(4):
                        load_weight(lambda a, b, g_=g_: wsT[:, g_, a:b], wsT_d[l, g_, :, :], 128, 128, stage, None)
                    LD(gvn[:], gvn_d[l:l + 1, :].broadcast_to([128, 256]), W=[b_gvn])
                    for g_ in range(4):
                        k_, half = g_ // 2, g_ % 2
                        LD(biasT[half * 64:(half + 1) * 64, k_, :], bs_d[l, g_:g_ + 1, :].broadcast_to([64, 128]), W=[b_biasT])
                    P.barrier()
                keep_consts()
                P.bufs.extend([b_w, b_gvn, b_biasT])

                xt_r = Rot(P, sb, 2, "xt", [128, 8, TS], F32)
                sq_r = Rot(P, sb, 1, "sq", [128, 8, TS], BF16)
                h_r = Rot(P, sb, 2, "h", [128, 8, TS], BF16)
                tabA_r = Rot(P, sb, 2, "tabA", [96, 2, TS], F32)
                tabB_r = Rot(P, sb, 2, "tabB", [128, 2, TS], F32)
                tf = Rot(P, sb, 8, "tf", [128, TS], F32)
                tb = Rot(P, sb, 4, "tb", [128, TS], BF16)
                cqn_r = Rot(P, sb, 1, "cqn", [128, 2, TS], BF16)
                qa_r = Rot(P, sb, 2, "qa", [96, 6, TS], BF16)
                kn_r = Rot(P, sb, 2, "kn", [128, 3, TS], BF16)
                kpe_r = Rot(P, sb, 2, "kpe", [96, TS], BF16)
                va_r = Rot(P, sb, 2, "va", [128, 4, 390], BF16)
                qb_r = Rot(P, sb, 2, "qb", [128, 3, TS], BF16)
                kb_r = Rot(P, sb, 2, "kb", [128, TS], BF16)
                vb_r = Rot(P, sb, 2, "vb", [128, 4, 130], BF16)
                u_r = Rot(P, sb, 1, "u", [128, 2, TS], F32)
                yc_r = Rot(P, sb, 2, "yc", [128, 2, TS], F32)
                vvg_r = Rot(P, sb, 4, "vvg", [128, 256], F32)
                vvn_r = Rot(P, sb, 4, "vvn", [128, 256], BF16)
                sc_r = Rot(P, sb, 2, "sc", [128, 8], F32)
                pbank = Rot(P, ps, 8, "pb", [128, 512], F32)
                for k in range(2):
                    for s_ in range(4):
                        MSET(P.pool, va_r.t[k][:, s_, :], 1.0, W=[va_r.b[k]])
                        MSET(P.pool, vb_r.t[k][:, s_, :], 1.0, W=[vb_r.b[k]])

                h_cb = [P.bufs_n(8, "hc") for _ in range(2)]

                def a_stage0(i):
                    st = {"t0": i * TS}
                    st["xt"], st["bxt"] = xt_r.next()
                    LD(st["xt"][:], xA[:, :, i * TS:(i + 1) * TS].rearrange("c p t -> p c t"), W=[st["bxt"]])
                    st["tabA"], st["btA"] = tabA_r.next()
                    LD(st["tabA"][64:96, :, :], ropeA_in[:, :, i * TS:(i + 1) * TS].rearrange("k p t -> p k t"), W=[st["btA"]])
                    st["tabB"], st["btB"] = tabB_r.next()
                    LD(st["tabB"][:], ropeB_in[:, :, i * TS:(i + 1) * TS].rearrange("k p t -> p k t"), W=[st["btB"]])
                    st["h"], st["bh"] = h_r.t[i % 2], h_cb[i % 2]
                    return st

                def a_stage1(st):
                    st["sq"], st["bsq"] = sq_r.next()
                    ACT(st["sq"][:], st["xt"][:], AF.Square, R=[st["bxt"]], W=[st["bsq"]])

                def a_stage2(st):
                    pb, bpb = pbank.next()
                    for c in range(8):
                        MM(pb[:], ones_bf[:], st["sq"][:, c, :], start=(c == 0), stop=(c == 7), R=[st["bsq"], b_ones], W=[bpb])
                    r0, br0 = tf.next()
                    rsqrt_tile(r0[:], pb[:], 1.0 / D, R=[bpb], W=[br0])
                    for c in range(8):
                        TT(P.dve if c % 2 == 0 else P.pool, st["h"][:, c, :], st["xt"][:, c, :], r0[:], ALU.mult, R=[st["bxt"], br0], W=[st["bh"][c]])

                a_cur = a_stage0(0)
                a_stage1(a_cur)
                a_stage2(a_cur)
                for i in range(NT):
                    t0 = i * TS
                    a_nxt = a_stage0(i + 1) if i + 1 < NT else None
                    tabA, btA, tabB, btB, h, bh = a_cur["tabA"], a_cur["btA"], a_cur["tabB"], a_cur["btB"], a_cur["h"], a_cur["bh"]

                    def proj(wt, col0, ncol):
                        pb_, bpb_ = pbank.next()
                        for c in range(8):
                            MM(pb_[0:ncol, :], wt[:, c, col0:col0 + ncol], h[:, c, :], start=(c == 0), stop=(c == 7), R=[bh[c], b_w], W=[bpb_])
                        return pb_, bpb_

                    cq = [proj(w_in, 0, 128), proj(w_in, 128, 128)]
                    sqc, bsqc = [], []
                    for k in range(2):
                        t_, b_ = tb.next()
                        ACT(t_[:], cq[k][0][:], AF.Square, R=[cq[k][1]], W=[b_])
                        sqc.append(t_); bsqc.append(b_)
                    pss, bpss = pbank.next()
                    for k in range(2):
                        MM(pss[:], ones_bf[:], sqc[k][:], start=(k == 0), stop=(k == 1), R=[bsqc[k], b_ones], W=[bpss])
                    rq, brq = tf.next()
                    rsqrt_tile(rq[:], pss[:], 1.0 / 256, R=[bpss], W=[brq])
                    cqn, bcqn = cqn_r.next()
                    for k in range(2):
                        TT(P.dve, cqn[:, k, :], cq[k][0][:], rq[:], ALU.mult, R=[cq[k][1], brq], W=[bcqn])
                    qa, bqa = qa_r.next()
                    for hh in range(6):
                        pq, bpq = pbank.next()
                        pqs, bpqs = pbank.next()
                        for k in range(2):
                            MM(pq[0:96, :], w_uq[:, k, hh * 96:(hh + 1) * 96], cqn[:, k, :], start=(k == 0), stop=(k == 1), R=[bcqn, b_w], W=[bpq])
                        for k in range(2):
                            MM(pqs[0:96, :], w_uqs[:, k, hh * 96:(hh + 1) * 96], cqn[:, k, :], start=(k == 0), stop=(k == 1), R=[bcqn, b_w], W=[bpqs])
                        P.op(P.act, lambda e, o=qa[0:64, hh, :], i_=pq[0:64, :]: e.copy(out=o, in_=i_), reads=[bpq], writes=[bqa])
                        t1, bt1 = tf.next()
                        t2, bt2 = tf.next()
                        TT(P.dve, t1[64:96, :], pq[64:96, :], tabA[64:96, 0, :], ALU.mult, R=[bpq, btA], W=[bt1])
                        TT(P.dve, t2[64:96, :], pqs[64:96, :], tabA[64:96, 1, :], ALU.mult, R=[bpqs, btA], W=[bt2])
                        TT(P.pool, qa[64:96, hh, :], t1[64:96, :], t2[64:96, :], ALU.add, R=[bt1, bt2], W=[bqa])
                    ST(QA[:, :, t0:t0 + TS].rearrange("h p t -> p h t"), qa[:], R=[bqa])
                    if a_nxt is not None:
                        a_stage1(a_nxt)

                    ckv = proj(w_in, 256, 128)
                    kr = proj(w_in, 320, 96)
                    krs = proj(w_sw, 0, 96)
                    t_, b_ = tb.next()
                    ACT(t_[:], ckv[0][:], AF.Square, R=[ckv[1]], W=[b_])
                    pss, bpss = pbank.next()
                    MM(pss[:], ones_bf[:], t_[:], R=[b_, b_ones], W=[bpss])
                    rk, brk = tf.next()
                    rsqrt_tile(rk[:], pss[:], 1.0 / 128, R=[bpss], W=[brk])
                    ckvn, bckvn = tb.next()
                    TT(P.dve, ckvn[:], ckv[0][:], rk[:], ALU.mult, R=[ckv[1], brk], W=[bckvn])
                    kn, bkn = kn_r.next()
                    for b3 in range(3):
                        pk, bpk = pbank.next()
                        MM(pk[:], w_ukv[:, b3 * 128:(b3 + 1) * 128], ckvn[:], R=[bckvn, b_w], W=[bpk])
                        if b3 % 2 == 0:
                            P.op(P.act, lambda e, o=kn[:, b3, :], i_=pk[:]: e.copy(out=o, in_=i_), reads=[bpk], writes=[bkn])
                        else:
                            CP(P.dve, kn[:, b3, :], pk[:], R=[bpk], W=[bkn])
                    ST(KN[:, :, t0:t0 + TS].rearrange("c p t -> p c t"), kn[:], R=[bkn])
                    kpe, bkpe = kpe_r.next()
                    t1, bt1 = tf.next()
                    t2, bt2 = tf.next()
                    TT(P.dve, t1[64:96, :], kr[0][64:96, :], tabA[64:96, 0, :], ALU.mult, R=[kr[1], btA], W=[bt1])
                    TT(P.dve, t2[64:96, :], krs[0][64:96, :], tabA[64:96, 1, :], ALU.mult, R=[krs[1], btA], W=[bt2])
                    TT(P.pool, kpe[64:96, :], t1[64:96, :], t2[64:96, :], ALU.add, R=[bt1, bt2], W=[bkpe])
                    ST(KPE[:, t0:t0 + TS], kpe[64:96, :], R=[bkpe])
                    va, bva = va_r.next()
                    for s_ in range(4):
                        pv, bpv = pbank.next()
                        MM(pv[:, 0:384], ckvn[:, s_ * 128:(s_ + 1) * 128], w_ukv[:, 384:768], R=[bckvn, b_w], W=[bpv])
                        dst = va[:, s_, :].rearrange("p (h c) -> p h c", c=65)[:, :, 0:64]
                        src = pv[:, 0:384].rearrange("p (h c) -> p h c", c=64)
                        if s_ % 2 == 0:
                            CP(P.dve, dst, src, R=[bpv], W=[bva])
                        else:
                            P.op(P.act, lambda e, o=dst, i_=src: e.copy(out=o, in_=i_), reads=[bpv], writes=[bva])
                    ST(VA[:, 4 * i:4 * i + 4, :], va[:], R=[bva])
                    if a_nxt is not None:
                        a_stage2(a_nxt)

                    qb, bqb = qb_r.next()
                    kb, bkb = kb_r.next()
                    for cc in range(4):
                        if cc < 3:
                            raw = proj(w_in, 416 + 128 * cc, 128)
                            swp = proj(w_sw, 96 + 128 * cc, 128)
                            og1, og2 = O_GQ1, O_GQ2
                            dst, bdst = qb[:, cc, :], bqb
                        else:
                            raw = proj(w_in, 800, 128)
                            swp = proj(w_sw, 96 + 384, 128)
                            og1, og2 = O_GK1, O_GK2
                            dst, bdst = kb[:], bkb
                        t_, b_ = tb.next()
                        ACT(t_[:], raw[0][:], AF.Square, R=[raw[1]], W=[b_])
                        pss, bpss = pbank.next()
                        MM(pss[:], blk_bf[:], t_[:], R=[b_, b_blk], W=[bpss])
                        rr, brr = tf.next()
                        rsqrt_tile(rr[:], pss[:], 1.0 / 64, R=[bpss], W=[brr])
                        t1, bt1 = tf.next()
                        t2, bt2 = tf.next()
                        STT(P.dve, t1[:], raw[0][:], smallcol(l, og1), tabB[:, 0, :], ALU.mult, ALU.mult, R=[raw[1], btB, b_small], W=[bt1])
                        STT(P.dve, t2[:], swp[0][:], smallcol(l, og2), tabB[:, 1, :], ALU.mult, ALU.mult, R=[swp[1], btB, b_small], W=[bt2])
                        TT(P.pool, t1[:], t1[:], t2[:], ALU.add, R=[bt2], W=[bt1])
                        TT(P.pool, dst, t1[:], rr[:], ALU.mult, R=[bt1, brr], W=[bdst])
                    ST(QB[:, :, t0:t0 + TS].rearrange("c p t -> p c t"), qb[:], R=[bqb])
                    ST(KB[:, t0:t0 + TS], kb[:], R=[bkb])
                    vb, bvb = vb_r.next()
                    for s_ in range(4):
                        pv, bpv = pbank.next()
                        for c in range(8):
                            MM(pv[:, 0:128], h[:, c, s_ * 128:(s_ + 1) * 128], w_in[:, c, 928:1056], start=(c == 0), stop=(c == 7), R=[bh[c], b_w], W=[bpv])
                        dst = vb[:, s_, :].rearrange("p (h c) -> p h c", c=65)[:, :, 0:64]
                        src = pv[:, 0:128].rearrange("p (h c) -> p h c", c=64)
                        P.op(P.act, lambda e, o=dst, i_=src: e.copy(out=o, in_=i_), reads=[bpv], writes=[bvb])
                    ST(VB[:, 4 * i:4 * i + 4, :], vb[:], R=[bvb])

                    u, bu = u_r.next()
                    for k in range(2):
                        pu = proj(w_in, 1056 + 128 * k, 128)
                        ACT(u[:, k, :], pu[0][:], AF.Gelu_apprx_tanh, R=[pu[1]], W=[bu])
                    yc, byc = yc_r.next()
                    sc, bsc = sc_r.next()
                    vv_l = []
                    for s_ in range(4):
                        pv, bpv = pbank.next()
                        for c in range(8):
                            MM(pv[:, 0:256], h[:, c, s_ * 128:(s_ + 1) * 128], w_in[:, c, 1312:1568], start=(c == 0), stop=(c == 7), R=[bh[c], b_w], W=[bpv])
                        vvg, bvvg = vvg_r.next()
                        ACT(vvg[:], pv[:, 0:256], AF.Gelu_apprx_tanh, R=[bpv], W=[bvvg])
                        vv_l.append((vvg, bvvg))
                    for s_ in range(4):
                        vvg, bvvg = vv_l[s_]
                        junk, bjunk = tf.next()
                        ACT(junk[:, 0:256], vvg[:], AF.Square, R=[bvvg], W=[bjunk, bsc], accum_out=sc[:, s_:s_ + 1])
                    ACT(sc[:, 4:8], sc[:, 0:4], AF.Sqrt, R=[bsc], W=[bsc], scale=1.0 / 256, bias=EPS)
                    RCP(sc[:, 4:8], sc[:, 4:8], R=[bsc], W=[bsc])
                    for s_ in range(4):
                        vvg, bvvg = vv_l[s_]
                        vvn, bvvn = vvn_r.next()
                        STT(P.dve, vvn[:], vvg[:], sc[:, 4 + s_:5 + s_], gvn[:], ALU.mult, ALU.mult, R=[bvvg, bsc, b_gvn], W=[bvvn])
                        for k in range(2):
                            for half in range(2):
                                g_ = 2 * k + half
                                pm, bpm = pbank.next()
                                MM(pm[:, 0:128], vvn[:, k * 128:(k + 1) * 128], wsT[:, g_, :], R=[bvvn, b_w], W=[bpm])
                                lo, hi = half * 64, half * 64 + 64
                                tm, btm = tf.next()
                                TT(P.dve, tm[lo:hi, 0:128], pm[lo:hi, 0:128], biasT[lo:hi, k, :], ALU.add, R=[bpm, b_biasT], W=[btm])
                                TT(P.pool, yc[lo:hi, k, s_ * 128:(s_ + 1) * 128], tm[lo:hi, 0:128], u[lo:hi, k, s_ * 128:(s_ + 1) * 128], ALU.mult, R=[btm, bu], W=[byc])
                    ST(YT[6:8, :, t0:t0 + TS].rearrange("c p t -> p c t"), yc[:], R=[byc])
                    a_cur = a_nxt
                P.barrier()
            keep_consts()
            if stop_after == "a%d" % l:
                P.emit()
                return nc

            def attention(group):
                pes, sb, ps = phase_scope()
                with pes:
                    if group == "a":
                        dk, nh, vw = 96, 6, 390
                        Vd = VA
                        scale = 96 ** -0.5
                    else:
                        dk, nh, vw = 64, 6, 130
                        Vd = VB
                        scale = 64 ** -0.5
                    v_sb = sb("v_sb", [128, 64, vw], BF16); b_v = P.buf()
                    LD(v_sb[:, 0:32, :], Vd[:, 0:32, :], W=[b_v])
                    LD(v_sb[:, 32:64, :], Vd[:, 32:64, :], W=[b_v])
                    dkp = 96 if group == "a" else 128
                    kt_r = Rot(P, sb, 2, "kt", [dkp, S], BF16)
                    q_r = Rot(P, sb, 3, "q", [dkp, TS], BF16)
                    if group == "b":
                        for k_ in range(2):
                            MSET(P.pool, kt_r.t[k_][64:128, :], 0.0, W=[kt_r.b[k_]])
                        for k_ in range(3):
                            MSET(P.pool, q_r.t[k_][64:128, :], 0.0, W=[q_r.b[k_]])
                    p_r = Rot(P, sb, 3, "p", [128, 2 * TS], BF16)
                    rd_r = Rot(P, sb, 2, "rd", [65, TS], F32)
                    bc_r = Rot(P, sb, 2, "bc", [64, TS], F32)
                    o_r = Rot(P, sb, 2, "o", [64, TS], F32)
                    s_r = Rot(P, ps, 3, "s", [128, 2 * TS], F32)
                    ops_r = Rot(P, ps, 2, "o_ps", [128, TS], F32)

                    def load_k(hh):
                        kt, bkt = kt_r.next()
                        if group == "a":
                            src = KN[hh // 2, (hh % 2) * 64:(hh % 2) * 64 + 64, :]
                            for q4 in range(4):
                                LD(kt[0:64, q4 * 2048:(q4 + 1) * 2048], src[:, q4 * 2048:(q4 + 1) * 2048], W=[bkt])
                                LD(kt[64:96, q4 * 2048:(q4 + 1) * 2048], KPE[:, q4 * 2048:(q4 + 1) * 2048], W=[bkt])
                        else:
                            src = KB[hh * 64:hh * 64 + 64, :]
                            for q4 in range(4):
                                LD(kt[0:64, q4 * 2048:(q4 + 1) * 2048], src[:, q4 * 2048:(q4 + 1) * 2048], W=[bkt])
                        return kt, bkt

                    def load_q(hh, qt):
                        q, bq = q_r.next()
                        if group == "a":
                            LD(q[:], QA[hh, :, qt * TS:(qt + 1) * TS], W=[bq])
                        else:
                            LD(q[0:64, :], QB[hh // 2, (hh % 2) * 64:(hh % 2) * 64 + 64, qt * TS:(qt + 1) * TS], W=[bq])
                        return q, bq

                    b_den = P.bufs_n(4, "den")
                    den_i = [0]
                    kcur = load_k(0)
                    for hh in range(nh):
                        kvh = hh if group == "a" else hh // 3
                        if group == "a":
                            knext = load_k(hh + 1) if hh + 1 < nh else None
                        else:
                            knext = load_k(1) if hh == 2 else None
                        kt, bkt = kcur
                        if hh == 0:
                            qn = load_q(0, 0)
                        for qt in range(NT):
                            q, bq = qn
                            if qt + 1 < NT:
                                qn = load_q(hh, qt + 1)
                            elif hh + 1 < nh:
                                qn = load_q(hh + 1, 0)
                            NJ = 32
                            sb_list = {}
                            o_ps, b_ops = ops_r.next()

                            def do_s(jj):
                                st, bst = s_r.next()
                                for k2 in range(2):
                                    j = 2 * jj + k2
                                    MM(st[:, k2 * TS:(k2 + 1) * TS], kt[:, j * 128:(j + 1) * 128], q[:], R=[bkt, bq], W=[bst])
                                sb_list[jj] = (st, bst)

                            def do_exp(jj):
                                st, bst = sb_list[jj]
                                p, bp = p_r.next()
                                ACT(p[:], st[:], AF.Exp, R=[bst], W=[bp], scale=scale)
                                return p, bp

                            def do_pv(jj, p, bp):
                                for k2 in range(2):
                                    j = 2 * jj + k2
                                    MM(o_ps[0:65, :], v_sb[:, j, kvh * 65:(kvh + 1) * 65], p[:, k2 * TS:(k2 + 1) * TS],
                                       start=(j == 0), stop=(j == 63), R=[b_v, bp], W=[b_ops])

                            do_s(0)
                            do_s(1)
                            for jj in range(NJ):
                                p, bp = do_exp(jj)
                                if jj + 2 < NJ:
                                    do_s(jj + 2)
                                do_pv(jj, p, bp)
                            rd, brd = rd_r.next()
                            RCP(rd[64:65, :], o_ps[64:65, :], R=[b_ops], W=[brd])
                            bc, bbc = bc_r.next()
                            dslot = den_i[0] % 4
                            den_i[0] += 1
                            P.dma(P.sp, lambda e, o=DEN[dslot:dslot + 1, :], i_=rd[64:65, :]: e.dma_start(out=o, in_=i_), reads=[brd], writes=[b_den[dslot]])
                            LD(bc[:], DEN[dslot:dslot + 1, :].broadcast_to([64, TS]), W=[bbc], R=[b_den[dslot]])
                            o, bo = o_r.next()
                            TT(P.dve, o[:], o_ps[0:64, :], bc[:], ALU.mult, R=[b_ops, bbc], W=[bo])
                            chunk = (0 if group == "a" else 3) + hh // 2
                            ST(YT[chunk, (hh % 2) * 64:(hh % 2) * 64 + 64, qt * TS:(qt + 1) * TS], o[:], R=[bo])
                        if knext is not None:
                            kcur = knext
                    P.barrier()
                keep_consts()

            attention("a")
            if stop_after == "ba%d" % l:
                P.emit()
                return nc
            attention("b")
            if stop_after == "b%d" % l:
                P.emit()
                return nc

            pes, sb, ps = phase_scope()
            with pes:
                w_out = sb("w_out", [128, 8, D], BF16)
                w_q = sb("w_q", [128, 8, 512], BF16)
                w_kv = sb("w_kv", [128, 8, 1024], BF16)
                w_o = sb("w_o", [128, 4, D], BF16)
                km = sb("km", [128, 4, 256], BF16); b_km = P.buf()
                vm = sb("vm", [128, 2, 512], BF16); b_vm = P.buf()
                b_w = P.buf("weights")
                wes = ExitStack()
                with wes:
                    wsb = lambda n, sh, dt: wes.enter_context(nc.sbuf_tensor(U(n), list(sh), dt))
                    stage = Rot(P, wsb, 3, "stg", [128, 2048], F32)
                    for c in range(8):
                        load_weight(lambda a, b, c=c: w_out[:, c, a:b], w_out_d[l, c * 128:(c + 1) * 128, :], 128, D, stage, smallcol(l, O_OUTN + c))
                        load_weight(lambda a, b, c=c: w_q[:, c, a:b], mem_w_q_d[l, c * 128:(c + 1) * 128, :], 128, 512, stage, smallcol(l, O_MEMX + c))
                        load_weight(lambda a, b, c=c: w_kv[:, c, a:b], mem_w_kv_d[l, c * 128:(c + 1) * 128, :], 128, 1024, stage, smallcol(l, O_MEMKV + c))
                    for c in range(4):
                        load_weight(lambda a, b, c=c: w_o[:, c, a:b], mem_w_o_d[l, c * 128:(c + 1) * 128, :], 128, D, stage, None)
                    P.barrier()
                    keep_consts()
                    P.bufs.extend([b_km, b_vm, b_w])
                    msq = wsb("msq", [128, 8, 256], BF16); b_msq = P.buf()
                    memn = wsb("memn", [128, 8, 256], BF16); b_memn = P.buf()
                    mr = wsb("mr", [128, 256], F32); b_mr = P.buf()
                    pbk = Rot(P, lambda n, sh, dt: wes.enter_context(nc.psum_tensor(U(n), list(sh), dt)), 4, "pbm", [128, 512], F32)
                    ACT(msq[:], memT[:], AF.Square, R=[b_memT], W=[b_msq])
                    pb, bpb = pbk.next()
                    for c in range(8):
                        MM(pb[:, 0:256], ones_bf[:], msq[:, c, :], start=(c == 0), stop=(c == 7), R=[b_msq, b_ones], W=[bpb])
                    rsqrt_tile(mr[:], pb[:, 0:256], 1.0 / D, R=[bpb], W=[b_mr])
                    for c in range(8):
                        TT(P.dve, memn[:, c, :], memT[:, c, :], mr[:], ALU.mult, R=[b_memT, b_mr], W=[b_memn])
                    for hm in range(4):
                        pb, bpb = pbk.next()
                        for c in range(8):
                            MM(pb[:, 0:256], w_kv[:, c, hm * 128:(hm + 1) * 128], memn[:, c, :], start=(c == 0), stop=(c == 7), R=[b_memn], W=[bpb])
                        CP(P.dve, km[:, hm, :], pb[:, 0:256], R=[bpb], W=[b_km])
                    for kt_ in range(2):
                        pb, bpb = pbk.next()
                        for c in range(8):
                            MM(pb[:], memn[:, c, kt_ * 128:(kt_ + 1) * 128], w_kv[:, c, 512:1024], start=(c == 0), stop=(c == 7), R=[b_memn], W=[bpb])
                        CP(P.dve, vm[:, kt_, :], pb[:], R=[bpb], W=[b_vm])
                    P.barrier()
                keep_consts()
                P.bufs.extend([b_km, b_vm, b_w])

                xt_r = Rot(P, sb, 2, "xt", [128, 8, TS], F32)
                yt_r = Rot(P, sb, 2, "yt", [128, 8, TS], F32)
                sqy_r = Rot(P, sb, 1, "sqy", [128, 8, TS], BF16)
                sqx_r = Rot(P, sb, 1, "sqx", [128, 8, TS], BF16)
                yn_r = Rot(P, sb, 2, "yn", [128, 8, TS], BF16)
                h2_r = Rot(P, sb, 1, "h2", [128, 8, TS], BF16)
                qm_r = Rot(P, sb, 1, "qm", [128, 4, TS], BF16)
                om_r = Rot(P, sb, 1, "om", [128, 4, TS], BF16)
                pm_r = Rot(P, sb, 2, "pm", [128, 2 * TS], BF16)
                tf = Rot(P, sb, 6, "tf", [128, TS], F32)
                pbank = Rot(P, ps, 3, "pb", [128, 512], F32)
                ssm_ps = ps("ssm", [128, 512], F32); b_ssm = P.buf()
                s_r = Rot(P, ps, 2, "s", [128, 2 * TS], F32)
                mscale = 128 ** -0.5
                xt_cb = [P.bufs_n(8, "xtc") for _ in range(2)]
                yn_cb = [P.bufs_n(8, "ync") for _ in range(2)]
                sqx_cb = P.bufs_n(8, "sqxc")
                h2_cb = P.bufs_n(8, "h2c")
                qm_cb = P.bufs_n(4, "qmc")
                om_cb = P.bufs_n(4, "omc")

                def c_stage0(i):
                    st = {"t0": i * TS, "k": i % 2}
                    k = i % 2
                    st["yt"], st["byt"] = yt_r.next()
                    LD(st["yt"][:], YT[:, :, i * TS:(i + 1) * TS].rearrange("c p t -> p c t"), W=[st["byt"]])
                    st["xt"], st["bx"] = xt_r.t[k], xt_cb[k]
                    LD(st["xt"][:], xA[:, :, i * TS:(i + 1) * TS].rearrange("c p t -> p c t"), W=st["bx"])
                    st["yn"], st["byn"] = yn_r.t[k], yn_cb[k]
                    return st

                def c_stage_y1(st):
                    sq, bsq = sqy_r.next()
                    ACT(sq[:], st["yt"][:], AF.Square, R=[st["byt"]], W=[bsq])
                    st["sqy"], st["bsqy"] = sq, bsq

                def c_stage_y2(st):
                    sq, bsq, yt, byt, yn, byn = st["sqy"], st["bsqy"], st["yt"], st["byt"], st["yn"], st["byn"]
                    for (c0, c1) in ((0, 3), (3, 6), (6, 8)):
                        pb, bpb = pbank.next()
                        for c in range(c0, c1):
                            MM(pb[:], ones_bf[:], sq[:, c, :], start=(c == c0), stop=(c == c1 - 1), R=[bsq, b_ones], W=[bpb])
                        rr, brr = tf.next()
                        rsqrt_tile(rr[:], pb[:], 1.0 / (128 * (c1 - c0)), R=[bpb], W=[brr])
                        for c in range(c0, c1):
                            TT(P.dve if c % 2 == 0 else P.pool, yn[:, c, :], yt[:, c, :], rr[:], ALU.mult, R=[byt, brr], W=[byn[c]])

                def c_body(st, nxt):
                    xt, bx, yn, byn, t0 = st["xt"], st["bx"], st["yn"], st["byn"], st["t0"]
                    sq = sqx_r.t[0]
                    for nb in range(8):
                        pb, bpb = pbank.next()
                        for c in range(8):
                            MM(pb[:], w_out[:, c, nb * 128:(nb + 1) * 128], yn[:, c, :], start=(c == 0), stop=(c == 7), R=[byn[c], b_w], W=[bpb])
                        TT(P.dve, xt[:, nb, :], pb[:], xt[:, nb, :], ALU.add, R=[bpb], W=[bx[nb]])
                        ACT(sq[:, nb, :], xt[:, nb, :], AF.Square, R=[bx[nb]], W=[sqx_cb[nb]])
                        if nb >= 1:
                            MM(ssm_ps[:], ones_bf[:], sq[:, nb - 1, :], start=(nb == 1), stop=False, R=[sqx_cb[nb - 1], b_ones], W=[b_ssm])
                    MM(ssm_ps[:], ones_bf[:], sq[:, 7, :], start=False, stop=True, R=[sqx_cb[7], b_ones], W=[b_ssm])
                    rr, brr = tf.next()
                    rsqrt_tile(rr[:], ssm_ps[:], 1.0 / D, R=[b_ssm], W=[brr])
                    h2 = h2_r.t[0]
                    for c in range(8):
                        TT(P.dve if c % 2 == 0 else P.pool, h2[:, c, :], xt[:, c, :], rr[:], ALU.mult, R=[bx[c], brr], W=[h2_cb[c]])
                    qm = qm_r.t[0]
                    for hm in range(4):
                        pb, bpb = pbank.next()
                        for c in range(8):
                            MM(pb[:], w_q[:, c, hm * 128:(hm + 1) * 128], h2[:, c, :], start=(c == 0), stop=(c == 7), R=[h2_cb[c], b_w], W=[bpb])
                        P.op(P.act, lambda e, o=qm[:, hm, :], i_=pb[:]: e.copy(out=o, in_=i_), reads=[bpb], writes=[qm_cb[hm]])
                    if nxt is not None:
                        c_stage_y1(nxt)
                    om = om_r.t[0]
                    sts = {}

                    def c_s(hm):
                        st_, bst_ = s_r.next()
                        for kt_ in range(2):
                            MM(st_[:, kt_ * TS:(kt_ + 1) * TS], km[:, hm, kt_ * 128:(kt_ + 1) * 128], qm[:, hm, :], R=[b_km, qm_cb[hm]], W=[bst_])
                        sts[hm] = (st_, bst_)

                    c_s(0)
                    c_s(1)
                    for hm in range(4):
                        st_, bst_ = sts[hm]
                        pm, bpm = pm_r.next()
                        ACT(pm[:], st_[:], AF.Exp, R=[bst_], W=[bpm], scale=mscale)
                        if hm + 2 < 4:
                            c_s(hm + 2)
                        po, bpo = pbank.next()
                        for kt_ in range(2):
                            MM(po[:], vm[:, kt_, hm * 128:(hm + 1) * 128], pm[:, kt_ * TS:(kt_ + 1) * TS], start=(kt_ == 0), stop=(kt_ == 1), R=[b_vm, bpm], W=[bpo])
                        pd, bpd = pbank.next()
                        for kt_ in range(2):
                            MM(pd[:], ones_bf[:], pm[:, kt_ * TS:(kt_ + 1) * TS], start=(kt_ == 0), stop=(kt_ == 1), R=[b_ones, bpm], W=[bpd])
                        rd, brd = tf.next()
                        RCP(rd[:], pd[:], R=[bpd], W=[brd])
                        TT(P.dve, om[:, hm, :], po[:], rd[:], ALU.mult, R=[bpo, brd], W=[om_cb[hm]])
                    if nxt is not None:
                        c_stage_y2(nxt)
                    for nb in range(8):
                        pb, bpb = pbank.next()
                        for c in range(4):
                            MM(pb[:], w_o[:, c, nb * 128:(nb + 1) * 128], om[:, c, :], start=(c == 0), stop=(c == 3), R=[om_cb[c], b_w], W=[bpb])
                        TT(P.dve, xt[:, nb, :], pb[:], xt[:, nb, :], ALU.add, R=[bpb], W=[bx[nb]])
                    ST(xB[:, :, t0:t0 + TS].rearrange("c p t -> p c t"), xt[:], R=bx)

                cur = c_stage0(0)
                c_stage_y1(cur)
                c_stage_y2(cur)
                for i in range(NT):
                    nxt = c_stage0(i + 1) if i + 1 < NT else None
                    c_body(cur, nxt)
                    cur = nxt
                P.barrier()
            keep_consts()
            if stop_after == "c%d" % l:
                P.emit()
                return nc

            NF = 11
            TW = 510
            NTD = (S + TW - 1) // TW
            for half in range(2):
                pes, sb, ps = phase_scope()
                with pes:
                    w_up = sb("w_up", [128, 8, 2 * NF * 128], BF16)
                    w_dn = sb("w_dn", [128, NF, D], BF16)
                    b_w = P.buf("weights")
                    wes = ExitStack()
                    with wes:
                        stage = Rot(P, lambda n, sh, dt: wes.enter_context(nc.sbuf_tensor(U(n), list(sh), dt)), 3, "stg", [128, 2048], F32)
                        f0 = half * NF * 128
                        for c in range(8):
                            g = smallcol(l, O_FFN + c)
                            load_weight(lambda a, b, c=c: w_up[:, c, a:b], w_up_d[l, c * 128:(c + 1) * 128, f0:f0 + NF * 128], 128, NF * 128, stage, g)
                            load_weight(lambda a, b, c=c: w_up[:, c, NF * 128 + a:NF * 128 + b], w_up_d[l, c * 128:(c + 1) * 128, D_FF + f0:D_FF + f0 + NF * 128], 128, NF * 128, stage, g)
                        for f in range(NF):
                            load_weight(lambda a, b, f=f: w_dn[:, f, a:b], w_dn_d[l, f0 + f * 128:f0 + (f + 1) * 128, :], 128, D, stage, None)
                        P.barrier()
                    keep_consts()
                    P.bufs.append(b_w)
                    xt_r = Rot(P, sb, 2, "xt", [128, 8, TS], F32)
                    ac_r = Rot(P, sb, 2, "ac", [128, 8, TS], F32)
                    sq_r = Rot(P, sb, 1, "sq", [128, 8, TS], BF16)
                    h_r = Rot(P, sb, 2, "h", [128, 8, TS], BF16)
                    g_r = Rot(P, sb, 1, "g", [128, NF, TS], BF16)
                    tf = Rot(P, sb, 8, "tf", [128, TS], F32)
                    rr_r = Rot(P, sb, 2, "rr", [128, TS], F32)
                    pbank = Rot(P, ps, 8, "pb", [128, 512], F32)
                    xt_cb = [P.bufs_n(8, "xtc") for _ in range(2)]
                    ac_cb = [P.bufs_n(8, "acc") for _ in range(2)]
                    h_cb = [P.bufs_n(8, "hc") for _ in range(2)]
                    g_cb = P.bufs_n(NF, "gc")
                    g = g_r.t[0]

                    def stage0(i):
                        st = {}
                        t0 = i * TW
                        a_lo = t0 - 1
                        n_out = min(TW, S - t0)
                        W_ = n_out + 2
                        lo_tok = max(a_lo, 0)
                        hi_tok = min(a_lo + W_, S)
                        c_lo = lo_tok - a_lo
                        c_hi = hi_tok - a_lo
                        k = i % 2
                        xt, bx = xt_r.t[k], xt_cb[k]
                        if c_lo > 0:
                            MSET(P.pool, xt[:, :, 0:c_lo], 0.0, W=bx)
                        if c_hi < W_:
                            MSET(P.pool, xt[:, :, c_hi:W_], 0.0, W=bx)
                        LD(xt[:, :, c_lo:c_hi], xB[:, :, lo_tok:hi_tok].rearrange("c p t -> p c t"), W=bx)
                        ac, bac = ac_r.t[k], ac_cb[k]
                        if half == 1:
                            LD(ac[:, :, 1:1 + n_out], xC[:, :, t0:t0 + n_out].rearrange("c p t -> p c t"), W=bac)
                        st.update(t0=t0, n_out=n_out, W_=W_, xt=xt, bx=bx, ac=ac, bac=bac, h=h_r.t[k], bh=h_cb[k])
                        return st

                    def stage1(st):
                        W_ = st["W_"]
                        sq, bsq = sq_r.next()
                        ACT(sq[:, :, 0:W_], st["xt"][:, :, 0:W_], AF.Square, R=st["bx"], W=[bsq])
                        st["sq"], st["bsq"] = sq, bsq

                    def stage2(st):
                        W_ = st["W_"]
                        sq, bsq = st["sq"], st["bsq"]
                        pb, bpb = pbank.next()
                        for c in range(8):
                            MM(pb[:, 0:W_], ones_bf[:], sq[:, c, 0:W_], start=(c == 0), stop=(c == 7), R=[bsq, b_ones], W=[bpb])
                        rr, brr = rr_r.next()
                        rsqrt_tile(rr[:, 0:W_], pb[:, 0:W_], 1.0 / D, R=[bpb], W=[brr])
                        for c in range(8):
                            TT(P.dve if c % 2 == 0 else P.pool, st["h"][:, c, 0:W_], st["xt"][:, c, 0:W_], rr[:, 0:W_], ALU.mult,
                               R=[st["bx"][c], brr], W=[st["bh"][c]])

                    def body(st, nxt):
                        n_out, W_, h, bh, ac, bac, t0 = st["n_out"], st["W_"], st["h"], st["bh"], st["ac"], st["bac"], st["t0"]
                        for f in range(NF):
                            if nxt is not None and f == 4:
                                stage1(nxt)
                            if nxt is not None and f == 7:
                                stage2(nxt)
                            fg = half * NF + f
                            res = []
                            for part in range(2):
                                pb, bpb = pbank.next()
                                for c in range(8):
                                    MM(pb[:, 0:W_], w_up[:, c, (part * NF + f) * 128:(part * NF + f + 1) * 128], h[:, c, 0:W_],
                                       start=(c == 0), stop=(c == 7), R=[bh[c], b_w], W=[bpb])
                                blk = fg + part * 22
                                cw = O_CW + blk * 3
                                t_, bt_ = tf.next()
                                ACT(t_[:, 0:n_out], pb[:, 0:n_out], AF.Identity, R=[bpb, b_small], W=[bt_],
                                    scale=smallcol(l, cw), bias=smallcol(l, O_CB + blk))
                                STT(P.dve, t_[:, 0:n_out], pb[:, 1:1 + n_out], smallcol(l, cw + 1), t_[:, 0:n_out], ALU.mult, ALU.add, R=[bpb, b_small], W=[bt_])
                                STT(P.dve, t_[:, 0:n_out], pb[:, 2:2 + n_out], smallcol(l, cw + 2), t_[:, 0:n_out], ALU.mult, ALU.add, R=[bpb, b_small], W=[bt_])
                                res.append((t_, bt_))
                            ACT(res[0][0][:, 0:n_out], res[0][0][:, 0:n_out], AF.Silu, R=[res[0][1]], W=[res[0][1]])
                            TT(P.pool, g[:, f, 0:n_out], res[0][0][:, 0:n_out], res[1][0][:, 0:n_out], ALU.mult, R=[res[0][1], res[1][1]], W=[g_cb[f]])
                        for nb in range(8):
                            pb, bpb = pbank.next()
                            for f in range(NF):
                                MM(pb[:, 0:n_out], w_dn[:, f, nb * 128:(nb + 1) * 128], g[:, f, 0:n_out], start=(f == 0), stop=(f == NF - 1), R=[g_cb[f], b_w], W=[bpb])
                            if half == 0:
                                TT(P.dve, ac[:, nb, 1:1 + n_out], pb[:, 0:n_out], st["xt"][:, nb, 1:1 + n_out], ALU.add, R=[bpb, st["bx"][nb]], W=[bac[nb]])
                            else:
                                TT(P.dve, ac[:, nb, 1:1 + n_out], pb[:, 0:n_out], ac[:, nb, 1:1 + n_out], ALU.add, R=[bpb], W=[bac[nb]])
                        dstT = xC if half == 0 else xA
                        ST(dstT[:, :, t0:t0 + n_out].rearrange("c p t -> p c t"), ac[:, :, 1:1 + n_out], R=bac)

                    cur = stage0(0)
                    stage1(cur)
                    stage2(cur)
                    for i in range(NTD):
                        nxt = stage0(i + 1) if i + 1 < NTD else None
                        body(cur, nxt)
                        cur = nxt
                    P.barrier()
                keep_consts()
            if stop_after == "d%d" % l:
                P.emit()
                return nc

        pes, sb, ps = phase_scope()
        with pes:
            xt_r = Rot(P, sb, 2, "xt", [128, 8, TS], F32)
            sq_r = Rot(P, sb, 1, "sq", [128, 8, TS], BF16)
            xn_r = Rot(P, sb, 2, "xn", [128, 8, TS], F32)
            yo_r = Rot(P, sb, 2, "yo", [128, 4, D], F32)
            tf = Rot(P, sb, 2, "tf", [128, TS], F32)
            pbank = Rot(P, ps, 6, "pb", [128, 512], F32)
            for i in range(NT):
                t0 = i * TS
                xt, bxt = xt_r.next()
                LD(xt[:], xA[:, :, t0:t0 + TS].rearrange("c p t -> p c t"), W=[bxt])
                sq, bsq = sq_r.next()
                ACT(sq[:], xt[:], AF.Square, R=[bxt], W=[bsq])
                pb, bpb = pbank.next()
                for c in range(8):
                    MM(pb[:], ones_bf[:], sq[:, c, :], start=(c == 0), stop=(c == 7), R=[bsq, b_ones], W=[bpb])
                rr, brr = tf.next()
                rsqrt_tile(rr[:], pb[:], 1.0 / D, R=[bpb], W=[brr])
                xn, bxn = xn_r.next()
                for c in range(8):
                    STT(P.dve, xn[:, c, :], xt[:, c, :], fin[:, c:c + 1], rr[:], ALU.mult, ALU.mult, R=[bxt, brr, b_fin], W=[bxn])
                yo, byo = yo_r.next()
                for s_ in range(4):
                    for c4 in range(2):
                        pb, bpb = pbank.next()
                        for cc in range(4):
                            c = c4 * 4 + cc
                            TR(pb[:, cc * 128:(cc + 1) * 128], xn[:, c, s_ * 128:(s_ + 1) * 128], ident[:], R=[bxn, b_ident], W=[bpb])
                        if (s_ + c4) % 2 == 0:
                            CP(P.dve, yo[:, s_, c4 * 512:(c4 + 1) * 512], pb[:], R=[bpb], W=[byo])
                        else:
                            P.op(P.act, lambda e, o=yo[:, s_, c4 * 512:(c4 + 1) * 512], i_=pb[:]: e.copy(out=o, in_=i_), reads=[bpb], writes=[byo])
                ST(y_out[t0:t0 + TS, :].rearrange("(s p) d -> p s d", p=128), yo[:], R=[byo])
            P.barrier()
        P.emit()
    return nc


def _swap_pairs(w):
    idx = np.arange(w.shape[-1]).reshape(-1, 2)[:, ::-1].reshape(-1)
    return w[..., idx]


def _rope_tables():
    rows = S // 64
    row = np.repeat(np.arange(rows, dtype=np.float32), 64)
    col = np.tile(np.arange(64, dtype=np.float32), rows)

    def tab(d_rot):
        n = d_rot // 4
        inv = (np.float32(10000.0) ** (-np.arange(n, dtype=np.float32) / np.float32(n))).astype(np.float32)
        ang = np.concatenate([row[:, None] * inv, col[:, None] * inv], axis=-1).astype(np.float32)
        c = np.cos(ang).astype(np.float32)
        s = np.sin(ang).astype(np.float32)
        cf = np.repeat(c, 2, axis=1)
        sf = np.repeat(s, 2, axis=1)
        sign = np.tile(np.array([-1.0, 1.0], np.float32), d_rot // 2)
        return np.ascontiguousarray(cf.T), np.ascontiguousarray((sf * sign).T)

    ca, sa = tab(32)
    cb, sb_ = tab(64)
    ropeA = np.stack([ca, sa]).astype(np.float32)
    ropeB = np.stack([np.concatenate([cb, cb]), np.concatenate([sb_, sb_])]).astype(np.float32)
    return ropeA, ropeB


def _host_layout(inputs):
    f = lambda k: np.asarray(inputs[k], dtype=np.float32)
    L = DEPTH
    w_in = f("w_in")
    sw_src = np.concatenate([w_in[:, :, 320:416], w_in[:, :, 416:800], w_in[:, :, 800:928]], axis=-1)
    w_in_sw = _swap_pairs(sw_src)
    w_uq = f("mla_w_uq")
    w_uq_sw = _swap_pairs(w_uq)
    w_ukv = f("mla_w_ukv").reshape(L, 128, 6, 2, 64)
    w_ukv_p = np.concatenate([w_ukv[:, :, :, 0, :].reshape(L, 128, 384), w_ukv[:, :, :, 1, :].reshape(L, 128, 384)], axis=-1)
    wsT = np.ascontiguousarray(f("gmlp_w_s").transpose(0, 1, 3, 2))
    pc = lambda v: v.reshape(L, -1, 128).transpose(0, 2, 1)
    gq = f("gqa_q_norm"); gk = f("gqa_k_norm")
    sw64 = np.arange(64).reshape(-1, 2)[:, ::-1].reshape(-1)
    tile2 = lambda v: np.concatenate([v, v], axis=-1)[:, :, None]
    cw = f("ffn_conv_w")
    cwp = cw.reshape(L, 3, 44, 128).transpose(0, 3, 2, 1).reshape(L, 128, 132)
    cb = pc(f("ffn_conv_b"))
    small = np.concatenate([
        pc(f("mix_norm")), pc(f("out_norm")), pc(f("mem_x_norm")), pc(f("mem_kv_norm")), pc(f("ffn_norm")),
        pc(f("mla_q_norm")), pc(f("mla_kv_norm")),
        tile2(gq), tile2(gq[:, sw64]), tile2(gk), tile2(gk[:, sw64]),
        cwp, cb], axis=-1).astype(np.float32)
    ropeA, ropeB = _rope_tables()
    shared = {
        "ident": np.eye(128, dtype=np.float32),
        "ropeA": ropeA, "ropeB": ropeB,
        "w_in": w_in, "w_in_sw": np.ascontiguousarray(w_in_sw),
        "w_uq": w_uq, "w_uq_sw": np.ascontiguousarray(w_uq_sw),
        "w_ukv": np.ascontiguousarray(w_ukv_p),
        "wsT": wsT, "bs": f("gmlp_b_s"), "gvn": f("gmlp_v_norm"),
        "w_out": f("w_out"), "mem_w_q": f("mem_w_q"), "mem_w_kv": f("mem_w_kv"), "mem_w_o": f("mem_w_o"),
        "w_up": f("ffn_w_up"), "w_dn": f("ffn_w_down"),
        "small": np.ascontiguousarray(small),
        "fin": np.ascontiguousarray(f("final_norm").reshape(8, 128).T),
    }
    return shared


_NC_CACHE = {}


def kernel(**inputs):
    shared = _host_layout(inputs)
    x = np.asarray(inputs["x"], dtype=np.float32)
    mem = np.asarray(inputs["mem"], dtype=np.float32)
    if "nc" not in _NC_CACHE:
        _NC_CACHE["nc"] = build()
    nc = _NC_CACHE["nc"]
    in_maps = []
    for c in range(NCORES):
        m = dict(shared)
        m["x"] = np.ascontiguousarray(x[c])
        m["mem"] = np.ascontiguousarray(mem[c])
        in_maps.append(m)
    res = run_bass_kernel_spmd(nc, in_maps, core_ids=list(range(NCORES)))
    return np.stack([res.results[c]["y"] for c in range(NCORES)], axis=0).astype(np.float32)
```
